# Optimizing a Trainium2 kernel written in Bass

```python
import math
import jax, jax.numpy as jnp
from jax import lax
import numpy as np

D_MODEL = 4096
BATCH = 4
SEQ = 4096
DEPTH = 2
DEC_BATCH = 16
DEC_SEQ = 32
PAST_LEN = 4096

CHUNK = 64
Q_BLOCK = 128
ROPE_THETA = 500000.0
DEEPNORM_ALPHA = (2 * DEPTH) ** 0.25
DEEPNORM_BETA = (8 * DEPTH) ** -0.25
NEG_INF = -1e30

SSD_WIDTH = D_MODEL // 2
SSD_HEAD_DIM = 64
SSD_HEADS = SSD_WIDTH // SSD_HEAD_DIM
SSD_GROUPS = 4
SSD_R = SSD_HEADS // SSD_GROUPS
SSD_STATE = 128
SSD_CONV = 4
SSD_CHUNK = CHUNK
SSD_CONV_DIM = SSD_WIDTH + 2 * SSD_GROUPS * SSD_STATE

DIFF_WIDTH = D_MODEL // 4
DIFF_QK_DIM = 64
DIFF_HEADS = DIFF_WIDTH // (2 * DIFF_QK_DIM)
DIFF_V_DIM = 2 * DIFF_QK_DIM
DIFF_ROT = DIFF_QK_DIM // 4
DIFF_SCALE = DIFF_QK_DIM ** -0.5

MLA_WIDTH = D_MODEL // 4
MLA_V_DIM = 128
MLA_HEADS = MLA_WIDTH // MLA_V_DIM
MLA_NOPE_DIM = 128
MLA_ROPE_DIM = 64
MLA_Q_RANK = 768
MLA_KV_RANK = 256
MLA_SCALE = (MLA_NOPE_DIM + MLA_ROPE_DIM) ** -0.5

MIX_WIDTH = SSD_WIDTH + DIFF_WIDTH + MLA_WIDTH
IN_SIZES = (SSD_WIDTH,
            SSD_CONV_DIM,
            SSD_HEADS,
            DIFF_WIDTH,
            DIFF_WIDTH,
            DIFF_WIDTH,
            DIFF_WIDTH,
            MLA_Q_RANK,
            MLA_KV_RANK,
            MLA_ROPE_DIM,
            MLA_WIDTH)
IN_WIDTH = sum(IN_SIZES)

kernel_name = 'hybrid_ssd_diffattn_mla_streaming_step'


def rmsnorm(x, w, eps=1e-6):
    xf = x.astype(jnp.float32)
    y = xf * lax.rsqrt(jnp.mean(xf * xf, axis=-1, keepdims=True) + eps)
    return (y * w.astype(jnp.float32)).astype(x.dtype)


def layernorm(x, g, b, eps=1e-5):
    xf = x.astype(jnp.float32)
    mu = jnp.mean(xf, axis=-1, keepdims=True)
    var = jnp.mean(jnp.square(xf - mu), axis=-1, keepdims=True)
    y = (xf - mu) * lax.rsqrt(var + eps) * g.astype(jnp.float32) + b.astype(jnp.float32)
    return y.astype(x.dtype)


def rope(x, pos, rot_dim):
    half = rot_dim // 2
    inv = ROPE_THETA ** (-jnp.arange(half, dtype=jnp.float32) * (2.0 / rot_dim))
    ang = pos.astype(jnp.float32)[:, None] * inv[None, :]
    cos = jnp.cos(ang)[None, :, None, :]
    sin = jnp.sin(ang)[None, :, None, :]
    xr = x[..., :rot_dim].astype(jnp.float32)
    x1, x2 = xr[..., :half], xr[..., half:]
    rot = jnp.concatenate([x1 * cos - x2 * sin, x2 * cos + x1 * sin], axis=-1).astype(x.dtype)
    return jnp.concatenate([rot, x[..., rot_dim:]], axis=-1)


def chunk_visible(q_pos, k_pos):
    return (k_pos[None, :] // CHUNK) <= (q_pos[:, None] // CHUNK)


def over_query_blocks(fn, q_parts, q_pos):
    L = q_pos.shape[0]
    blk = Q_BLOCK if L % Q_BLOCK == 0 else L
    nb = L // blk
    split = lambda t: jnp.moveaxis(t.reshape((t.shape[0], nb, blk) + t.shape[2:]), 1, 0)
    out = lax.map(lambda a: fn(a[0], a[1]), (tuple(split(t) for t in q_parts), q_pos.reshape(nb, blk)))
    out = jnp.moveaxis(out, 0, 1)
    return out.reshape((out.shape[0], L) + out.shape[3:])


def causal_conv(xbc, conv_state, w, b):
    L = xbc.shape[1]
    xp = jnp.concatenate([conv_state.astype(xbc.dtype), xbc], axis=1)
    y = b
    for j in range(SSD_CONV):
        y = y + xp[:, j:j + L] * w[j]
    return jax.nn.silu(y), xp[:, L:]


def ssd_scan(x, dt, A, B, C, state0):
    b, L = x.shape[0], x.shape[1]
    pad = (-L) % SSD_CHUNK
    padt = lambda t: jnp.pad(t, [(0, 0), (0, pad)] + [(0, 0)] * (t.ndim - 2))
    x, dt, B, C = padt(x), padt(dt), padt(B), padt(C)
    nc = (L + pad) // SSD_CHUNK
    ch = lambda t: t.reshape((b, nc, SSD_CHUNK) + t.shape[2:])
    x, dt, B, C = ch(x), ch(dt), ch(B), ch(C)
    a_cs = jnp.cumsum(dt * A, axis=2)
    xd = x * dt[..., None]
    seg = a_cs[:, :, :, None] - a_cs[:, :, None, :]
    causal = jnp.tril(jnp.ones((SSD_CHUNK, SSD_CHUNK), dtype=bool))[None, None, :, :, None, None]
    decay = jnp.exp(jnp.where(causal, seg, -jnp.inf))
    cb = jnp.einsum('bclgn,bcsgn->bclsg', C, B)
    y_diag = jnp.einsum('bclsgr,bcsgrp->bclgrp', cb[..., None] * decay, xd)
    to_end = jnp.exp(a_cs[:, :, -1:] - a_cs)
    chunk_states = jnp.einsum('bcsgn,bcsgrp->bcgrpn', B, xd * to_end[..., None])
    chunk_decay = jnp.exp(a_cs[:, :, -1])

    def step(s, inp):
        cs, cd = inp
        return s * cd[..., None, None] + cs, s

    final, prev = lax.scan(step, state0, (jnp.moveaxis(chunk_states, 1, 0), jnp.moveaxis(chunk_decay, 1, 0)))
    prev = jnp.moveaxis(prev, 0, 1)
    y_off = jnp.einsum('bclgn,bcgrpn->bclgrp', C, prev) * jnp.exp(a_cs)[..., None]
    y = (y_diag + y_off).reshape((b, nc * SSD_CHUNK) + x.shape[3:])[:, :L]
    return y, final


def trunk_layer(x, c, k_cache, v_cache, lat_cache, kr_cache, ssm_state, conv_state, p, layer_idx):
    f32 = jnp.float32
    b, L, _ = x.shape
    P = k_cache.shape[1]
    q_pos = P + jnp.arange(L, dtype=jnp.int32)
    k_pos = jnp.arange(P + L, dtype=jnp.int32)

    mod = jax.nn.silu(c) @ p['w_mod'] + p['b_mod']
    shift, scale, gate = jnp.split(mod, 3, axis=-1)
    u = x * (1 + scale[:, None]) + shift[:, None]

    h = u @ p['w_in']
    offs = [int(o) for o in np.cumsum(IN_SIZES)[:-1]]
    z, xbc, dt_raw, dq, dk, dv, dgate, cq, ckv, kr, mgate = jnp.split(h, offs, axis=-1)

    xbc, conv_new = causal_conv(xbc, conv_state, p['conv_w'], p['conv_b'])
    xs, Bm, Cm = jnp.split(xbc, [SSD_WIDTH, SSD_WIDTH + SSD_GROUPS * SSD_STATE], axis=-1)
    xs5 = xs.reshape(b, L, SSD_GROUPS, SSD_R, SSD_HEAD_DIM).astype(f32)
    dt = jax.nn.softplus((dt_raw + p['dt_bias']).astype(f32)).reshape(b, L, SSD_GROUPS, SSD_R)
    A = -jnp.exp(p['a_log'].astype(f32)).reshape(SSD_GROUPS, SSD_R)
    y, ssm_new = ssd_scan(xs5, dt, A,
                          Bm.reshape(b, L, SSD_GROUPS, SSD_STATE).astype(f32),
                          Cm.reshape(b, L, SSD_GROUPS, SSD_STATE).astype(f32),
                          ssm_state.astype(f32).reshape(b, SSD_GROUPS, SSD_R, SSD_HEAD_DIM, SSD_STATE))
    y = y + p['d_skip'].astype(f32).reshape(SSD_GROUPS, SSD_R)[..., None] * xs5
    yg = y.reshape(b, L, SSD_WIDTH) * jax.nn.silu(z.astype(f32))
    y_ssd = rmsnorm(yg.reshape(b, L, SSD_GROUPS, SSD_WIDTH // SSD_GROUPS),
                    p['ssd_norm_w'].reshape(SSD_GROUPS, SSD_WIDTH // SSD_GROUPS)).reshape(b, L, SSD_WIDTH).astype(x.dtype)
    ssm_new = ssm_new.reshape(b, SSD_HEADS, SSD_HEAD_DIM, SSD_STATE).astype(ssm_state.dtype)

    lam_init = 0.8 - 0.6 * math.exp(-0.3 * layer_idx)
    lam = (jnp.exp(jnp.sum(p['lambda_q1'].astype(f32) * p['lambda_k1'].astype(f32)))
           - jnp.exp(jnp.sum(p['lambda_q2'].astype(f32) * p['lambda_k2'].astype(f32))) + lam_init)
    q = rope(dq.reshape(b, L, DIFF_HEADS * 2, DIFF_QK_DIM), q_pos, DIFF_ROT).reshape(b, L, DIFF_HEADS, 2, DIFF_QK_DIM)
    k = rope(dk.reshape(b, L, DIFF_HEADS * 2, DIFF_QK_DIM), q_pos, DIFF_ROT).reshape(b, L, DIFF_HEADS, 2, DIFF_QK_DIM)
    v = dv.reshape(b, L, DIFF_HEADS, DIFF_V_DIM)
    k_all = jnp.concatenate([k_cache.astype(k.dtype), k], axis=1)
    v_all = jnp.concatenate([v_cache.astype(v.dtype), v], axis=1)

    def diff_block(qs, pos):
        (qb,) = qs
        s = jnp.einsum('bqhcd,bkhcd->bchqk', qb, k_all, preferred_element_type=f32) * DIFF_SCALE
        s = jnp.where(chunk_visible(pos, k_pos), s, NEG_INF)
        a = jax.nn.softmax(s, axis=-1)
        amap = a[:, 0] - lam * a[:, 1]
        return jnp.einsum('bhqk,bkhe->bqhe', amap, v_all.astype(f32))

    o = over_query_blocks(diff_block, (q,), q_pos)
    o = rmsnorm(o, p['diff_norm_w']) * (1.0 - lam_init)
    y_diff = (o.reshape(b, L, DIFF_WIDTH) * jax.nn.silu(dgate.astype(f32))).astype(x.dtype)

    qm = (rmsnorm(cq, p['mla_q_norm_w']) @ p['w_uq']).reshape(b, L, MLA_HEADS, MLA_NOPE_DIM + MLA_ROPE_DIM)
    q_nope, q_rope = qm[..., :MLA_NOPE_DIM], qm[..., MLA_NOPE_DIM:]
    q_rope = rope(q_rope, q_pos, MLA_ROPE_DIM)
    lat = rmsnorm(ckv, p['mla_kv_norm_w'])
    krot = rope(kr[:, :, None, :], q_pos, MLA_ROPE_DIM)[:, :, 0]
    lat_all = jnp.concatenate([lat_cache.astype(lat.dtype), lat], axis=1)
    kr_all = jnp.concatenate([kr_cache.astype(krot.dtype), krot], axis=1)
    q_lat = jnp.einsum('bqhd,ehd->bqhe', q_nope, p['w_uk'])

    def mla_block(qs, pos):
        ql, qr = qs
        s = (jnp.einsum('bqhe,bke->bhqk', ql, lat_all, preferred_element_type=f32)
             + jnp.einsum('bqhr,bkr->bhqk', qr, kr_all, preferred_element_type=f32)) * MLA_SCALE
        s = jnp.where(chunk_visible(pos, k_pos), s, NEG_INF)
        a = jax.nn.softmax(s, axis=-1)
        return jnp.einsum('bhqk,bke->bqhe', a, lat_all.astype(f32))

    o_lat = over_query_blocks(mla_block, (q_lat, q_rope), q_pos)
    om = jnp.einsum('bqhe,ehd->bqhd', o_lat, p['w_uv'].astype(f32)).reshape(b, L, MLA_WIDTH)
    y_mla = (om * jax.nn.silu(mgate.astype(f32))).astype(x.dtype)

    mix = jnp.concatenate([y_ssd, y_diff, y_mla], axis=-1) @ p['w_out']
    x_new = layernorm(DEEPNORM_ALPHA * x + gate[:, None] * mix, p['ln_g'], p['ln_b'])
    return x_new, (k, v, lat, krot, ssm_new, conv_new)


def setup_inputs(seed: int = 0) -> dict:
    key = jax.random.key(seed)
    ks = iter(jax.random.split(key, 48))
    f32 = jnp.float32
    nrm = lambda shape, s=1.0: jax.random.normal(next(ks), shape, f32) * s
    ones_noise = lambda shape: 1.0 + 0.02 * jax.random.normal(next(ks), shape, f32)
    dt0 = jnp.exp(jax.random.uniform(next(ks), (DEPTH, SSD_HEADS), f32, math.log(1e-3), math.log(1e-1)))
    dt_bias = dt0 + jnp.log(-jnp.expm1(-dt0))
    a_log = jnp.log(jax.random.uniform(next(ks), (DEPTH, SSD_HEADS), f32, 1.0, 16.0))
    return {
        'x_prompt': nrm((BATCH, SEQ, D_MODEL)),
        'x_sample': nrm((DEC_BATCH, DEC_SEQ, D_MODEL)),
        'cache_diff_k': nrm((DEPTH, DEC_BATCH, PAST_LEN, DIFF_HEADS, 2, DIFF_QK_DIM)),
        'cache_diff_v': nrm((DEPTH, DEC_BATCH, PAST_LEN, DIFF_HEADS, DIFF_V_DIM)),
        'cache_mla_latent': nrm((DEPTH, DEC_BATCH, PAST_LEN, MLA_KV_RANK)),
        'cache_mla_krope': nrm((DEPTH, DEC_BATCH, PAST_LEN, MLA_ROPE_DIM)),
        'state_ssm': nrm((DEPTH, DEC_BATCH, SSD_HEADS, SSD_HEAD_DIM, SSD_STATE), 0.5),
        'state_conv': nrm((DEPTH, DEC_BATCH, SSD_CONV - 1, SSD_CONV_DIM)),
        'c_prompt': nrm((BATCH, D_MODEL)),
        'c_sample': nrm((DEC_BATCH, D_MODEL)),
        'w_mod': nrm((DEPTH, D_MODEL, 3 * D_MODEL), 0.5 * D_MODEL ** -0.5),
        'b_mod': nrm((DEPTH, 3 * D_MODEL), 0.01),
        'w_in': nrm((DEPTH, D_MODEL, IN_WIDTH), D_MODEL ** -0.5),
        'conv_w': nrm((DEPTH, SSD_CONV, SSD_CONV_DIM), SSD_CONV ** -0.5),
        'conv_b': nrm((DEPTH, SSD_CONV_DIM), 0.01),
        'dt_bias': dt_bias,
        'a_log': a_log,
        'd_skip': ones_noise((DEPTH, SSD_HEADS)),
        'ssd_norm_w': ones_noise((DEPTH, SSD_WIDTH)),
        'lambda_q1': nrm((DEPTH, DIFF_QK_DIM), 0.1),
        'lambda_k1': nrm((DEPTH, DIFF_QK_DIM), 0.1),
        'lambda_q2': nrm((DEPTH, DIFF_QK_DIM), 0.1),
        'lambda_k2': nrm((DEPTH, DIFF_QK_DIM), 0.1),
        'diff_norm_w': ones_noise((DEPTH, DIFF_V_DIM)),
        'mla_q_norm_w': ones_noise((DEPTH, MLA_Q_RANK)),
        'mla_kv_norm_w': ones_noise((DEPTH, MLA_KV_RANK)),
        'w_uq': nrm((DEPTH, MLA_Q_RANK, MLA_HEADS * (MLA_NOPE_DIM + MLA_ROPE_DIM)), MLA_Q_RANK ** -0.5),
        'w_uk': nrm((DEPTH, MLA_KV_RANK, MLA_HEADS, MLA_NOPE_DIM), MLA_KV_RANK ** -0.5),
        'w_uv': nrm((DEPTH, MLA_KV_RANK, MLA_HEADS, MLA_V_DIM), MLA_KV_RANK ** -0.5),
        'w_out': nrm((DEPTH, MIX_WIDTH, D_MODEL), MIX_WIDTH ** -0.5 * DEEPNORM_BETA),
        'ln_g': ones_noise((DEPTH, D_MODEL)),
        'ln_b': nrm((DEPTH, D_MODEL), 0.01),
    }


def reference(x_prompt, x_sample, cache_diff_k, cache_diff_v, cache_mla_latent, cache_mla_krope,
              state_ssm, state_conv, c_prompt, c_sample, w_mod, b_mod, w_in, conv_w, conv_b,
              dt_bias, a_log, d_skip, ssd_norm_w, lambda_q1, lambda_k1, lambda_q2, lambda_k2,
              diff_norm_w, mla_q_norm_w, mla_kv_norm_w, w_uq, w_uk, w_uv, w_out, ln_g, ln_b):
    bp = x_prompt.shape[0]
    dtp = x_prompt.dtype
    hp, hs = x_prompt, x_sample
    st_prompt, st_sample = [], []
    for l in range(DEPTH):
        p = dict(w_mod=w_mod[l], b_mod=b_mod[l], w_in=w_in[l], conv_w=conv_w[l], conv_b=conv_b[l],
                 dt_bias=dt_bias[l], a_log=a_log[l], d_skip=d_skip[l], ssd_norm_w=ssd_norm_w[l],
                 lambda_q1=lambda_q1[l], lambda_k1=lambda_k1[l], lambda_q2=lambda_q2[l], lambda_k2=lambda_k2[l],
                 diff_norm_w=diff_norm_w[l], mla_q_norm_w=mla_q_norm_w[l], mla_kv_norm_w=mla_kv_norm_w[l],
                 w_uq=w_uq[l], w_uk=w_uk[l], w_uv=w_uv[l], w_out=w_out[l], ln_g=ln_g[l], ln_b=ln_b[l])
        hp, st_p = trunk_layer(
            hp, c_prompt,
            jnp.zeros((bp, 0, DIFF_HEADS, 2, DIFF_QK_DIM), dtp),
            jnp.zeros((bp, 0, DIFF_HEADS, DIFF_V_DIM), dtp),
            jnp.zeros((bp, 0, MLA_KV_RANK), dtp),
            jnp.zeros((bp, 0, MLA_ROPE_DIM), dtp),
            jnp.zeros((bp, SSD_HEADS, SSD_HEAD_DIM, SSD_STATE), dtp),
            jnp.zeros((bp, SSD_CONV - 1, SSD_CONV_DIM), dtp),
            p, l)
        hs, st_s = trunk_layer(
            hs, c_sample, cache_diff_k[l], cache_diff_v[l], cache_mla_latent[l], cache_mla_krope[l],
            state_ssm[l], state_conv[l], p, l)
        st_prompt.append(st_p)
        st_sample.append(st_s)
    diff_k_prompt = jnp.stack([s[0] for s in st_prompt])
    diff_v_prompt = jnp.stack([s[1] for s in st_prompt])
    mla_latent_prompt = jnp.stack([s[2] for s in st_prompt])
    mla_krope_prompt = jnp.stack([s[3] for s in st_prompt])
    ssm_prompt = jnp.stack([s[4] for s in st_prompt])
    conv_prompt = jnp.stack([s[5] for s in st_prompt])
    diff_k_sample = jnp.stack([s[0] for s in st_sample])
    diff_v_sample = jnp.stack([s[1] for s in st_sample])
    mla_latent_sample = jnp.stack([s[2] for s in st_sample])
    mla_krope_sample = jnp.stack([s[3] for s in st_sample])
    ssm_sample = jnp.stack([s[4] for s in st_sample])
    conv_sample = jnp.stack([s[5] for s in st_sample])
    return (hp, hs, diff_k_prompt, diff_v_prompt, mla_latent_prompt, mla_krope_prompt, ssm_prompt, conv_prompt,
            diff_k_sample, diff_v_sample, mla_latent_sample, mla_krope_sample, ssm_sample, conv_sample)
```

```python
import math
from contextlib import ExitStack
import numpy as np
import concourse.bass as bass
import concourse.mybir as mybir
from concourse.bass_utils import run_bass_kernel_spmd

F32 = mybir.dt.float32
BF16 = mybir.dt.bfloat16
AF = mybir.ActivationFunctionType
ALU = mybir.AluOpType
AX = mybir.AxisListType

D = 4096
KT = 32
DEPTH = 2
SEQ = 4096
PAST = 4096
DEC_SEQ = 32
NH = 32
HP = 64
NS = 128
CONVD = 3072
INW = 11360
ROPE_THETA = 500000.0
ALPHA = (2 * DEPTH) ** 0.25
DIFF_SCALE = 64 ** -0.5
MLA_SCALE = 192 ** -0.5
O_Z, O_XBC, O_DT, O_DQ, O_DK, O_DV, O_DG, O_CQ, O_CKV, O_KR, O_MG = (
    0, 2048, 5120, 5152, 6176, 7200, 8224, 9248, 10016, 10272, 10336)
BW = 256

SEM_LIMIT = 30000


class Counter:
    def __init__(self, prog, step, kind="eng", depth=0):
        self.prog, self.step, self.kind, self.depth = prog, step, kind, depth
        self.sem, self.val, self.gen = None, 0, -1
        self.max_wait = self.safe = 0
        self.final = {}
        prog.register_counter(self)

    def bump(self):
        if self.sem is None or self.val + self.step > SEM_LIMIT:
            if self.sem is not None:
                self.final[self.gen] = (self.sem, self.val)
            self.sem, self.val = self.prog.new_sem(self.kind)
            self.max_wait = self.safe = self.val
            self.gen += 1
        self.val += self.step
        return (self.sem, self.val, self, self.gen)


_DEPTH = [0]


class Buf:
    __slots__ = ("w", "r", "ctr", "name", "depth", "excl")

    def __init__(self, name="", excl=False):
        self.excl = excl
        self.w = None
        self.r = {}
        self.ctr = None
        self.name = name
        self.depth = _DEPTH[0]


class _Scope:
    def __init__(self, prog):
        self.prog = prog
        self.st = ExitStack()

    def __enter__(self):
        self.prog.scope_stack.append([])
        _DEPTH[0] = len(self.prog.scope_stack)
        return self.st

    def __exit__(self, *a):
        P = self.prog
        ctrs = P.scope_stack.pop()
        _DEPTH[0] = len(P.scope_stack)
        names = ("pe", "act", "dve", "pool", "sp")
        for n in names:
            E = P.engs[n]
            for c in ctrs:
                if c.sem is not None and c.step == 16:
                    P._wait_raw(E, (c.sem, c.val))
                    c.max_wait = max(c.max_wait, c.val)
        P.barrier(names)
        for c in ctrs:
            if c.sem is not None and c.step == 16:
                P.free_sems.setdefault(c.kind, []).append((c.sem, c.val))
                c.dead = True
        self.st.close()
        return False


class Eng:
    def __init__(self, prog, name, h):
        self.name, self.h = name, h
        self.ctr = Counter(prog, 1)
        self.waited = {}
        self.last = None


class Prog:
    def __init__(self, nc, es):
        self.nc, self.es = nc, es
        self.nsem = 0
        self.free_sems = {}
        self.log = []
        self.semname = {}
        self.scope_stack = []
        self.all_counters = []
        self.engs = {
            "pe": Eng(self, "pe", nc.tensor), "act": Eng(self, "act", nc.scalar),
            "dve": Eng(self, "dve", nc.vector), "pool": Eng(self, "pool", nc.gpsimd),
            "sp": Eng(self, "sp", nc.sync),
        }
        self.ninstr = 0

    def new_sem(self, kind):
        fs = self.free_sems.setdefault(kind, [])
        for i, (sem, val) in enumerate(fs):
            if val < SEM_LIMIT // 2:
                fs.pop(i)
                return sem, val
        self.nsem += 1
        sem = self.es.enter_context(self.nc.semaphore(f"s{self.nsem}"))
        self.semname[id(sem)] = f"s{self.nsem}"
        return sem, 0

    def register_counter(self, c):
        self.all_counters.append(c)
        if c.depth > 0:
            self.scope_stack[c.depth - 1].append(c)

    def scope(self):
        return _Scope(self)

    def _wait(self, E, tok):
        if tok is None:
            return
        sem, val, ctr, gen, owner = tok
        if owner == "pe" and E.name == "pe":
            return
        if owner is None:
            if gen == ctr.gen:
                sem, val = ctr.sem, ctr.val
                ctr.max_wait = max(ctr.max_wait, val)
            else:
                sem, val = ctr.final[gen]
        key = id(sem)
        if E.waited.get(key, 0) >= val:
            return
        E.h.wait_ge(sem, val)
        E.waited[key] = val
        self.log.append(("wait", E.name, self.semname.get(id(sem)), val, None))

    def _deps(self, E, reads, writes):
        for b in reads:
            self._wait(E, b.w)
            if b.excl:
                for t in b.r.values():
                    if t[4] != E.name:
                        self._wait(E, t)
        for b in writes:
            self._wait(E, b.w)
            for t in b.r.values():
                self._wait(E, t)

    def _record(self, tok, reads, writes):
        for b in reads:
            b.r[id(tok[2])] = tok
        for b in writes:
            b.w = tok
            b.r = {}

    def op(self, eng, fn, reads=(), writes=()):
        E = self.engs[eng]
        self._deps(E, reads, writes)
        ins = fn()
        sem, val, ctr, gen = E.ctr.bump()
        ins.then_inc(sem, 1)
        tok = (sem, val, ctr, gen, eng)
        E.last = tok
        self._record(tok, reads, writes)
        self.ninstr += 1
        return tok

    def dma(self, q, out, in_, reads=(), writes=(), cbuf=None, nc_ok=False):
        E = self.engs[q]
        self._deps(E, reads, writes)
        kind = "sw" if q == "pool" else "hw"
        if cbuf.ctr is None:
            cbuf.ctr = {}
        if kind not in cbuf.ctr:
            cbuf.ctr[kind] = Counter(self, 16, kind, cbuf.depth)
        c = cbuf.ctr[kind]
        if c.sem is not None and c.max_wait > c.safe and c.val + 16 <= SEM_LIMIT:
            self._wait_raw(E, (c.sem, c.val))
            c.safe = c.val
        if nc_ok:
            ins = E.h.dma_start(out=out, in_=in_, allow_slow_non_contiguous=True)
        else:
            ins = E.h.dma_start(out=out, in_=in_)
        sem, val, ctr, gen = c.bump()
        ins.then_inc(sem, 16)
        self.log.append(("dma", q, self.semname.get(id(sem)), val, cbuf.name))
        tok = (sem, val, ctr, gen, None)
        self._record(tok, reads, writes)
        self.ninstr += 1
        return tok

    def barrier(self, names=("pe", "act", "dve", "pool")):
        toks = [self.engs[n].last for n in names]
        for n in names:
            E = self.engs[n]
            for t in toks:
                if t is not None and t[4] != n:
                    self._wait_raw(E, t)

    def _wait_raw(self, E, tok):
        sem, val = tok[0], tok[1]
        key = id(sem)
        if E.waited.get(key, 0) >= val:
            return
        E.h.wait_ge(sem, val)
        E.waited[key] = val
        self.log.append(("waitraw", E.name, self.semname.get(id(sem)), val, None))


def in_blocks():
    bl = []
    bl.append(("dtb", "TM", O_DT, BW))
    bl.append(("krb", "TM", O_KR, BW))
    for i in range(12):
        bl.append((f"xbc{i}", "FM", O_XBC + i * BW, BW))
    for i in range(8):
        bl.append((f"z{i}", "FM", O_Z + i * BW, BW))
    for i in range(4):
        bl.append((f"dq{i}", "TM", O_DQ + i * BW, BW))
    for i in range(4):
        bl.append((f"dk{i}", "TM", O_DK + i * BW, BW))
    for i in range(4):
        bl.append((f"dv{i}", "TM", O_DV + i * BW, BW))
    for i in range(4):
        bl.append((f"dg{i}", "FM", O_DG + i * BW, BW))
    for i in range(3):
        bl.append((f"cq{i}", "TM", O_CQ + i * BW, BW))
    bl.append(("ckv", "TM", O_CKV, BW))
    for i in range(4):
        bl.append((f"mg{i}", "FM", O_MG + i * BW, BW))
    return bl


IN_BLOCKS = in_blocks()
NIB = len(IN_BLOCKS)
NOB = D // BW


class _StopBuild(Exception):
    pass


def build(T, NL=DEPTH, do_sample=True, stop_after=None, DL=DEPTH):
    assert T % 512 == 0
    nc = bass.Bass("TRN2", target_bir_lowering=False)
    TS = 64
    TK = PAST + DEC_SEQ

    def din(name, shape, dt=F32):
        return nc.dram_tensor(name, list(shape), dt, kind="ExternalInput").ap()

    def dout(name, shape, dt=F32):
        return nc.dram_tensor(name, list(shape), dt, kind="ExternalOutput").ap()

    def dint(name, shape, dt=F32):
        return nc.dram_tensor(name, list(shape), dt, kind="Internal").ap()

    x_p = din("x_p", [T, D]); x_s = din("x_s", [TS, D]); cT = din("cT", [128, KT, 3])
    ck = din("ck", [DEPTH, 2, PAST, 1024]); cv = din("cv", [DEPTH, 2, PAST, 1024])
    cl = din("cl", [DEPTH, 2, PAST, 256]); cr = din("cr", [DEPTH, 2, PAST, 64])
    st_ssm = din("st_ssm", [DEPTH, 2, 2048, 128]); st_conv = din("st_conv", [DEPTH, 2, 3, CONVD])
    w_mod = din("w_mod", [DL, D, 3 * D]); b_mod = din("b_mod", [DL, 3 * D])
    w_in = din("w_in", [DL, D, INW]); conv_w = din("conv_w", [DEPTH, 4, CONVD]); conv_b = din("conv_b", [DEPTH, CONVD])
    dt_bias = din("dt_bias", [DEPTH, NH]); a_log = din("a_log", [DEPTH, NH]); d_skip = din("d_skip", [DEPTH, NH])
    ssd_norm_w = din("ssd_norm_w", [DEPTH, 2048])
    lam_q1 = din("lambda_q1", [DEPTH, 64]); lam_k1 = din("lambda_k1", [DEPTH, 64])
    lam_q2 = din("lambda_q2", [DEPTH, 64]); lam_k2 = din("lambda_k2", [DEPTH, 64])
    diff_norm_w = din("diff_norm_w", [DEPTH, 128]); q_norm_w = din("mla_q_norm_w", [DEPTH, 768])
    kv_norm_w = din("mla_kv_norm_w", [DEPTH, 256])
    w_uq = din("w_uq", [DEPTH, 768, 1536]); w_ukT = din("w_ukT", [DEPTH, 128, 8, 256]); w_uv = din("w_uv", [DEPTH, 256, 8, 128])
    w_out = din("w_out", [DL, D, D]); ln_g = din("ln_g", [DEPTH, D]); ln_b = din("ln_b", [DEPTH, D])
    rope_d_p = din("rope_d_p", [T, 16]); rope_m_p = din("rope_m_p", [T, 64])
    rope_d_s = din("rope_d_s", [TS, 16]); rope_m_s = din("rope_m_s", [TS, 64])
    tri_in = din("tri", [128, 128]); negmask_in = din("negmask", [128, 128])

    y_p = dout("y_p", [T, D]); y_s = dout("y_s", [TS, D])
    dk_p = dout("dk_p", [DEPTH, T, 1024]); dv_p = dout("dv_p", [DEPTH, T, 1024])
    lat_p = dout("lat_p", [DEPTH, T, 256]); kr_p = dout("kr_p", [DEPTH, T, 64])
    ssm_p = dout("ssm_p", [DEPTH, 2048, 128]); conv_p = dout("conv_p", [DEPTH, 3, CONVD])
    dk_s = dout("dk_s", [DEPTH, TS, 1024]); dv_s = dout("dv_s", [DEPTH, TS, 1024])
    lat_s = dout("lat_s", [DEPTH, TS, 256]); kr_s = dout("kr_s", [DEPTH, TS, 64])
    ssm_s = dout("ssm_s", [DEPTH, 2, 2048, 128]); conv_s = dout("conv_s", [DEPTH, 2, 3, CONVD])

    w_in_bf = dint("w_in_bf", [DEPTH, NIB, 128, KT, BW], BF16)
    w_out_bf = dint("w_out_bf", [DEPTH, NOB, 128, KT, BW], BF16)
    w_uq_bf = dint("w_uq_bf", [DEPTH, 128, 6, 1536], BF16)
    w_ukT_bf = dint("w_ukT_bf", [DEPTH, 128, 8, 256], BF16)
    w_uv_bf = dint("w_uv_bf", [DEPTH, 128, 2, 8, 128], BF16)
    mod_scr = dint("mod_scr", [DEPTH, 3, 3 * D])
    x1_p = dint("x1_p", [T, D]); x1_s = dint("x1_s", [TS, D])
    TKP = T
    KT_scr = [dint("KT_p", [8, 128, TKP], BF16), dint("KT_s0", [8, 128, TK], BF16), dint("KT_s1", [8, 128, TK], BF16)]
    V_scr = [dint("V_p", [TKP, 1024], BF16), dint("V_s0", [TK, 1024], BF16), dint("V_s1", [TK, 1024], BF16)]
    LT_scr = [dint("LT_p", [2, 128, TKP], BF16), dint("LT_s0", [2, 128, TK], BF16), dint("LT_s1", [2, 128, TK], BF16)]
    L_scr = [dint("L_p", [TKP, 256], BF16), dint("L_s0", [TK, 256], BF16), dint("L_s1", [TK, 256], BF16)]
    RT_scr = [dint("RT_p", [64, TKP], BF16), dint("RT_s0", [64, TK], BF16), dint("RT_s1", [64, TK], BF16)]
    scrB = [dict(KT=Buf(), V=Buf(), LT=Buf(), L=Buf(), RT=Buf()) for _ in range(3)]

    es = ExitStack()
    P = Prog(nc, es)

    uniq = [0]

    def sb(name, shape, dt, stack=None):
        uniq[0] += 1
        return (stack or es).enter_context(nc.sbuf_tensor(f"{name}_{uniq[0]}", list(shape), dt))

    def ps(name, shape, dt):
        return es.enter_context(nc.psum_tensor(name, list(shape), dt))

    PB = [ps(f"pb{i}", [128, 512], F32) for i in range(6)]
    PBb = [Buf(f"pb{i}", excl=True) for i in range(6)]
    PT = [ps(f"pt{i}", [128, 1024], BF16) for i in range(2)]
    PTb = [Buf(f"pt{i}", excl=True) for i in range(2)]
    pb_rr = [0]
    pt_rr = [0]

    def next_pb(lo=0, hi=6):
        i = lo + pb_rr[0] % (hi - lo)
        pb_rr[0] += 1
        return i

    def next_pt():
        i = pt_rr[0] % 2
        pt_rr[0] += 1
        return i

    uT = sb("uT", [128, KT, 512], BF16); uTb = Buf("uT")
    mixT = sb("mixT", [128, KT, 512], BF16); mixTb = [Buf(f"mix{i}") for i in range(3)]
    NWS = 2
    WS = [sb(f"ws{i}", [128, KT, BW], BF16) for i in range(NWS)]
    WSb = [Buf(f"ws{i}") for i in range(NWS)]
    ident = sb("ident", [128, 128], BF16); identb = Buf("ident")
    identf = sb("identf", [128, 128], F32)
    tri = sb("tri", [128, 128], F32); negmask = sb("negmask", [128, 128], F32)
    onesf = sb("onesf", [128, 128], F32)
    ones512 = sb("ones512", [128, 128], BF16)
    constb = Buf("const")
    S32 = sb("S32", [128, 2048], F32); Sbf = sb("Sbf", [128, 2048], BF16)
    Sb = [Buf(f"S{i}") for i in range(4)]
    sc1 = sb("sc1", [128, 3, KT], F32); sh = sb("sh", [128, 3, KT], F32); modb = Buf("mod")
    convw = sb("convw", [128, 24, 4], F32); convb = sb("convb", [128, 24], F32)
    dtb_bc = sb("dtb_bc", [128, NH], F32); A_bc = sb("A_bc", [128, NH], F32); Dcol = sb("Dcol", [128, 16], F32)
    normw = sb("normw", [128, 16], F32)
    dnw_bc = sb("dnw_bc", [128, 128], F32); qnw_bc = sb("qnw_bc", [128, 768], F32); kvnw_bc = sb("kvnw_bc", [128, 256], F32)
    neglam = sb("neglam", [128, 1], F32)
    lamt = sb("lamt", [128, 4, 64], F32); lams = sb("lams", [128, 2], F32)
    parb = Buf("par")
    carry = sb("carry", [128, 24, 4], BF16); carryb = Buf("carry")
    kr_keep = sb("kr_keep", [128, 4, 64], F32); krb = Buf("kr_keep")

    P.dma("sp", tri[:], tri_in[:, :], writes=[constb], cbuf=constb)
    P.dma("sp", negmask[:], negmask_in[:, :], writes=[constb], cbuf=constb)
    P.op("pool", lambda: nc.gpsimd.memset(identf[:], 0.0), writes=[constb])
    P.op("pool", lambda: nc.gpsimd.affine_select(out=identf[:], in_=identf[:], pattern=[[-1, 128]],
                                                 compare_op=ALU.not_equal, fill=1.0, base=0, channel_multiplier=1),
         writes=[constb])
    P.op("pool", lambda: nc.gpsimd.tensor_copy(out=ident[:], in_=identf[:]), writes=[constb, identb])
    P.op("pool", lambda: nc.gpsimd.memset(onesf[:], 1.0), writes=[constb])
    P.op("pool", lambda: nc.gpsimd.memset(ones512[:], 1.0 / 512.0), writes=[constb])
    P.op("pool", lambda: nc.gpsimd.memset(uT[:], 0.0), writes=[uTb])
    P.op("pool", lambda: nc.gpsimd.memset(mixT[:], 0.0), writes=mixTb)

    wcast = [Buf(f"wcast{l}") for l in range(DEPTH)]

    def cast_weights(l):
        wc = wcast[l]
        for bi, (name, kind, c0, w) in enumerate(IN_BLOCKS):
            P.dma("pool", w_in_bf[l, bi, :, :, 0:w],
                  w_in[l, :, c0:c0 + w].rearrange("(k p) w -> p k w", p=128), writes=[wc], cbuf=wc)
        for bi in range(NOB):
            P.dma("pool", w_out_bf[l, bi], w_out[l, :, bi * BW:(bi + 1) * BW].rearrange("(k p) w -> p k w", p=128),
                  writes=[wc], cbuf=wc)
        P.dma("pool", w_uq_bf[l], w_uq[l].rearrange("(k p) w -> p k w", p=128), writes=[wc], cbuf=wc)
        P.dma("pool", w_ukT_bf[l], w_ukT[l], writes=[wc], cbuf=wc)
        P.dma("pool", w_uv_bf[l], w_uv[l].rearrange("(k p) h d -> p k h d", p=128), writes=[wc], cbuf=wc)

    for l in range(NL):
        cast_weights(l)

    with P.scope() as st:
      if stop_after is None or stop_after >= 1:
            scT = sb("scT", [128, KT, 3], F32, st); scTb = sb("scTb", [128, KT, 3], BF16, st)
            modrow = [sb(f"modrow{i}", [3, BW], F32, st) for i in range(2)]; bmod = [sb(f"bmod{i}", [3, BW], F32, st) for i in range(2)]
            mrb = [Buf("mr0"), Buf("mr1")]
            tb = Buf("modtmp")
            mws = 0
            P.dma("sp", scT[:], cT[:, :, :], writes=[tb], cbuf=tb)
            P.op("act", lambda: nc.scalar.activation(out=scTb[:], in_=scT[:], func=AF.Silu), reads=[tb], writes=[tb])
            for l in range(NL):
                for cb in range(3 * D // BW):
                    si = mws % NWS
                    mi = mws % 2
                    mws += 1
                    P.dma("pool", WS[si][:], w_mod[l, :, cb * BW:(cb + 1) * BW].rearrange("(k p) w -> p k w", p=128),
                          writes=[WSb[si]], cbuf=WSb[si])
                    P.dma("sp", bmod[mi][:], b_mod[l:l + 1, cb * BW:(cb + 1) * BW].partition_broadcast(3), writes=[mrb[mi]], cbuf=mrb[mi])
                    bi = next_pb()

                    def mm(si=si, bi=bi):
                        for k in range(KT):
                            ins = nc.tensor.matmul(PB[bi][0:3, 0:BW], lhsT=scTb[:, k, :], rhs=WS[si][:, k, :],
                                                   start=(k == 0), stop=(k == KT - 1))
                        return ins
                    P.op("pe", mm, reads=[WSb[si], tb], writes=[PBb[bi]])
                    P.op("dve", lambda bi=bi, mi=mi: nc.vector.tensor_tensor(
                        out=modrow[mi][:], in0=PB[bi][0:3, 0:BW], in1=bmod[mi][:], op=ALU.add),
                        reads=[PBb[bi], mrb[mi]], writes=[mrb[mi]])
                    P.dma("sp", mod_scr[l, :, cb * BW:(cb + 1) * BW], modrow[mi][:], reads=[mrb[mi]], writes=[modb], cbuf=mrb[mi])

    groups = []
    for gi in range(T // 512):
        groups.append(dict(
            G=512, tiles=[(gi * 512 + i * 128, 128, 0) for i in range(4)], segs=[(0, 0, 512, [0, 1, 2, 3])], prompt=True, g0=gi * 512,
            xin=[x_p, x1_p], xout=[x1_p if NL > 1 else y_p, y_p], rope_d=rope_d_p, rope_m=rope_m_p))
    if do_sample:
        groups.append(dict(
            G=160, tiles=[(0, 32, 1), (32, 32, 2)], segs=[(1, 0, 32, [0]), (2, 128, 32, [1])], prompt=False, g0=0,
            xin=[x_s, x1_s], xout=[x1_s if NL > 1 else y_s, y_s], rope_d=rope_d_s, rope_m=rope_m_s))

    wseq = []
    for l in range(NL):
        for grp in groups:
            for bi in range(NIB):
                wseq.append((w_in_bf[l, bi], l, IN_BLOCKS[bi][3]))
            for t in range(len(grp["tiles"])):
                for bi in range(NOB):
                    wseq.append((w_out_bf[l, bi], l, BW))
    wstate = dict(issued=0, consumed=0)

    def next_w():
        while wstate["issued"] < min(len(wseq), wstate["consumed"] + NWS):
            i = wstate["issued"]
            si = i % NWS
            ap, l, w = wseq[i]
            P.dma("sp", WS[si][:, :, 0:w], ap[:, :, 0:w], reads=[wcast[l]], writes=[WSb[si]], cbuf=WSb[si])
            wstate["issued"] += 1
        si = wstate["consumed"] % NWS
        wstate["consumed"] += 1
        return si

    def phase(n):
        if stop_after is not None and n > stop_after:
            raise _StopBuild()

    def layer_setup(l):
        for s in range(3):
            P.dma("sp", sh[:, s, :], mod_scr[l, s, 0:D].rearrange("(k p) -> p k", p=128), reads=[modb], writes=[parb],
                  cbuf=parb, nc_ok=True)
            P.dma("sp", sc1[:, s, :], mod_scr[l, s, D:2 * D].rearrange("(k p) -> p k", p=128), reads=[modb], writes=[parb],
                  cbuf=parb, nc_ok=True)
        P.op("dve", lambda: nc.vector.tensor_scalar_add(out=sc1[:], in0=sc1[:], scalar1=1.0), reads=[parb], writes=[parb])
        for j in range(4):
            P.dma("sp", convw[:, :, j], conv_w[l, j].rearrange("(t p) -> p t", p=128), writes=[parb], cbuf=parb, nc_ok=True)
        P.dma("sp", convb[:], conv_b[l].rearrange("(t p) -> p t", p=128), writes=[parb], cbuf=parb, nc_ok=True)
        P.dma("sp", dtb_bc[:], dt_bias[l:l + 1, :].partition_broadcast(128), writes=[parb], cbuf=parb)
        P.dma("sp", A_bc[:], a_log[l:l + 1, :].partition_broadcast(128), writes=[parb], cbuf=parb)
        P.op("act", lambda: nc.scalar.activation(out=A_bc[:], in_=A_bc[:], func=AF.Exp), reads=[parb], writes=[parb])
        P.op("dve", lambda: nc.vector.tensor_scalar_mul(out=A_bc[:], in0=A_bc[:], scalar1=-1.0), reads=[parb], writes=[parb])
        for hh in range(2):
            src = bass.AP(d_skip.tensor, d_skip[l, hh:hh + 1].offset, [[0, 64], [2, 16]])
            P.dma("sp", Dcol[hh * 64:(hh + 1) * 64, :], src, writes=[parb], cbuf=parb, nc_ok=True)
        P.dma("sp", normw[:], ssd_norm_w[l].rearrange("(t p) -> p t", p=128), writes=[parb], cbuf=parb, nc_ok=True)
        P.dma("sp", dnw_bc[:], diff_norm_w[l:l + 1, :].partition_broadcast(128), writes=[parb], cbuf=parb)
        lam_init = 0.8 - 0.6 * math.exp(-0.3 * l)
        P.op("dve", lambda: nc.vector.tensor_scalar_mul(out=dnw_bc[:], in0=dnw_bc[:], scalar1=1.0 - lam_init),
             reads=[parb], writes=[parb])
        P.dma("sp", qnw_bc[:], q_norm_w[l:l + 1, :].partition_broadcast(128), writes=[parb], cbuf=parb)
        P.dma("sp", kvnw_bc[:], kv_norm_w[l:l + 1, :].partition_broadcast(128), writes=[parb], cbuf=parb)
        for i, v in enumerate((lam_q1, lam_k1, lam_q2, lam_k2)):
            P.dma("sp", lamt[:, i, :], v[l:l + 1, :].partition_broadcast(128), writes=[parb], cbuf=parb)
        P.op("dve", lambda: nc.vector.tensor_tensor(out=lamt[:, 0, :], in0=lamt[:, 0, :], in1=lamt[:, 1, :], op=ALU.mult),
             reads=[parb], writes=[parb])
        P.op("dve", lambda: nc.vector.tensor_tensor(out=lamt[:, 2, :], in0=lamt[:, 2, :], in1=lamt[:, 3, :], op=ALU.mult),
             reads=[parb], writes=[parb])
        P.op("dve", lambda: nc.vector.reduce_sum(out=lams[:, 0:1], in_=lamt[:, 0, :], axis=AX.X), reads=[parb], writes=[parb])
        P.op("dve", lambda: nc.vector.reduce_sum(out=lams[:, 1:2], in_=lamt[:, 2, :], axis=AX.X), reads=[parb], writes=[parb])
        P.op("act", lambda: nc.scalar.activation(out=lams[:], in_=lams[:], func=AF.Exp), reads=[parb], writes=[parb])
        P.op("dve", lambda: nc.vector.tensor_tensor(out=neglam[:], in0=lams[:, 1:2], in1=lams[:, 0:1], op=ALU.subtract),
             reads=[parb], writes=[parb])
        P.op("dve", lambda: nc.vector.tensor_scalar_add(out=neglam[:], in0=neglam[:], scalar1=-lam_init),
             reads=[parb], writes=[parb])

    def transposes(items, evac, reads):
        i = 0
        while i < len(items):
            chunk = items[i:i + 8]
            ti = next_pt()

            def tr(chunk=chunk, ti=ti):
                for j, src in enumerate(chunk):
                    r, c = src.shape[0], src.shape[1]
                    ins = nc.tensor.transpose(PT[ti][0:c, j * 128:j * 128 + r], src, ident[0:r, 0:r])
                return ins
            P.op("pe", tr, reads=list(reads) + [identb], writes=[PTb[ti]])
            evac(ti, i, len(chunk))
            i += 8

    def rope_tm(eng, xv, tab, half, nr, nhd, tmpA, tmpB, bufs_r, bufs_w):
        eh = nc.vector if eng == "dve" else nc.gpsimd
        x1 = xv[:, :, 0:half]; x2 = xv[:, :, half:2 * half]
        cos = tab[:, 0:half].unsqueeze(1).to_broadcast([nr, nhd, half])
        sin = tab[:, half:2 * half].unsqueeze(1).to_broadcast([nr, nhd, half])
        a = tmpA[0:nr, 0:nhd, 0:half]; b = tmpB[0:nr, 0:nhd, 0:half]
        P.op(eng, lambda: eh.tensor_tensor(out=a, in0=x1, in1=sin, op=ALU.mult), reads=bufs_r, writes=bufs_w)
        P.op(eng, lambda: eh.tensor_tensor(out=b, in0=x2, in1=sin, op=ALU.mult), reads=bufs_r, writes=bufs_w)
        P.op(eng, lambda: eh.tensor_tensor(out=x1, in0=x1, in1=cos, op=ALU.mult), reads=bufs_r, writes=bufs_w)
        P.op(eng, lambda: eh.tensor_tensor(out=x2, in0=x2, in1=cos, op=ALU.mult), reads=bufs_r, writes=bufs_w)
        P.op(eng, lambda: eh.tensor_tensor(out=x1, in0=x1, in1=b, op=ALU.subtract), reads=bufs_r, writes=bufs_w)
        P.op(eng, lambda: eh.tensor_tensor(out=x2, in0=x2, in1=a, op=ALU.add), reads=bufs_r, writes=bufs_w)

    def rsqrt(out, in_, scale, eps, reads, writes):
        P.op("act", lambda: nc.scalar.activation(out=out, in_=in_, func=AF.Ln, scale=scale, bias=eps), reads=reads, writes=writes)
        P.op("act", lambda: nc.scalar.activation(out=out, in_=out, func=AF.Exp, scale=-0.5), reads=writes, writes=writes)

    outb = Buf("out")
    x1b = Buf("x1")
    NKT_MAX = 33

    def process_group(l, grp):
        G = grp["G"]; tiles = grp["tiles"]; segs = grp["segs"]; NT = len(tiles)
        xin = grp["xin"][l]; xout = grp["xout"][l]
        is_p = grp["prompt"]
        g0 = grp["g0"]
        rope_d = grp["rope_d"]; rope_m = grp["rope_m"]
        last_grp = (not is_p) or (g0 + G == T)

        def kbase_of(s):
            return g0 if s == 0 else PAST

        def fm_block(si, w, evac):
            for sub in range(w // 128):
                bi = next_pb()

                def mm(sub=sub, bi=bi):
                    for k in range(KT):
                        ins = nc.tensor.matmul(PB[bi][:, 0:G], lhsT=WS[si][:, k, sub * 128:(sub + 1) * 128], rhs=uT[:, k, 0:G],
                                               start=(k == 0), stop=(k == KT - 1))
                    return ins
                P.op("pe", mm, reads=[WSb[si], uTb], writes=[PBb[bi]])
                evac(bi, sub)

        def tm_block(si, w, evac):
            per_bank = 512 // w
            ti = 0
            while ti < NT:
                bi = next_pb()
                grp_t = list(range(ti, min(NT, ti + per_bank)))

                def mm(bi=bi, grp_t=grp_t):
                    for jj, t in enumerate(grp_t):
                        r0, nr, s = tiles[t]
                        for k in range(KT):
                            ins = nc.tensor.matmul(PB[bi][0:nr, jj * w:(jj + 1) * w], lhsT=uT[:, k, t * 128:t * 128 + nr],
                                                   rhs=WS[si][:, k, 0:w], start=(k == 0), stop=(k == KT - 1))
                    return ins
                P.op("pe", mm, reads=[WSb[si], uTb], writes=[PBb[bi]])
                for jj, t in enumerate(grp_t):
                    evac(bi, jj * w, t)
                ti += per_bank

        phase(2)
        with P.scope() as st:
            xt = sb("xt", [128, D], F32, st); xtb = Buf("xt")
            xb = sb("xb", [128, D], BF16, st); xbb = Buf("xb")
            for t, (r0, nr, s) in enumerate(tiles):
                P.dma("sp", xt[0:nr, :], xin[r0:r0 + nr, :], writes=[xtb], cbuf=xtb)
                P.op("dve", lambda nr=nr: nc.vector.tensor_copy(out=xb[0:nr, :], in_=xt[0:nr, :]), reads=[xtb], writes=[xbb])
                c0 = t * 128
                for k8 in range(KT // 8):
                    pti = next_pt()

                    def tr(k8=k8, pti=pti, nr=nr):
                        for j in range(8):
                            k = k8 * 8 + j
                            ins = nc.tensor.transpose(PT[pti][:, j * 128:j * 128 + nr], xb[0:nr, k * 128:(k + 1) * 128],
                                                      ident[0:nr, 0:nr])
                        return ins
                    P.op("pe", tr, reads=[xbb, identb], writes=[PTb[pti]])
                    for j in range(8):
                        k = k8 * 8 + j
                        if j % 2 == 0:
                            P.op("act", lambda k=k, j=j, s=s, pti=pti, c0=c0, nr=nr: nc.scalar.activation(
                                out=uT[:, k, c0:c0 + nr], in_=PT[pti][:, j * 128:j * 128 + nr], func=AF.Identity,
                                scale=sc1[:, s, k:k + 1], bias=sh[:, s, k:k + 1]), reads=[PTb[pti], parb], writes=[uTb])
                        else:
                            P.op("dve", lambda k=k, j=j, s=s, pti=pti, c0=c0, nr=nr: nc.vector.tensor_scalar(
                                out=uT[:, k, c0:c0 + nr], in0=PT[pti][:, j * 128:j * 128 + nr],
                                scalar1=sc1[:, s, k:k + 1], scalar2=sh[:, s, k:k + 1], op0=ALU.mult, op1=ALU.add),
                                reads=[PTb[pti], parb], writes=[uTb])

        phase(3)
        with P.scope() as st:
            NSEG = len(segs)
            SEGW = max(sn for (_, _, sn, _) in segs) + 4
            xbcT = sb("xbcT", [128, 24, NSEG * SEGW], BF16, st); xbcb = Buf("xbcT")
            sz = sb("sz", [128, 16, G], BF16, st); szb = Buf("sz")
            dt_tm = sb("dt_tm", [128, NT, NH], F32, st); dtb = Buf("dt")
            a_tm = sb("a_tm", [128, NT, NH], F32, st)
            c32 = sb("c32", [128, 24, NSEG * 4], F32, st); c32b = Buf("c32")
            tmp5 = sb("tmp5", [128, NT, NH], F32, st)
            CH = 128
            xcT = sb("xcT", [128, 24, CH], BF16, st); xcTb = Buf("xcT")
            ctmp = sb("ctmp", [128, 2, CH], F32, st); ctb = [Buf("ct0"), Buf("ct1")]
            ctmp2 = sb("ctmp2", [128, CH], F32, st); ct2b = Buf("ct2")
            xtok = sb("xtok", [128, 2048], BF16, st); xtokb = Buf("xtok")
            Btok = sb("Btok", [128, 4, 128], BF16, st); Btokb = Buf("Btok")
            xd = xtok; xdb = xtokb
            xdw = sb("xdw", [128, 2048], BF16, st); xdwb = Buf("xdw")
            acs = sb("acs", [128, NH], F32, st); acsb = Buf("acs")
            rhsb = sb("rhsb", [128, 8, CH], F32, st); rhsbb = Buf("rhsb")
            sg1 = sb("sg1", [128, 8, CH], F32, st); sg1b = Buf("sg1")
            Et = sb("Et", [128, 8, CH], F32, st); Etb = Buf("Et")
            MT = sb("MT", [128, 8, CH], BF16, st); MTb = Buf("MT")
            Cp = sb("Cp", [128, 8, CH], BF16, st); Cpb = Buf("Cp")
            te = sb("te", [128, 8], F32, st); teb = Buf("te")
            ygf = sb("ygf", [128, 4, CH], F32, st); ygb = Buf("ygf")
            sq = sb("sq", [128, 4, CH], BF16, st); sqb = Buf("sq")
            rstd = sb("rstd", [128, CH], F32, st); rstdb = Buf("rstd")
            stmp = sb("stmp", [128, 512], F32, st); stmpb = Buf("stmp")
            stio = sb("stio", [128, 16, 128], F32, st); stiob = Buf("stio")
            cst = [stio[0:3, i * 4:(i + 1) * 4, :].rearrange("p a b -> p (a b)") for i in range(2)]; cstb = [stiob, stiob]

            phase(3.01)
            si = next_w()
            phase(3.02)

            def ev_dt(bi, co, t):
                r0, nr, s = tiles[t]
                P.op("dve", lambda: nc.vector.tensor_tensor(out=dt_tm[0:nr, t, :], in0=PB[bi][0:nr, co:co + 32],
                                                            in1=dtb_bc[0:nr, :], op=ALU.add), reads=[PBb[bi], parb], writes=[dtb])
            tm_block(si, BW, ev_dt)
            si = next_w()

            def ev_kr(bi, co, t):
                r0, nr, s = tiles[t]
                P.op("act", lambda: nc.scalar.activation(out=kr_keep[0:nr, t, :], in_=PB[bi][0:nr, co:co + 64], func=AF.Identity),
                     reads=[PBb[bi]], writes=[krb])
            tm_block(si, BW, ev_kr)
            phase(3.05)
            nra = tiles[0][1]
            P.op("dve", lambda: nc.vector.tensor_scalar_mul(out=tmp5[0:nra], in0=dt_tm[0:nra], scalar1=-1.0), reads=[dtb], writes=[dtb])
            P.op("dve", lambda: nc.vector.tensor_tensor(out=tmp5[0:nra], in0=tmp5[0:nra], in1=dt_tm[0:nra], op=ALU.max), reads=[dtb], writes=[dtb])
            P.op("act", lambda: nc.scalar.activation(out=tmp5[0:nra], in_=tmp5[0:nra], func=AF.Exp, scale=-1.0),
                 reads=[dtb], writes=[dtb])
            P.op("dve", lambda: nc.vector.tensor_scalar_add(out=tmp5[0:nra], in0=tmp5[0:nra], scalar1=1.0), reads=[dtb], writes=[dtb])
            P.op("act", lambda: nc.scalar.activation(out=tmp5[0:nra], in_=tmp5[0:nra], func=AF.Ln),
                 reads=[dtb], writes=[dtb])
            P.op("dve", lambda: nc.vector.tensor_scalar_max(out=dt_tm[0:nra], in0=dt_tm[0:nra], scalar1=0.0),
                 reads=[dtb], writes=[dtb])
            P.op("dve", lambda: nc.vector.tensor_tensor(out=dt_tm[0:nra], in0=dt_tm[0:nra], in1=tmp5[0:nra], op=ALU.add),
                 reads=[dtb], writes=[dtb])
            P.op("dve", lambda: nc.vector.tensor_tensor(
                out=a_tm[0:nra], in0=dt_tm[0:nra], in1=A_bc[0:nra, :].unsqueeze(1).to_broadcast([nra, NT, NH]),
                op=ALU.mult), reads=[dtb, parb], writes=[dtb])

            phase(3.1)
            for sgi, (s, sc0, sn, stl) in enumerate(segs):
                base = sgi * SEGW
                if is_p:
                    if g0 == 0:
                        P.op("dve", lambda base=base: nc.vector.memset(xbcT[:, :, base:base + 4], 0.0), writes=[xbcb])
                    else:
                        P.op("dve", lambda base=base: nc.vector.tensor_copy(out=xbcT[:, :, base:base + 4], in_=carry[:]),
                             reads=[carryb], writes=[xbcb])
                else:
                    P.op("dve", lambda sgi=sgi: nc.vector.memset(c32[:, :, sgi * 4:sgi * 4 + 1], 0.0), writes=[c32b])
                    for j in range(3):
                        P.dma("sp", c32[:, :, sgi * 4 + 1 + j], st_conv[l, s - 1, j].rearrange("(t p) -> p t", p=128),
                              writes=[c32b], cbuf=c32b, nc_ok=True)
                    P.op("dve", lambda base=base, sgi=sgi: nc.vector.tensor_copy(out=xbcT[:, :, base:base + 4],
                                                                                in_=c32[:, :, sgi * 4:sgi * 4 + 4]),
                         reads=[c32b], writes=[xbcb])

            for i in range(12):
                si = next_w()

                def ev_xbc(bi, sub, i=i):
                    pt_ = i * 2 + sub
                    for sgi, (s, sc0, sn, stl) in enumerate(segs):
                        base = sgi * SEGW + 4
                        if pt_ % 2 == 0 and not _HOOK.get("dbg_noact"):
                            P.op("act", lambda: nc.scalar.activation(out=xbcT[:, pt_, base:base + sn], in_=PB[bi][:, sc0:sc0 + sn], func=AF.Identity),
                                 reads=[PBb[bi]], writes=[xbcb])
                        else:
                            P.op("dve", lambda: nc.vector.tensor_copy(out=xbcT[:, pt_, base:base + sn], in_=PB[bi][:, sc0:sc0 + sn]),
                                 reads=[PBb[bi]], writes=[xbcb])
                        if last_grp and not _HOOK.get("dbg_noact"):
                            P.op("dve", lambda: nc.vector.tensor_copy(out=c32[:, pt_, sgi * 4 + 1:sgi * 4 + 4],
                                                                      in_=PB[bi][:, sc0 + sn - 3:sc0 + sn]),
                                 reads=[PBb[bi]], writes=[c32b])
                fm_block(si, BW, ev_xbc)
            phase(3.12)
            for sgi, (s, sc0, sn, stl) in enumerate(segs):
                base = sgi * SEGW
                if is_p:
                    P.op("dve", lambda base=base, sn=sn: nc.vector.tensor_copy(out=carry[:], in_=xbcT[:, :, base + sn:base + sn + 4]),
                         reads=[xbcb], writes=[carryb])
                phase(3.15)
                if last_grp:
                    dst = conv_p[l] if is_p else conv_s[l, s - 1]
                    for c6 in range(6):
                        bi = next_pb()

                        def trc(bi=bi, c6=c6, sgi=sgi):
                            for j in range(4):
                                ins = nc.tensor.transpose(PB[bi][0:3, j * 128:(j + 1) * 128], c32[:, c6 * 4 + j, sgi * 4 + 1:sgi * 4 + 4], identf[:])
                            return ins
                        P.op("pe", trc, reads=[c32b, constb], writes=[PBb[bi]])
                        ci_ = c6 % 2
                        P.op("dve", lambda bi=bi, ci_=ci_: nc.vector.tensor_copy(out=cst[ci_], in_=PB[bi][0:3, :]), reads=[PBb[bi]], writes=[cstb[ci_]])
                        P.dma("pool", dst[:, c6 * 512:(c6 + 1) * 512], cst[ci_], reads=[cstb[ci_]], writes=[outb], cbuf=cstb[ci_])

            phase(3.2)
            for i in range(8):
                si = next_w()

                def ev_z(bi, sub, i=i):
                    P.op("act", lambda: nc.scalar.activation(out=sz[:, i * 2 + sub, 0:G], in_=PB[bi][:, 0:G], func=AF.Silu),
                         reads=[PBb[bi]], writes=[szb])
                fm_block(si, BW, ev_z)

            phase(3.3)
            for sgi, (s, sc0, sn, stl) in enumerate(segs):
                base = sgi * SEGW
                if not is_p:
                    P.dma("sp", stio[:], st_ssm[l, s - 1].rearrange("(t p) n -> p t n", p=128), writes=[stiob], cbuf=stiob)
                    for t4 in range(4):
                        bi = next_pb()

                        def tr(t4=t4, bi=bi):
                            for j in range(4):
                                ins = nc.tensor.transpose(PB[bi][:, j * 128:(j + 1) * 128], stio[:, t4 * 4 + j, :], identf[:])
                            return ins
                        P.op("pe", tr, reads=[stiob, constb], writes=[PBb[bi]])
                        P.op("dve", lambda t4=t4, bi=bi: nc.vector.tensor_copy(out=S32[:, t4 * 512:(t4 + 1) * 512], in_=PB[bi][:]),
                             reads=[PBb[bi]], writes=[Sb[t4]])
                        P.op("act", lambda t4=t4, bi=bi: nc.scalar.activation(out=Sbf[:, t4 * 512:(t4 + 1) * 512], in_=PB[bi][:], func=AF.Identity),
                             reads=[PBb[bi]], writes=[Sb[t4]])
                elif g0 == 0:
                    for t4 in range(4):
                        P.op("pool", lambda t4=t4: nc.gpsimd.memset(S32[:, t4 * 512:(t4 + 1) * 512], 0.0), writes=[Sb[t4]])
                        P.op("pool", lambda t4=t4: nc.gpsimd.memset(Sbf[:, t4 * 512:(t4 + 1) * 512], 0.0), writes=[Sb[t4]])

                for c, tt in enumerate(stl):
                    L = tiles[tt][1]
                    col0 = tt * 128
                    xc0 = base + 1 + c * CH
                    for pt_ in range(24):
                        ci = pt_ % 2
                        if True:
                            P.op("dve", lambda pt_=pt_, ci=ci: nc.vector.tensor_scalar(
                                out=ctmp[:, ci, 0:L], in0=xbcT[:, pt_, xc0:xc0 + L], scalar1=convw[:, pt_, 0:1],
                                scalar2=convb[:, pt_:pt_ + 1], op0=ALU.mult, op1=ALU.add), reads=[xbcb, parb], writes=[ctb[ci]])
                            for j in range(1, 4):
                                P.op("dve", lambda pt_=pt_, ci=ci, j=j: nc.vector.scalar_tensor_tensor(
                                    out=ctmp[:, ci, 0:L], in0=xbcT[:, pt_, xc0 + j:xc0 + j + L], scalar=convw[:, pt_, j:j + 1],
                                    in1=ctmp[:, ci, 0:L], op0=ALU.mult, op1=ALU.add), reads=[xbcb, parb], writes=[ctb[ci]])
                        else:
                            P.op("pool", lambda pt_=pt_, ci=ci: nc.gpsimd.tensor_scalar(
                                out=ctmp[:, ci, 0:L], in0=xbcT[:, pt_, xc0:xc0 + L], scalar1=convw[:, pt_, 0:1],
                                scalar2=convb[:, pt_:pt_ + 1], op0=ALU.mult, op1=ALU.add), reads=[xbcb, parb], writes=[ctb[ci]])
                            for j in range(1, 4):
                                P.op("pool", lambda pt_=pt_, j=j: nc.gpsimd.tensor_scalar(
                                    out=ctmp2[:, 0:L], in0=xbcT[:, pt_, xc0 + j:xc0 + j + L], scalar1=convw[:, pt_, j:j + 1],
                                    scalar2=None, op0=ALU.mult), reads=[xbcb, parb], writes=[ct2b])
                                P.op("pool", lambda ci=ci: nc.gpsimd.tensor_tensor(
                                    out=ctmp[:, ci, 0:L], in0=ctmp[:, ci, 0:L], in1=ctmp2[:, 0:L], op=ALU.add),
                                    reads=[ct2b], writes=[ctb[ci]])
                        P.op("act", lambda pt_=pt_, ci=ci: nc.scalar.activation(out=xcT[:, pt_, 0:L], in_=ctmp[:, ci, 0:L], func=AF.Silu),
                             reads=[ctb[ci]], writes=[xcTb])

                    phase(3.4)
                    def ev_x(ti_, i0, n):
                        P.op("act", lambda: nc.scalar.activation(out=xtok[0:L, i0 * 128:(i0 + n) * 128], in_=PT[ti_][0:L, 0:n * 128], func=AF.Identity),
                             reads=[PTb[ti_]], writes=[xtokb])
                    transposes([xcT[:, j, 0:L] for j in range(16)], ev_x, [xcTb])

                    def ev_b(ti_, i0, n):
                        P.op("dve", lambda: nc.vector.tensor_copy(out=Btok[0:L, :, :].rearrange("p a b -> p (a b)"), in_=PT[ti_][0:L, 0:512]),
                             reads=[PTb[ti_]], writes=[Btokb])
                    transposes([xcT[:, 16 + j, 0:L] for j in range(4)], ev_b, [xcTb])

                    P.op("dve", lambda: nc.vector.tensor_tensor(
                        out=xd[0:L, :].rearrange("p (h e) -> p h e", e=HP), in0=xtok[0:L, :].rearrange("p (h e) -> p h e", e=HP),
                        in1=dt_tm[0:L, tt, :].unsqueeze(2).to_broadcast([L, NH, HP]), op=ALU.mult),
                        reads=[xtokb, dtb], writes=[xdb])
                    bi = next_pb()
                    P.op("pe", lambda bi=bi: nc.tensor.matmul(PB[bi][0:L, 0:NH], lhsT=tri[0:L, 0:L], rhs=a_tm[0:L, tt, :], start=True, stop=True),
                         reads=[dtb, constb], writes=[PBb[bi]])
                    P.op("dve", lambda bi=bi: nc.vector.tensor_copy(out=acs[0:L, :], in_=PB[bi][0:L, 0:NH]),
                         reads=[PBb[bi]], writes=[acsb])

                    phase(3.5)
                    for gi in range(4):
                        hs = slice(gi * 8, gi * 8 + 8)
                        P.op("pool", lambda hs=hs: nc.gpsimd.tensor_tensor(
                            out=rhsb[0:L, :, 0:L], in0=a_tm[0:L, tt, hs].unsqueeze(2).to_broadcast([L, 8, L]),
                            in1=tri[0:L, 0:L].unsqueeze(1).to_broadcast([L, 8, L]), op=ALU.mult),
                            reads=[dtb, constb], writes=[rhsbb])
                        b0 = 2 * (gi % 2); b1 = b0 + 1

                        def bcmm(b0=b0, b1=b1):
                            for half, bb in enumerate((b0, b1)):
                                ins = nc.tensor.matmul(PB[bb][:, 0:4 * L], lhsT=onesf[0:L, :],
                                                       rhs=rhsb[0:L, half * 4:half * 4 + 4, 0:L], start=True, stop=True)
                            return ins
                        P.op("pe", bcmm, reads=[rhsbb, constb], writes=[PBb[b0], PBb[b1]])
                        for half, bb in enumerate((b0, b1)):
                            h4 = slice(half * 4, half * 4 + 4)
                            bcv = PB[bb][:, 0:4 * L].rearrange("p (h l) -> p h l", l=L)
                            P.op("dve", lambda bcv=bcv, h4=h4: nc.vector.tensor_tensor(
                                out=sg1[0:L, h4, 0:L], in0=bcv[0:L], in1=negmask[0:L, 0:L].unsqueeze(1).to_broadcast([L, 4, L]),
                                op=ALU.add), reads=[PBb[bb], constb], writes=[sg1b])
                            P.op("act", lambda bcv=bcv, h4=h4: nc.scalar.activation(out=Et[:, h4, 0:L], in_=bcv, func=AF.Exp),
                                 reads=[PBb[bb]], writes=[Etb])
                            P.op("dve", lambda bcv=bcv, h4=h4, half=half, gi=gi: nc.vector.tensor_tensor(
                                out=te[0:L, h4], in0=bcv[0:L, :, L - 1], in1=acs[0:L, gi * 8 + half * 4:gi * 8 + half * 4 + 4],
                                op=ALU.subtract), reads=[PBb[bb], acsb], writes=[teb])
                        P.op("pool", lambda gi=gi: nc.gpsimd.tensor_tensor(
                            out=sg1[0:L, :, 0:L], in0=sg1[0:L, :, 0:L],
                            in1=acs[0:L, gi * 8:gi * 8 + 8].unsqueeze(2).to_broadcast([L, 8, L]), op=ALU.subtract),
                            reads=[sg1b, acsb], writes=[sg1b])
                        P.op("act", lambda: nc.scalar.activation(out=sg1[0:L, :, 0:L], in_=sg1[0:L, :, 0:L], func=AF.Exp),
                             reads=[sg1b], writes=[sg1b])
                        P.op("act", lambda: nc.scalar.activation(out=te[0:L, :], in_=te[0:L, :], func=AF.Exp),
                             reads=[teb], writes=[teb])
                        phase(3.6)
                        bc_ = next_pb(4, 6)
                        P.op("pe", lambda bc_=bc_, gi=gi: nc.tensor.matmul(PB[bc_][0:L, 0:L], lhsT=xcT[:, 16 + gi, 0:L],
                                                                         rhs=xcT[:, 20 + gi, 0:L], start=True, stop=True),
                             reads=[xcTb], writes=[PBb[bc_]])
                        P.op("dve", lambda bc_=bc_: nc.vector.tensor_tensor(
                            out=MT[0:L, :, 0:L], in0=sg1[0:L, :, 0:L], in1=PB[bc_][0:L, 0:L].unsqueeze(1).to_broadcast([L, 8, L]),
                            op=ALU.mult), reads=[sg1b, PBb[bc_]], writes=[MTb])
                        P.op("pool", lambda gi=gi: nc.gpsimd.tensor_tensor(
                            out=Cp[:, :, 0:L], in0=Et[:, :, 0:L], in1=xcT[:, 20 + gi, 0:L].unsqueeze(1).to_broadcast([128, 8, L]),
                            op=ALU.mult), reads=[Etb, xcTb], writes=[Cpb])
                        phase(3.7)
                        by = next_pb(4, 6)

                        def ymm(by=by, gi=gi):
                            for jj in range(4):
                                for hh in range(2):
                                    hl = jj * 2 + hh
                                    h = gi * 8 + hl
                                    o = PB[by][hh * 64:(hh + 1) * 64, jj * 128:jj * 128 + L]
                                    nc.tensor.matmul(o, lhsT=xd[0:L, h * HP:(h + 1) * HP], rhs=MT[0:L, hl, 0:L], start=True, stop=False)
                                    ins = nc.tensor.matmul(o, lhsT=Sbf[:, h * HP:(h + 1) * HP], rhs=Cp[:, hl, 0:L], start=False, stop=True)
                            return ins
                        P.op("pe", ymm, reads=[xdb, MTb, Sb[gi], Cpb], writes=[PBb[by]])
                        for jj in range(4):
                            j = gi * 4 + jj
                            P.op("dve", lambda jj=jj, j=j, by=by: nc.vector.scalar_tensor_tensor(
                                out=ygf[:, jj, 0:L], in0=xcT[:, j, 0:L], scalar=Dcol[:, j:j + 1], in1=PB[by][:, jj * 128:jj * 128 + L],
                                op0=ALU.mult, op1=ALU.add), reads=[xcTb, parb, PBb[by]], writes=[ygb])
                            P.op("pool", lambda jj=jj, j=j: nc.gpsimd.tensor_tensor(
                                out=ygf[:, jj, 0:L], in0=ygf[:, jj, 0:L], in1=sz[:, j, col0:col0 + L], op=ALU.mult),
                                reads=[ygb, szb], writes=[ygb])
                        P.op("act", lambda: nc.scalar.activation(out=sq[:, :, 0:L], in_=ygf[:, :, 0:L], func=AF.Square),
                             reads=[ygb], writes=[sqb])
                        phase(3.8)
                        bm = next_pb(4, 6)

                        def msmm(bm=bm):
                            for jj in range(4):
                                ins = nc.tensor.matmul(PB[bm][:, 0:L], lhsT=ones512[:], rhs=sq[:, jj, 0:L], start=(jj == 0), stop=(jj == 3))
                            return ins
                        P.op("pe", msmm, reads=[sqb, constb], writes=[PBb[bm]])
                        rsqrt(rstd[:, 0:L], PB[bm][:, 0:L], 1.0, 1e-6, [PBb[bm]], [rstdb])
                        for jj in range(4):
                            j = gi * 4 + jj
                            P.op("dve", lambda jj=jj, j=j: nc.vector.scalar_tensor_tensor(
                                out=mixT[:, j, col0:col0 + L], in0=ygf[:, jj, 0:L], scalar=normw[:, j:j + 1], in1=rstd[:, 0:L],
                                op0=ALU.mult, op1=ALU.mult), reads=[ygb, rstdb, parb], writes=[mixTb[0]])
                        phase(3.85)
                        gc = slice(gi * 512, (gi + 1) * 512)
                        P.op("pool", lambda gc=gc: nc.gpsimd.tensor_tensor(
                            out=xdw[0:L, gc].rearrange("p (h e) -> p h e", e=HP), in0=xd[0:L, gc].rearrange("p (h e) -> p h e", e=HP),
                            in1=te[0:L, :].unsqueeze(2).to_broadcast([L, 8, HP]), op=ALU.mult), reads=[xdb, teb], writes=[xdwb])
                        bs = next_pb(4, 6)
                        P.op("pe", lambda bs=bs, gi=gi, gc=gc: nc.tensor.matmul(PB[bs][:, :], lhsT=Btok[0:L, gi, :], rhs=xdw[0:L, gc],
                                                                              start=True, stop=True),
                             reads=[Btokb, xdwb], writes=[PBb[bs]])
                        P.op("dve", lambda gc=gc: nc.vector.tensor_tensor(
                            out=stmp[:, :].rearrange("p (h e) -> p h e", e=HP), in0=S32[:, gc].rearrange("p (h e) -> p h e", e=HP),
                            in1=Et[:, :, L - 1].unsqueeze(2).to_broadcast([128, 8, HP]), op=ALU.mult),
                            reads=[Sb[gi], Etb], writes=[stmpb])
                        P.op("dve", lambda gc=gc, bs=bs: nc.vector.tensor_tensor(out=S32[:, gc], in0=stmp[:, :], in1=PB[bs][:, :], op=ALU.add),
                             reads=[stmpb, PBb[bs]], writes=[Sb[gi]])
                        P.op("act", lambda gc=gc: nc.scalar.activation(out=Sbf[:, gc], in_=S32[:, gc], func=AF.Identity), reads=[Sb[gi]], writes=[Sb[gi]])

                phase(3.9)
                if last_grp:
                    for t4 in range(4):
                        bi = next_pb()

                        def tr(t4=t4, bi=bi):
                            for j in range(4):
                                ins = nc.tensor.transpose(PB[bi][:, j * 128:(j + 1) * 128], S32[:, (t4 * 4 + j) * 128:(t4 * 4 + j + 1) * 128], identf[:])
                            return ins
                        P.op("pe", tr, reads=[Sb[t4], constb], writes=[PBb[bi]])
                        P.op("dve", lambda t4=t4, bi=bi: nc.vector.tensor_copy(
                            out=stio[:, t4 * 4:(t4 + 1) * 4, :].rearrange("p a b -> p (a b)"), in_=PB[bi][:]),
                            reads=[PBb[bi]], writes=[stiob])
                    dst = ssm_p[l] if is_p else ssm_s[l, s - 1]
                    P.dma("pool", dst.rearrange("(t p) n -> p t n", p=128), stio[:], reads=[stiob], writes=[outb], cbuf=stiob)

        if not is_p:
            with P.scope() as st:
                kt32 = sb("kt32", [128, 1024], F32, st); kt32b = Buf("kt32")
                ktbf = sb("ktbf", [128, 1024], BF16, st); ktbfb = Buf("ktbf")
                kTt = sb("kTt", [128, 8, 128], BF16, st); kTtb = Buf("kTt")
                v32 = sb("v32", [128, 1024], F32, st); v32b = Buf("v32")
                vbf = sb("vbf", [128, 1024], BF16, st); vbfb = Buf("vbf")
                l32 = sb("l32", [128, 320], F32, st); l32b = Buf("l32")
                lbf = sb("lbf", [128, 320], BF16, st); lbfb = Buf("lbf")
                lTt = sb("lTt", [128, 3, 128], BF16, st); lTtb = Buf("lTt")
                for (s, sc0, sn, stl) in segs:
                    sl = s - 1
                    for j in range(PAST // 128):
                        rs = slice(j * 128, (j + 1) * 128)
                        P.dma("sp", kt32[:], ck[l, sl, rs, :], writes=[kt32b], cbuf=kt32b)
                        P.op("dve", lambda: nc.vector.tensor_copy(out=ktbf[:], in_=kt32[:]), reads=[kt32b], writes=[ktbfb])

                        def ev_k(ti_, i0, n):
                            P.op("act", lambda: nc.scalar.activation(out=kTt[:, i0:i0 + n, :].rearrange("p a b -> p (a b)"), in_=PT[ti_][:, 0:n * 128], func=AF.Identity),
                                 reads=[PTb[ti_]], writes=[kTtb])
                        transposes([ktbf[:, h * 128:(h + 1) * 128] for h in range(8)], ev_k, [ktbfb])
                        P.dma("pool", KT_scr[s][:, :, rs].rearrange("h p t -> p h t"), kTt[:], reads=[kTtb], writes=[scrB[s]["KT"]], cbuf=kTtb)
                        P.dma("sp", v32[:], cv[l, sl, rs, :], writes=[v32b], cbuf=v32b)
                        P.op("pool", lambda: nc.gpsimd.tensor_copy(out=vbf[:], in_=v32[:]), reads=[v32b], writes=[vbfb])
                        P.dma("pool", V_scr[s][rs, :], vbf[:], reads=[vbfb], writes=[scrB[s]["V"]], cbuf=vbfb)
                        P.dma("sp", l32[:, 0:256], cl[l, sl, rs, :], writes=[l32b], cbuf=l32b)
                        P.dma("sp", l32[:, 256:320], cr[l, sl, rs, :], writes=[l32b], cbuf=l32b)
                        P.op("dve", lambda: nc.vector.tensor_copy(out=lbf[:], in_=l32[:]), reads=[l32b], writes=[lbfb])

                        def ev_l(ti_, i0, n):
                            P.op("act", lambda: nc.scalar.activation(out=lTt[:, :, :].rearrange("p a b -> p (a b)"), in_=PT[ti_][:, 0:384], func=AF.Identity),
                                 reads=[PTb[ti_]], writes=[lTtb])
                        transposes([lbf[:, 0:128], lbf[:, 128:256], lbf[:, 256:320]], ev_l, [lbfb])
                        P.dma("pool", LT_scr[s][:, :, rs].rearrange("a p t -> p a t"), lTt[:, 0:2, :], reads=[lTtb], writes=[scrB[s]["LT"]], cbuf=lTtb)
                        P.dma("pool", RT_scr[s][:, rs], lTt[0:64, 2, :], reads=[lTtb], writes=[scrB[s]["RT"]], cbuf=lTtb)
                        P.dma("pool", L_scr[s][rs, :], lbf[:, 0:256], reads=[lbfb], writes=[scrB[s]["L"]], cbuf=lbfb)

        def seg_qtiles(s, sc0, sn, stl):
            kb = kbase_of(s)
            return [(tt * 128, tiles[tt][1], (kb + c * 128) // 128) for c, tt in enumerate(stl)]

        phase(4)
        with P.scope() as st:
            QT = sb("QT", [128, 8, G], BF16, st); QTb = Buf("QT")
            gT = sb("gT", [128, 8, G], BF16, st); gTb = Buf("gT")
            with P.scope() as st1:
                kv32 = sb("kv32", [128, NT, 1024], F32, st1); kv32b = Buf("kv32")
                qkb = sb("qkb", [128, NT, 1024], BF16, st1); qkbb = Buf("qkb")
                rtab = sb("rtab", [128, NT, 16], F32, st1); rtabb = Buf("rtab")
                rtA = sb("rtA", [128, 16, 8], F32, st1); rtB = sb("rtB", [128, 16, 8], F32, st1)
                kTt = sb("kTt2", [128, 8, 128], BF16, st1); kTtb = Buf("kTt2")
                for t, (r0, nr, s) in enumerate(tiles):
                    P.dma("sp", rtab[0:nr, t, :], rope_d[r0:r0 + nr, :], writes=[rtabb], cbuf=rtabb)

                def ev_kv(i):
                    def ev(bi, co, t):
                        r0, nr, s = tiles[t]
                        P.op("act", lambda: nc.scalar.activation(out=kv32[0:nr, t, i * BW:(i + 1) * BW], in_=PB[bi][0:nr, co:co + BW], func=AF.Identity),
                             reads=[PBb[bi]], writes=[kv32b])
                    return ev
                for i in range(4):
                    si = next_w()
                    tm_block(si, BW, ev_kv(i))
                for t, (r0, nr, s) in enumerate(tiles):
                    rope_tm("dve", kv32[0:nr, t, :].rearrange("p (h e) -> p h e", e=64), rtab[0:nr, t, :], 8, nr, 16, rtA, rtB,
                            [rtabb, kv32b], [kv32b])
                    P.op("pool", lambda t=t, nr=nr: nc.gpsimd.tensor_copy(out=qkb[0:nr, t, :], in_=kv32[0:nr, t, :]), reads=[kv32b], writes=[qkbb])

                    def ev_qt(ti_, i0, n, t=t, nr=nr):
                        P.op("act", lambda: nc.scalar.activation(out=QT[:, i0:i0 + n, t * 128:t * 128 + nr],
                                                           in_=PT[ti_][:, 0:n * 128].rearrange("p (a b) -> p a b", b=128)[:, :, 0:nr], func=AF.Identity),
                             reads=[PTb[ti_]], writes=[QTb])
                    transposes([qkb[0:nr, t, h * 128:(h + 1) * 128] for h in range(8)], ev_qt, [qkbb])
                for i in range(4):
                    si = next_w()
                    tm_block(si, BW, ev_kv(i))
                dk_out = dk_p if is_p else dk_s
                for t, (r0, nr, s) in enumerate(tiles):
                    rope_tm("dve", kv32[0:nr, t, :].rearrange("p (h e) -> p h e", e=64), rtab[0:nr, t, :], 8, nr, 16, rtA, rtB,
                            [rtabb, kv32b], [kv32b])
                    P.dma("pool", dk_out[l, r0:r0 + nr, :], kv32[0:nr, t, :], reads=[kv32b], writes=[outb], cbuf=kv32b)
                    P.op("pool", lambda t=t, nr=nr: nc.gpsimd.tensor_copy(out=qkb[0:nr, t, :], in_=kv32[0:nr, t, :]), reads=[kv32b], writes=[qkbb])
                    kpos = kbase_of(s) + (r0 - g0 if is_p else 0)

                    def ev_kt(ti_, i0, n, nr=nr):
                        P.op("act", lambda: nc.scalar.activation(out=kTt[:, i0:i0 + n, 0:nr],
                                                           in_=PT[ti_][:, 0:n * 128].rearrange("p (a b) -> p a b", b=128)[:, :, 0:nr], func=AF.Identity),
                             reads=[PTb[ti_]], writes=[kTtb])
                    transposes([qkb[0:nr, t, h * 128:(h + 1) * 128] for h in range(8)], ev_kt, [qkbb])
                    P.dma("pool", KT_scr[s][:, :, kpos:kpos + nr].rearrange("h p t -> p h t"), kTt[:, :, 0:nr],
                          reads=[kTtb], writes=[scrB[s]["KT"]], cbuf=kTtb)
                for i in range(4):
                    si = next_w()
                    tm_block(si, BW, ev_kv(i))
                dv_out = dv_p if is_p else dv_s
                for t, (r0, nr, s) in enumerate(tiles):
                    P.dma("pool", dv_out[l, r0:r0 + nr, :], kv32[0:nr, t, :], reads=[kv32b], writes=[outb], cbuf=kv32b)
                    P.op("pool", lambda t=t, nr=nr: nc.gpsimd.tensor_copy(out=qkb[0:nr, t, :], in_=kv32[0:nr, t, :]), reads=[kv32b], writes=[qkbb])
                    kpos = kbase_of(s) + (r0 - g0 if is_p else 0)
                    P.dma("pool", V_scr[s][kpos:kpos + nr, :], qkb[0:nr, t, :], reads=[qkbb], writes=[scrB[s]["V"]], cbuf=qkbb)
                for i in range(4):
                    si = next_w()

                    def ev_g(bi, sub, i=i):
                        P.op("act", lambda: nc.scalar.activation(out=gT[:, i * 2 + sub, 0:G], in_=PB[bi][:, 0:G], func=AF.Silu),
                             reads=[PBb[bi]], writes=[gTb])
                    fm_block(si, BW, ev_g)

            phase(5)
            with P.scope() as st1:
                KTs = [sb(f"KTs{i}", [128, NKT_MAX * 128], BF16, st1) for i in range(2)]
                KTsb = [Buf(f"KTs{i}") for i in range(2)]
                Vs = [sb(f"Vs{i}", [128, NKT_MAX, 130], BF16, st1) for i in range(2)]
                Vsb = [Buf(f"Vs{i}") for i in range(2)]
                PTt = [sb(f"PTt{i}", [128, 512], BF16, st1) for i in range(3)]
                PTtb = [Buf(f"PTt{i}") for i in range(3)]
                osb = sb("osb", [128, 128], F32, st1); osbb = Buf("osb")
                otmp = sb("otmp", [128, 128], F32, st1)
                onb = sb("onb", [128, 128], BF16, st1); onbb = Buf("onb")
                rr = sb("rr", [128, 4], F32, st1); rrb = Buf("rr")
                junk = sb("junk", [128, 128], BF16, st1)
                for i in range(2):
                    P.op("dve", lambda i=i: nc.vector.memset(Vs[i][:, :, 128:129], 1.0), writes=[Vsb[i]])
                ptt_rr = 0
                hcount = 0
                for (s, sc0, sn, stl) in segs:
                    kbase = kbase_of(s)
                    nkeys = kbase + sn
                    nkt = (nkeys + 127) // 128
                    qts = seg_qtiles(s, sc0, sn, stl)
                    for h in range(8):
                        slot = hcount % 2
                        hcount += 1
                        P.dma("sp", KTs[slot][:, 0:nkeys], KT_scr[s][h, :, 0:nkeys], reads=[scrB[s]["KT"]], writes=[KTsb[slot]], cbuf=KTsb[slot])
                        nfull = nkeys // 128
                        if nfull:
                            P.dma("sp", Vs[slot][:, 0:nfull, 0:128],
                                  V_scr[s][0:nfull * 128, h * 128:(h + 1) * 128].rearrange("(j p) e -> p j e", p=128),
                                  reads=[scrB[s]["V"]], writes=[Vsb[slot]], cbuf=Vsb[slot])
                        if nkeys % 128:
                            rem = nkeys % 128
                            P.dma("sp", Vs[slot][0:rem, nfull, 0:128], V_scr[s][nfull * 128:nkeys, h * 128:(h + 1) * 128],
                                  reads=[scrB[s]["V"]], writes=[Vsb[slot]], cbuf=Vsb[slot])
                        for m in range(2):
                            ob = (2 + 2 * m, 3 + 2 * m)
                            for j in range(nkt):
                                nk = min(128, nkeys - j * 128)
                                vis = [qi for qi, (qc, nq, qg) in enumerate(qts) if (not is_p) or qg >= j]
                                if not vis:
                                    continue
                                qlo = qts[vis[0]][0]; qhi = qts[vis[-1]][0] + qts[vis[-1]][1]
                                sbk = next_pb(0, 2)
                                P.op("pe", lambda sbk=sbk, j=j, nk=nk, qlo=qlo, qhi=qhi, m=m, h=h, slot=slot: nc.tensor.matmul(
                                    PB[sbk][0:nk, 0:qhi - qlo], lhsT=KTs[slot][m * 64:(m + 1) * 64, j * 128:j * 128 + nk],
                                    rhs=QT[m * 64:(m + 1) * 64, h, qlo:qhi], start=True, stop=True),
                                    reads=[KTsb[slot], QTb], writes=[PBb[sbk]])
                                pi = ptt_rr % 3
                                ptt_rr += 1
                                P.op("act", lambda sbk=sbk, pi=pi, nk=nk, qlo=qlo, qhi=qhi: nc.scalar.activation(
                                    out=PTt[pi][0:nk, 0:qhi - qlo], in_=PB[sbk][0:nk, 0:qhi - qlo], func=AF.Exp, scale=DIFF_SCALE),
                                    reads=[PBb[sbk]], writes=[PTtb[pi]])
                                if is_p and qts[vis[0]][2] == j:
                                    P.op("pool", lambda pi=pi: nc.gpsimd.memset(PTt[pi][64:128, 0:64], 0.0), writes=[PTtb[pi]])

                                def avmm(pi=pi, j=j, nk=nk, vis=vis, qlo=qlo, ob=ob, slot=slot):
                                    for qi in vis:
                                        qc, nq, qg = qts[qi]
                                        o = PB[ob[qi // 2]][0:nq, (qi % 2) * 129:(qi % 2) * 129 + 129]
                                        last_j = (qg if is_p else nkt - 1)
                                        ins = nc.tensor.matmul(o, lhsT=PTt[pi][0:nk, qc - qlo:qc - qlo + nq], rhs=Vs[slot][0:nk, j, 0:129],
                                                               start=(j == 0 and qi % 2 == 0), stop=(j == last_j), skip_group_check=True)
                                    return ins
                                P.op("pe", avmm, reads=[PTtb[pi], Vsb[slot]], writes=[PBb[ob[0]], PBb[ob[1]]])
                        for qi, (qc, nq, qg) in enumerate(qts):
                            o0 = PB[2 + qi // 2][0:nq, (qi % 2) * 129:(qi % 2) * 129 + 129]
                            o1 = PB[4 + qi // 2][0:nq, (qi % 2) * 129:(qi % 2) * 129 + 129]
                            bufs = [PBb[2 + qi // 2], PBb[4 + qi // 2]]
                            P.op("dve", lambda o0=o0, nq=nq: nc.vector.reciprocal(out=rr[0:nq, 0:1], in_=o0[:, 128:129]), reads=bufs, writes=[rrb])
                            P.op("dve", lambda o1=o1, nq=nq: nc.vector.reciprocal(out=rr[0:nq, 1:2], in_=o1[:, 128:129]), reads=bufs, writes=[rrb])
                            P.op("dve", lambda nq=nq: nc.vector.tensor_tensor(out=rr[0:nq, 1:2], in0=rr[0:nq, 1:2], in1=neglam[0:nq, :], op=ALU.mult),
                                 reads=[rrb, parb], writes=[rrb])
                            P.op("dve", lambda o1=o1, nq=nq: nc.vector.tensor_scalar_mul(out=otmp[0:nq, :], in0=o1[:, 0:128], scalar1=rr[0:nq, 1:2]),
                                 reads=bufs + [rrb], writes=[osbb])
                            P.op("dve", lambda o0=o0, nq=nq: nc.vector.scalar_tensor_tensor(
                                out=osb[0:nq, :], in0=o0[:, 0:128], scalar=rr[0:nq, 0:1], in1=otmp[0:nq, :], op0=ALU.mult, op1=ALU.add),
                                reads=bufs + [rrb, osbb], writes=[osbb])
                            P.op("act", lambda nq=nq: nc.scalar.activation(out=junk[0:nq, :], in_=osb[0:nq, :], func=AF.Square,
                                                                          accum_out=rr[0:nq, 2:3]), reads=[osbb], writes=[rrb])
                            rsqrt(rr[0:nq, 3:4], rr[0:nq, 2:3], 1.0 / 128.0, 1e-6, [rrb], [rrb])
                            P.op("dve", lambda nq=nq: nc.vector.scalar_tensor_tensor(
                                out=onb[0:nq, :], in0=osb[0:nq, :], scalar=rr[0:nq, 3:4], in1=dnw_bc[0:nq, :], op0=ALU.mult, op1=ALU.mult),
                                reads=[osbb, rrb, parb], writes=[onbb])

                            def ev_o(ti_, i0, n, qc=qc, nq=nq, h=h):
                                P.op("dve", lambda: nc.vector.tensor_tensor(out=mixT[:, 16 + h, qc:qc + nq], in0=PT[ti_][:, 0:nq],
                                                                            in1=gT[:, h, qc:qc + nq], op=ALU.mult),
                                     reads=[PTb[ti_], gTb], writes=[mixTb[1]])
                            transposes([onb[0:nq, :]], ev_o, [onbb])

        phase(6)
        with P.scope() as st:
            qlT = sb("qlT", [128, 8, 2, G], BF16, st); qlTb = Buf("qlT")
            qrT = sb("qrT", [64, 8, G], BF16, st); qrTb = Buf("qrT")
            mgT = sb("mgT", [128, 8, G], BF16, st); mgTb = Buf("mgT")
            with P.scope() as st1:
                wuq = sb("wuq", [128, 6, 1536], BF16, st1); wukT = sb("wukT", [128, 8, 256], BF16, st1)
                wb = Buf("wuq")
                P.dma("sp", wuq[:], w_uq_bf[l], reads=[wcast[l]], writes=[wb], cbuf=wb)
                P.dma("sp", wukT[:], w_ukT_bf[l], reads=[wcast[l]], writes=[wb], cbuf=wb)
                cqb = sb("cqb", [128, NT, 768], BF16, st1); cqbb = Buf("cqb")
                cqT = sb("cqT", [128, 6, G], BF16, st1); cqTb = Buf("cqT")
                if not is_p:
                    P.op("pool", lambda: nc.gpsimd.memset(cqT[:], 0.0), writes=[cqTb])
                l32 = sb("l32m", [128, NT, 256], F32, st1); l32b = Buf("l32m")
                lkb = sb("lkb", [128, NT, 320], BF16, st1); lkbb = Buf("lkb")
                lTt = sb("lTt2", [128, 3, 128], BF16, st1); lTtb = Buf("lTt2")
                rtm = sb("rtm", [128, NT, 64], F32, st1); rtmb = Buf("rtm")
                rA = sb("rA", [128, 8, 32], F32, st1); rB = sb("rB", [128, 8, 32], F32, st1)
                ssq = sb("ssq", [128, NT, 8], F32, st1); ssqb = Buf("ssq")
                junk2 = sb("junk2", [128, 256], BF16, st1)
                qr32 = sb("qr32", [128, 512], F32, st1); qr32b = Buf("qr32")
                qrb = sb("qrb", [128, 512], BF16, st1); qrbb = Buf("qrb")
                qnT = sb("qnT", [128, G], BF16, st1); qnTb = Buf("qnT")
                for t, (r0, nr, s) in enumerate(tiles):
                    P.dma("sp", rtm[0:nr, t, :], rope_m[r0:r0 + nr, :], writes=[rtmb], cbuf=rtmb)
                for i in range(3):
                    si = next_w()

                    def ev_cq(bi, co, t, i=i):
                        r0, nr, s = tiles[t]
                        P.op("act", lambda: nc.scalar.activation(out=junk2[0:nr, :], in_=PB[bi][0:nr, co:co + BW], func=AF.Square,
                                                                 accum_out=ssq[0:nr, t, i:i + 1]), reads=[PBb[bi]], writes=[ssqb])
                        P.op("dve", lambda: nc.vector.tensor_copy(out=cqb[0:nr, t, i * BW:(i + 1) * BW], in_=PB[bi][0:nr, co:co + BW]),
                             reads=[PBb[bi]], writes=[cqbb])
                    tm_block(si, BW, ev_cq)
                si = next_w()

                def ev_ckv(bi, co, t):
                    r0, nr, s = tiles[t]
                    P.op("act", lambda: nc.scalar.activation(out=l32[0:nr, t, :], in_=PB[bi][0:nr, co:co + BW], func=AF.Identity), reads=[PBb[bi]], writes=[l32b])
                tm_block(si, BW, ev_ckv)
                lat_out = lat_p if is_p else lat_s
                kr_out = kr_p if is_p else kr_s
                for t, (r0, nr, s) in enumerate(tiles):
                    P.op("dve", lambda t=t, nr=nr: nc.vector.reduce_sum(out=ssq[0:nr, t, 3:4], in_=ssq[0:nr, t, 0:3], axis=AX.X),
                         reads=[ssqb], writes=[ssqb])
                    rsqrt(ssq[0:nr, t, 4:5], ssq[0:nr, t, 3:4], 1.0 / 768.0, 1e-6, [ssqb], [ssqb])
                    P.op("dve", lambda t=t, nr=nr: nc.vector.scalar_tensor_tensor(
                        out=cqb[0:nr, t, :], in0=cqb[0:nr, t, :], scalar=ssq[0:nr, t, 4:5], in1=qnw_bc[0:nr, :], op0=ALU.mult, op1=ALU.mult),
                        reads=[cqbb, ssqb, parb], writes=[cqbb])

                    def ev_cqT(ti_, i0, n, t=t, nr=nr):
                        P.op("act", lambda: nc.scalar.activation(out=cqT[:, i0:i0 + n, t * 128:t * 128 + nr],
                                                           in_=PT[ti_][:, 0:n * 128].rearrange("p (a b) -> p a b", b=128)[:, :, 0:nr], func=AF.Identity),
                             reads=[PTb[ti_]], writes=[cqTb])
                    transposes([cqb[0:nr, t, k * 128:(k + 1) * 128] for k in range(6)], ev_cqT, [cqbb])
                    P.op("act", lambda t=t, nr=nr: nc.scalar.activation(out=junk2[0:nr, 0:256], in_=l32[0:nr, t, :], func=AF.Square,
                                                                       accum_out=ssq[0:nr, t, 5:6]), reads=[l32b], writes=[ssqb])
                    rsqrt(ssq[0:nr, t, 6:7], ssq[0:nr, t, 5:6], 1.0 / 256.0, 1e-6, [ssqb], [ssqb])
                    P.op("dve", lambda t=t, nr=nr: nc.vector.scalar_tensor_tensor(
                        out=l32[0:nr, t, :], in0=l32[0:nr, t, :], scalar=ssq[0:nr, t, 6:7], in1=kvnw_bc[0:nr, :], op0=ALU.mult, op1=ALU.mult),
                        reads=[l32b, ssqb, parb], writes=[l32b])
                    P.dma("pool", lat_out[l, r0:r0 + nr, :], l32[0:nr, t, :], reads=[l32b], writes=[outb], cbuf=l32b)
                    rope_tm("pool", kr_keep[0:nr, t, :].unsqueeze(1), rtm[0:nr, t, :], 32, nr, 1, rA, rB, [rtmb, krb], [krb])
                    P.dma("pool", kr_out[l, r0:r0 + nr, :], kr_keep[0:nr, t, :], reads=[krb], writes=[outb], cbuf=krb)
                    P.op("pool", lambda t=t, nr=nr: nc.gpsimd.tensor_copy(out=lkb[0:nr, t, 0:256], in_=l32[0:nr, t, :]), reads=[l32b], writes=[lkbb])
                    P.op("pool", lambda t=t, nr=nr: nc.gpsimd.tensor_copy(out=lkb[0:nr, t, 256:320], in_=kr_keep[0:nr, t, :]), reads=[krb], writes=[lkbb])
                    kpos = kbase_of(s) + (r0 - g0 if is_p else 0)

                    def ev_l(ti_, i0, n, nr=nr):
                        P.op("act", lambda: nc.scalar.activation(out=lTt[:, :, 0:nr], in_=PT[ti_][:, 0:384].rearrange("p (a b) -> p a b", b=128)[:, :, 0:nr], func=AF.Identity),
                             reads=[PTb[ti_]], writes=[lTtb])
                    transposes([lkb[0:nr, t, 0:128], lkb[0:nr, t, 128:256], lkb[0:nr, t, 256:320]], ev_l, [lkbb])
                    P.dma("pool", LT_scr[s][:, :, kpos:kpos + nr].rearrange("a p t -> p a t"), lTt[:, 0:2, 0:nr], reads=[lTtb],
                          writes=[scrB[s]["LT"]], cbuf=lTtb)
                    P.dma("pool", RT_scr[s][:, kpos:kpos + nr], lTt[0:64, 2, 0:nr], reads=[lTtb], writes=[scrB[s]["RT"]], cbuf=lTtb)
                    P.dma("pool", L_scr[s][kpos:kpos + nr, :], lkb[0:nr, t, 0:256], reads=[lkbb], writes=[scrB[s]["L"]], cbuf=lkbb)
                for t, (r0, nr, s) in enumerate(tiles):
                    for hh4 in range(2):
                        bi = next_pb()

                        def mm(bi=bi, t=t, nr=nr, hh4=hh4):
                            for hq in range(4):
                                hd = hh4 * 4 + hq
                                for k in range(6):
                                    ins = nc.tensor.matmul(PB[bi][0:nr, hq * 64:hq * 64 + 64],
                                                           lhsT=cqT[:, k, t * 128:t * 128 + nr], rhs=wuq[:, k, hd * 192 + 128:hd * 192 + 192],
                                                           start=(k == 0), stop=(k == 5))
                            return ins
                        P.op("pe", mm, reads=[wb, cqTb], writes=[PBb[bi]])
                        P.op("act", lambda nr=nr, bi=bi, hh4=hh4: nc.scalar.activation(
                            out=qr32[0:nr, hh4 * 256:(hh4 + 1) * 256], in_=PB[bi][0:nr, 0:256], func=AF.Identity),
                            reads=[PBb[bi]], writes=[qr32b])
                    rope_tm("dve", qr32[0:nr, :].rearrange("p (h e) -> p h e", e=64), rtm[0:nr, t, :], 32, nr, 8, rA, rB, [rtmb, qr32b], [qr32b])
                    P.op("pool", lambda nr=nr: nc.gpsimd.tensor_copy(out=qrb[0:nr, :], in_=qr32[0:nr, :]), reads=[qr32b], writes=[qrbb])

                    def ev_qrT(ti_, i0, n, t=t, nr=nr):
                        for a in range(n):
                            for hh in range(2):
                                P.op("act", lambda a=a, hh=hh: nc.scalar.activation(out=qrT[:, (i0 + a) * 2 + hh, t * 128:t * 128 + nr],
                                                                             in_=PT[ti_][hh * 64:(hh + 1) * 64, a * 128:a * 128 + nr], func=AF.Identity),
                                     reads=[PTb[ti_]], writes=[qrTb])
                    transposes([qrb[0:nr, a * 128:(a + 1) * 128] for a in range(4)], ev_qrT, [qrbb])
                for h in range(8):
                    bi = next_pb()

                    def mm(bi=bi, h=h):
                        for k in range(6):
                            ins = nc.tensor.matmul(PB[bi][:, 0:G], lhsT=wuq[:, k, h * 192:h * 192 + 128], rhs=cqT[:, k, 0:G],
                                                   start=(k == 0), stop=(k == 5))
                        return ins
                    P.op("pe", mm, reads=[wb, cqTb], writes=[PBb[bi]])
                    P.op("dve", lambda bi=bi: nc.vector.tensor_copy(out=qnT[:, 0:G], in_=PB[bi][:, 0:G]), reads=[PBb[bi]], writes=[qnTb])
                    for eh in range(2):
                        b2 = next_pb()
                        P.op("pe", lambda b2=b2, h=h, eh=eh: nc.tensor.matmul(PB[b2][:, 0:G], lhsT=wukT[:, h, eh * 128:(eh + 1) * 128],
                                                                            rhs=qnT[:, 0:G], start=True, stop=True),
                             reads=[wb, qnTb], writes=[PBb[b2]])
                        P.op("act", lambda b2=b2, h=h, eh=eh: nc.scalar.activation(out=qlT[:, h, eh, 0:G], in_=PB[b2][:, 0:G], func=AF.Identity),
                             reads=[PBb[b2]], writes=[qlTb])
                for i in range(4):
                    si = next_w()

                    def ev_g(bi, sub, i=i):
                        P.op("act", lambda: nc.scalar.activation(out=mgT[:, i * 2 + sub, 0:G], in_=PB[bi][:, 0:G], func=AF.Silu),
                             reads=[PBb[bi]], writes=[mgTb])
                    fm_block(si, BW, ev_g)

            phase(7)
            with P.scope() as st1:
                wuv = sb("wuv", [128, 2, 8, 128], BF16, st1); wuvb = Buf("wuv")
                P.dma("sp", wuv[:], w_uv_bf[l], reads=[wcast[l]], writes=[wuvb], cbuf=wuvb)
                LTs = sb("LTs", [128, 2, NKT_MAX * 128], BF16, st1); RTs = sb("RTs", [64, NKT_MAX * 128], BF16, st1)
                Ls = sb("Ls", [128, NKT_MAX, 258], BF16, st1)
                kvb = Buf("mlakv")
                PTt = [sb(f"PTm{i}", [128, 512], BF16, st1) for i in range(3)]
                PTtb = [Buf(f"PTm{i}") for i in range(3)]
                olb = sb("olb", [128, 256], BF16, st1); olbb = Buf("olb")
                olT = sb("olT", [128, 2, G], BF16, st1); olTb = Buf("olT")
                rr = sb("rrm", [128, 1], F32, st1); rrb = Buf("rrm")
                P.op("dve", lambda: nc.vector.memset(Ls[:, :, 256:257], 1.0), writes=[kvb])
                ptt_rr = 0
                for (s, sc0, sn, stl) in segs:
                    kbase = kbase_of(s)
                    nkeys = kbase + sn
                    nkt = (nkeys + 127) // 128
                    qts = seg_qtiles(s, sc0, sn, stl)
                    P.dma("sp", LTs[:, :, 0:nkeys], LT_scr[s][:, :, 0:nkeys].rearrange("a p t -> p a t"), reads=[scrB[s]["LT"]], writes=[kvb], cbuf=kvb)
                    P.dma("sp", RTs[:, 0:nkeys], RT_scr[s][:, 0:nkeys], reads=[scrB[s]["RT"]], writes=[kvb], cbuf=kvb)
                    nfull = nkeys // 128
                    if nfull:
                        P.dma("sp", Ls[:, 0:nfull, 0:256], L_scr[s][0:nfull * 128, :].rearrange("(j p) e -> p j e", p=128),
                              reads=[scrB[s]["L"]], writes=[kvb], cbuf=kvb)
                    if nkeys % 128:
                        rem = nkeys % 128
                        P.dma("sp", Ls[0:rem, nfull, 0:256], L_scr[s][nfull * 128:nkeys, :], reads=[scrB[s]["L"]], writes=[kvb], cbuf=kvb)
                    for h in range(8):
                        for j in range(nkt):
                            nk = min(128, nkeys - j * 128)
                            vis = [qi for qi, (qc, nq, qg) in enumerate(qts) if (not is_p) or qg >= j]
                            if not vis:
                                continue
                            qlo = qts[vis[0]][0]; qhi = qts[vis[-1]][0] + qts[vis[-1]][1]
                            sbk = next_pb(0, 2)

                            def smm(sbk=sbk, j=j, nk=nk, qlo=qlo, qhi=qhi, h=h):
                                o = PB[sbk][0:nk, 0:qhi - qlo]
                                nc.tensor.matmul(o, lhsT=LTs[:, 0, j * 128:j * 128 + nk], rhs=qlT[:, h, 0, qlo:qhi], start=True, stop=False)
                                nc.tensor.matmul(o, lhsT=LTs[:, 1, j * 128:j * 128 + nk], rhs=qlT[:, h, 1, qlo:qhi], start=False, stop=False)
                                return nc.tensor.matmul(o, lhsT=RTs[:, j * 128:j * 128 + nk], rhs=qrT[:, h, qlo:qhi], start=False, stop=True)
                            P.op("pe", smm, reads=[kvb, qlTb, qrTb], writes=[PBb[sbk]])
                            pi = ptt_rr % 3
                            ptt_rr += 1
                            P.op("act", lambda sbk=sbk, pi=pi, nk=nk, qlo=qlo, qhi=qhi: nc.scalar.activation(
                                out=PTt[pi][0:nk, 0:qhi - qlo], in_=PB[sbk][0:nk, 0:qhi - qlo], func=AF.Exp, scale=MLA_SCALE),
                                reads=[PBb[sbk]], writes=[PTtb[pi]])
                            if is_p and qts[vis[0]][2] == j:
                                P.op("pool", lambda pi=pi: nc.gpsimd.memset(PTt[pi][64:128, 0:64], 0.0), writes=[PTtb[pi]])

                            def avmm(pi=pi, j=j, nk=nk, vis=vis, qlo=qlo):
                                for qi in vis:
                                    qc, nq, qg = qts[qi]
                                    last_j = (qg if is_p else nkt - 1)
                                    ins = nc.tensor.matmul(PB[2 + qi][0:nq, 0:257], lhsT=PTt[pi][0:nk, qc - qlo:qc - qlo + nq], rhs=Ls[0:nk, j, 0:257],
                                                           start=(j == 0), stop=(j == last_j))
                                return ins
                            P.op("pe", avmm, reads=[PTtb[pi], kvb], writes=[PBb[2 + qi] for qi in vis])
                        for qi, (qc, nq, qg) in enumerate(qts):
                            ob = PB[2 + qi]
                            P.op("dve", lambda ob=ob, nq=nq: nc.vector.reciprocal(out=rr[0:nq, :], in_=ob[0:nq, 256:257]), reads=[PBb[2 + qi]], writes=[rrb])
                            P.op("dve", lambda ob=ob, nq=nq: nc.vector.tensor_scalar_mul(out=olb[0:nq, :], in0=ob[0:nq, 0:256], scalar1=rr[0:nq, 0:1]),
                                 reads=[PBb[2 + qi], rrb], writes=[olbb])

                            def ev_ol(ti_, i0, n, qc=qc, nq=nq):
                                P.op("act", lambda: nc.scalar.activation(out=olT[:, :, qc:qc + nq],
                                                                   in_=PT[ti_][:, 0:256].rearrange("p (a b) -> p a b", b=128)[:, :, 0:nq], func=AF.Identity),
                                     reads=[PTb[ti_]], writes=[olTb])
                            transposes([olb[0:nq, 0:128], olb[0:nq, 128:256]], ev_ol, [olbb])
                        c_lo, c_hi = sc0, sc0 + sn
                        bo = next_pb(0, 2)

                        def omm(bo=bo, h=h, c_lo=c_lo, c_hi=c_hi):
                            nc.tensor.matmul(PB[bo][:, 0:c_hi - c_lo], lhsT=wuv[:, 0, h, :], rhs=olT[:, 0, c_lo:c_hi], start=True, stop=False)
                            return nc.tensor.matmul(PB[bo][:, 0:c_hi - c_lo], lhsT=wuv[:, 1, h, :], rhs=olT[:, 1, c_lo:c_hi], start=False, stop=True)
                        P.op("pe", omm, reads=[wuvb, olTb], writes=[PBb[bo]])
                        P.op("dve", lambda bo=bo, h=h, c_lo=c_lo, c_hi=c_hi: nc.vector.tensor_tensor(
                            out=mixT[:, 24 + h, c_lo:c_hi], in0=PB[bo][:, 0:c_hi - c_lo], in1=mgT[:, h, c_lo:c_hi], op=ALU.mult),
                            reads=[PBb[bo], mgTb], writes=[mixTb[2]])

        phase(8)
        with P.scope() as st:
            ot = sb("ot", [128, D], F32, st); otb = Buf("ot")
            xr = sb("xr", [128, D], F32, st); xrb = Buf("xr")
            gbc = sb("gbc", [128, D], F32, st); gbcb = Buf("gbc")
            lg = sb("lg", [128, D], F32, st); lb_ = sb("lb", [128, D], F32, st); lgb = Buf("lg")
            stats = sb("stats", [128, 8, 6], F32, st); mv = sb("mv", [128, 2], F32, st); stb = Buf("stats")
            P.dma("sp", lg[:], ln_g[l:l + 1, :].partition_broadcast(128), writes=[lgb], cbuf=lgb)
            P.dma("sp", lb_[:], ln_b[l:l + 1, :].partition_broadcast(128), writes=[lgb], cbuf=lgb)
            for t, (r0, nr, s) in enumerate(tiles):
                if t == 0 or not is_p:
                    P.dma("sp", gbc[0:nr, :], mod_scr[l, s:s + 1, 2 * D:3 * D].partition_broadcast(nr), reads=[modb], writes=[gbcb], cbuf=gbcb)
                P.dma("sp", xr[0:nr, :], xin[r0:r0 + nr, :], writes=[xrb], cbuf=xrb)
                P.op("pool", lambda nr=nr: nc.gpsimd.tensor_scalar(out=xr[0:nr, :], in0=xr[0:nr, :], scalar1=ALPHA, scalar2=None, op0=ALU.mult),
                     reads=[xrb], writes=[xrb])
                for bi_ in range(NOB):
                    si = next_w()
                    bk = next_pb()

                    def mm(bk=bk, si=si, t=t, nr=nr):
                        for k in range(KT):
                            ins = nc.tensor.matmul(PB[bk][0:nr, 0:BW], lhsT=mixT[:, k, t * 128:t * 128 + nr], rhs=WS[si][:, k, :],
                                                   start=(k == 0), stop=(k == KT - 1))
                        return ins
                    P.op("pe", mm, reads=[WSb[si]] + mixTb, writes=[PBb[bk]])
                    cs = slice(bi_ * BW, (bi_ + 1) * BW)
                    P.op("dve", lambda bk=bk, cs=cs, nr=nr: nc.vector.tensor_tensor(out=ot[0:nr, cs], in0=PB[bk][0:nr, 0:BW], in1=gbc[0:nr, cs], op=ALU.mult),
                         reads=[PBb[bk], gbcb], writes=[otb])
                    P.op("pool", lambda cs=cs, nr=nr: nc.gpsimd.tensor_tensor(out=ot[0:nr, cs], in0=ot[0:nr, cs], in1=xr[0:nr, cs], op=ALU.add),
                         reads=[xrb, otb], writes=[otb])
                for c in range(8):
                    P.op("dve", lambda c=c, nr=nr: nc.vector.bn_stats(out=stats[0:nr, c, :], in_=ot[0:nr, c * 512:(c + 1) * 512]), reads=[otb], writes=[stb])
                P.op("dve", lambda nr=nr: nc.vector.bn_aggr(out=mv[0:nr, :], in_=stats[0:nr, :, :]), reads=[stb], writes=[stb])
                rsqrt(mv[0:nr, 1:2], mv[0:nr, 1:2], 1.0, 1e-5, [stb], [stb])
                for hf in range(2):
                    cs = slice(hf * 2048, (hf + 1) * 2048)
                    P.op("dve", lambda cs=cs, nr=nr: nc.vector.tensor_scalar(out=ot[0:nr, cs], in0=ot[0:nr, cs], scalar1=mv[0:nr, 0:1], scalar2=mv[0:nr, 1:2],
                                                                           op0=ALU.subtract, op1=ALU.mult), reads=[otb, stb], writes=[otb])
                    P.op("pool", lambda cs=cs, nr=nr: nc.gpsimd.tensor_tensor(out=ot[0:nr, cs], in0=ot[0:nr, cs], in1=lg[0:nr, cs], op=ALU.mult),
                         reads=[otb, lgb], writes=[otb])
                    P.op("pool", lambda cs=cs, nr=nr: nc.gpsimd.tensor_tensor(out=ot[0:nr, cs], in0=ot[0:nr, cs], in1=lb_[0:nr, cs], op=ALU.add),
                         reads=[otb, lgb], writes=[otb])
                P.dma("pool", xout[r0:r0 + nr, :], ot[0:nr, :], reads=[otb], writes=[x1b if (l == 0 and NL > 1) else outb], cbuf=otb)

    try:
        phase(1)
        for l in range(NL):
            layer_setup(l)
            for grp in groups:
                process_group(l, grp)
    except _StopBuild:
        pass
    E = P.engs["sp"]
    for c in P.all_counters:
        if c.step == 16 and c.sem is not None and not getattr(c, "dead", False):
            P._wait_raw(E, (c.sem, c.val))
    P.barrier(("pe", "act", "dve", "pool", "sp"))
    print(f"[build] T={T} NL={NL} instr_ops={P.ninstr} sems={P.nsem}")
    es.close()
    return nc


_CACHE = {}
_HOOK = {}


def _rope_tab(pos, rot_dim):
    half = rot_dim // 2
    inv = (ROPE_THETA ** (-np.arange(half, dtype=np.float32) * (2.0 / rot_dim))).astype(np.float32)
    ang = pos.astype(np.float32)[:, None] * inv[None, :]
    return np.concatenate([np.cos(ang), np.sin(ang)], axis=1).astype(np.float32)


def kernel(T=SEQ, NL=DEPTH, **inp):
    n = 8
    f = lambda a: np.ascontiguousarray(np.asarray(a, dtype=np.float32))
    key = (T, NL)
    if key not in _CACHE:
        _CACHE[key] = build(T, NL, do_sample=not _HOOK.get("no_sample"), stop_after=_HOOK.get("stop_after"), DL=_HOOK.get("DL", DEPTH))
    nc = _CACHE[key]
    pos_p = np.arange(T)
    pos_s = np.tile(PAST + np.arange(DEC_SEQ), 2)
    tri = np.triu(np.ones((128, 128), np.float32))
    negmask = np.where(np.arange(128)[None, :] >= np.arange(128)[:, None], 0.0, -1e30).astype(np.float32)
    shared = dict(
        w_mod=f(np.asarray(inp["w_mod"])[:_HOOK.get("DL", DEPTH)]), b_mod=f(np.asarray(inp["b_mod"])[:_HOOK.get("DL", DEPTH)]),
        w_in=f(np.asarray(inp["w_in"])[:_HOOK.get("DL", DEPTH)]), conv_w=f(inp["conv_w"]), conv_b=f(inp["conv_b"]),
        dt_bias=f(inp["dt_bias"]), a_log=f(inp["a_log"]), d_skip=f(inp["d_skip"]), ssd_norm_w=f(inp["ssd_norm_w"]),
        lambda_q1=f(inp["lambda_q1"]), lambda_k1=f(inp["lambda_k1"]), lambda_q2=f(inp["lambda_q2"]), lambda_k2=f(inp["lambda_k2"]),
        diff_norm_w=f(inp["diff_norm_w"]), mla_q_norm_w=f(inp["mla_q_norm_w"]), mla_kv_norm_w=f(inp["mla_kv_norm_w"]),
        w_uq=f(inp["w_uq"]), w_ukT=f(np.transpose(np.asarray(inp["w_uk"]), (0, 3, 2, 1))), w_uv=f(inp["w_uv"]),
        w_out=f(np.asarray(inp["w_out"])[:_HOOK.get("DL", DEPTH)]), ln_g=f(inp["ln_g"]), ln_b=f(inp["ln_b"]),
        rope_d_p=_rope_tab(pos_p, 16), rope_m_p=_rope_tab(pos_p, 64), rope_d_s=_rope_tab(pos_s, 16), rope_m_s=_rope_tab(pos_s, 64),
        tri=tri, negmask=negmask)
    xp = np.asarray(inp["x_prompt"]); xs = np.asarray(inp["x_sample"])
    cp = np.asarray(inp["c_prompt"]); cs = np.asarray(inp["c_sample"])
    in_maps = []
    for c in range(n):
        b = c % 4
        s0 = 2 * c
        cc = np.stack([cp[b], cs[s0], cs[s0 + 1]], axis=0)
        cTl = f(cc.T.reshape(KT, 128, 3).transpose(1, 0, 2))
        m = dict(shared)
        m.update(
            x_p=f(xp[b, :T]), x_s=f(xs[s0:s0 + 2].reshape(64, D)), cT=cTl,
            ck=f(np.asarray(inp["cache_diff_k"])[:, s0:s0 + 2].reshape(DEPTH, 2, PAST, 1024)),
            cv=f(np.asarray(inp["cache_diff_v"])[:, s0:s0 + 2].reshape(DEPTH, 2, PAST, 1024)),
            cl=f(np.asarray(inp["cache_mla_latent"])[:, s0:s0 + 2]), cr=f(np.asarray(inp["cache_mla_krope"])[:, s0:s0 + 2]),
            st_ssm=f(np.asarray(inp["state_ssm"])[:, s0:s0 + 2].reshape(DEPTH, 2, 2048, 128)),
            st_conv=f(np.asarray(inp["state_conv"])[:, s0:s0 + 2]))
        in_maps.append(m)
    if _HOOK.get("in_maps_only"):
        return nc, in_maps
    if _HOOK.get("n_cores"):
        m_ = _HOOK["n_cores"]
        res = run_bass_kernel_spmd(nc, in_maps[:m_], core_ids=list(range(m_)))
        return res.results
    res = run_bass_kernel_spmd(nc, in_maps, core_ids=list(range(n)))
    R = res.results
    B = 4
    y_prompt = np.stack([R[b]["y_p"] for b in range(B)])
    y_sample = np.concatenate([R[c]["y_s"].reshape(2, DEC_SEQ, D) for c in range(n)])
    def gp(name, shp):
        return np.stack([R[b][name].reshape(shp) for b in range(B)], axis=1)
    def gs(name, shp):
        return np.concatenate([R[c][name].reshape(shp) for c in range(n)], axis=1)
    outs = (
        y_prompt, y_sample,
        gp("dk_p", (DEPTH, T, 8, 2, 64)), gp("dv_p", (DEPTH, T, 8, 128)), gp("lat_p", (DEPTH, T, 256)), gp("kr_p", (DEPTH, T, 64)),
        gp("ssm_p", (DEPTH, NH, HP, NS)), gp("conv_p", (DEPTH, 3, CONVD)),
        gs("dk_s", (DEPTH, 2, DEC_SEQ, 8, 2, 64)), gs("dv_s", (DEPTH, 2, DEC_SEQ, 8, 128)), gs("lat_s", (DEPTH, 2, DEC_SEQ, 256)),
        gs("kr_s", (DEPTH, 2, DEC_SEQ, 64)), gs("ssm_s", (DEPTH, 2, NH, HP, NS)), gs("conv_s", (DEPTH, 2, 3, CONVD)))
    return tuple(np.ascontiguousarray(o.astype(np.float32)) for o in outs)
```

```python
import math
from contextlib import ExitStack
import numpy as np
import concourse.bass as bass
import concourse.mybir as mybir
from concourse.bass_utils import run_bass_kernel_spmd

F32 = mybir.dt.float32
BF16 = mybir.dt.bfloat16
AF = mybir.ActivationFunctionType
ALU = mybir.AluOpType
AX = mybir.AxisListType

D = 4096
KT = 32
DEPTH = 2
SEQ = 4096
PAST = 4096
DEC_SEQ = 32
NH = 32
HP = 64
NS = 128
CONVD = 3072
INW = 11360
ROPE_THETA = 500000.0
ALPHA = (2 * DEPTH) ** 0.25
DIFF_SCALE = 64 ** -0.5
MLA_SCALE = 192 ** -0.5
O_Z, O_XBC, O_DT, O_DQ, O_DK, O_DV, O_DG, O_CQ, O_CKV, O_KR, O_MG = (
    0, 2048, 5120, 5152, 6176, 7200, 8224, 9248, 10016, 10272, 10336)
BW = 256

SEM_LIMIT = 30000


class Counter:
    def __init__(self, prog, step, kind="eng", depth=0):
        self.prog, self.step, self.kind, self.depth = prog, step, kind, depth
        self.sem, self.val, self.gen = None, 0, -1
        self.max_wait = self.safe = 0
        self.final = {}
        prog.register_counter(self)

    def bump(self):
        if self.sem is None or self.val + self.step > SEM_LIMIT:
            if self.sem is not None:
                self.final[self.gen] = (self.sem, self.val)
            self.sem, self.val = self.prog.new_sem(self.kind)
            self.max_wait = self.safe = self.val
            self.gen += 1
        self.val += self.step
        return (self.sem, self.val, self, self.gen)


_DEPTH = [0]


class Buf:
    __slots__ = ("w", "r", "ctr", "name", "depth", "excl")

    def __init__(self, name="", excl=False):
        self.excl = excl
        self.w = None
        self.r = {}
        self.ctr = None
        self.name = name
        self.depth = _DEPTH[0]


class _Scope:
    def __init__(self, prog):
        self.prog = prog
        self.st = ExitStack()

    def __enter__(self):
        self.prog.scope_stack.append([])
        _DEPTH[0] = len(self.prog.scope_stack)
        return self.st

    def __exit__(self, *a):
        P = self.prog
        ctrs = P.scope_stack.pop()
        _DEPTH[0] = len(P.scope_stack)
        names = ("pe", "act", "dve", "pool", "sp")
        for n in names:
            E = P.engs[n]
            for c in ctrs:
                if c.sem is not None and c.step == 16:
                    P._wait_raw(E, (c.sem, c.val))
                    c.max_wait = max(c.max_wait, c.val)
        P.barrier(names)
        for c in ctrs:
            if c.sem is not None and c.step == 16:
                P.free_sems.setdefault(c.kind, []).append((c.sem, c.val))
                c.dead = True
        self.st.close()
        return False


class Eng:
    def __init__(self, prog, name, h):
        self.name, self.h = name, h
        self.ctr = Counter(prog, 1)
        self.waited = {}
        self.last = None


class Prog:
    def __init__(self, nc, es):
        self.nc, self.es = nc, es
        self.nsem = 0
        self.free_sems = {}
        self.log = []
        self.semname = {}
        self.scope_stack = []
        self.all_counters = []
        self.engs = {
            "pe": Eng(self, "pe", nc.tensor), "act": Eng(self, "act", nc.scalar),
            "dve": Eng(self, "dve", nc.vector), "pool": Eng(self, "pool", nc.gpsimd),
            "sp": Eng(self, "sp", nc.sync),
        }
        self.ninstr = 0

    def new_sem(self, kind):
        fs = self.free_sems.setdefault(kind, [])
        for i, (sem, val) in enumerate(fs):
            if val < SEM_LIMIT // 2:
                fs.pop(i)
                return sem, val
        self.nsem += 1
        sem = self.es.enter_context(self.nc.semaphore(f"s{self.nsem}"))
        self.semname[id(sem)] = f"s{self.nsem}"
        return sem, 0

    def register_counter(self, c):
        self.all_counters.append(c)
        if c.depth > 0:
            self.scope_stack[c.depth - 1].append(c)

    def scope(self):
        return _Scope(self)

    def _wait(self, E, tok):
        if tok is None:
            return
        sem, val, ctr, gen, owner = tok
        if owner == "pe" and E.name == "pe":
            return
        if owner is None:
            if gen == ctr.gen:
                sem, val = ctr.sem, ctr.val
                ctr.max_wait = max(ctr.max_wait, val)
            else:
                sem, val = ctr.final[gen]
        key = id(sem)
        if E.waited.get(key, 0) >= val:
            return
        E.h.wait_ge(sem, val)
        E.waited[key] = val
        self.log.append(("wait", E.name, self.semname.get(id(sem)), val, None))

    def _deps(self, E, reads, writes):
        for b in reads:
            self._wait(E, b.w)
            if b.excl:
                for t in b.r.values():
                    if t[4] != E.name:
                        self._wait(E, t)
        for b in writes:
            self._wait(E, b.w)
            for t in b.r.values():
                self._wait(E, t)

    def _record(self, tok, reads, writes):
        for b in reads:
            b.r[id(tok[2])] = tok
        for b in writes:
            b.w = tok
            b.r = {}

    def op(self, eng, fn, reads=(), writes=()):
        E = self.engs[eng]
        self._deps(E, reads, writes)
        ins = fn()
        sem, val, ctr, gen = E.ctr.bump()
        ins.then_inc(sem, 1)
        tok = (sem, val, ctr, gen, eng)
        E.last = tok
        self._record(tok, reads, writes)
        self.ninstr += 1
        return tok

    def dma(self, q, out, in_, reads=(), writes=(), cbuf=None, nc_ok=False):
        E = self.engs[q]
        self._deps(E, reads, writes)
        kind = "sw" if q == "pool" else "hw"
        if cbuf.ctr is None:
            cbuf.ctr = {}
        if kind not in cbuf.ctr:
            cbuf.ctr[kind] = Counter(self, 16, kind, cbuf.depth)
        c = cbuf.ctr[kind]
        if c.sem is not None and c.max_wait > c.safe and c.val + 16 <= SEM_LIMIT:
            self._wait_raw(E, (c.sem, c.val))
            c.safe = c.val
        if nc_ok:
            ins = E.h.dma_start(out=out, in_=in_, allow_slow_non_contiguous=True)
        else:
            ins = E.h.dma_start(out=out, in_=in_)
        sem, val, ctr, gen = c.bump()
        ins.then_inc(sem, 16)
        self.log.append(("dma", q, self.semname.get(id(sem)), val, cbuf.name))
        tok = (sem, val, ctr, gen, None)
        self._record(tok, reads, writes)
        self.ninstr += 1
        return tok

    def barrier(self, names=("pe", "act", "dve", "pool")):
        toks = [self.engs[n].last for n in names]
        for n in names:
            E = self.engs[n]
            for t in toks:
                if t is not None and t[4] != n:
                    self._wait_raw(E, t)

    def _wait_raw(self, E, tok):
        sem, val = tok[0], tok[1]
        key = id(sem)
        if E.waited.get(key, 0) >= val:
            return
        E.h.wait_ge(sem, val)
        E.waited[key] = val
        self.log.append(("waitraw", E.name, self.semname.get(id(sem)), val, None))


def in_blocks():
    bl = []
    bl.append(("dtb", "TM", O_DT, BW))
    bl.append(("krb", "TM", O_KR, BW))
    for i in range(12):
        bl.append((f"xbc{i}", "FM", O_XBC + i * BW, BW))
    for i in range(8):
        bl.append((f"z{i}", "FM", O_Z + i * BW, BW))
    for i in range(4):
        bl.append((f"dq{i}", "TM", O_DQ + i * BW, BW))
    for i in range(4):
        bl.append((f"dk{i}", "TM", O_DK + i * BW, BW))
    for i in range(4):
        bl.append((f"dv{i}", "TM", O_DV + i * BW, BW))
    for i in range(4):
        bl.append((f"dg{i}", "FM", O_DG + i * BW, BW))
    for i in range(3):
        bl.append((f"cq{i}", "TM", O_CQ + i * BW, BW))
    bl.append(("ckv", "TM", O_CKV, BW))
    for i in range(4):
        bl.append((f"mg{i}", "FM", O_MG + i * BW, BW))
    return bl


IN_BLOCKS = in_blocks()
NIB = len(IN_BLOCKS)
NOB = D // BW


class _StopBuild(Exception):
    pass


def build(T, NL=DEPTH, do_sample=True, stop_after=None, DL=DEPTH):
    assert T % 512 == 0
    nc = bass.Bass("TRN2", target_bir_lowering=False)
    TS = 64
    TK = PAST + DEC_SEQ

    def din(name, shape, dt=F32):
        return nc.dram_tensor(name, list(shape), dt, kind="ExternalInput").ap()

    def dout(name, shape, dt=F32):
        return nc.dram_tensor(name, list(shape), dt, kind="ExternalOutput").ap()

    def dint(name, shape, dt=F32):
        return nc.dram_tensor(name, list(shape), dt, kind="Internal").ap()

    x_p = din("x_p", [T, D]); x_s = din("x_s", [TS, D]); cT = din("cT", [128, KT, 3])
    ck = din("ck", [DEPTH, 2, PAST, 1024]); cv = din("cv", [DEPTH, 2, PAST, 1024])
    cl = din("cl", [DEPTH, 2, PAST, 256]); cr = din("cr", [DEPTH, 2, PAST, 64])
    st_ssm = din("st_ssm", [DEPTH, 2, 2048, 128]); st_conv = din("st_conv", [DEPTH, 2, 3, CONVD])
    w_mod = din("w_mod", [DL, D, 3 * D]); b_mod = din("b_mod", [DL, 3 * D])
    w_in = din("w_in", [DL, D, INW]); conv_w = din("conv_w", [DEPTH, 4, CONVD]); conv_b = din("conv_b", [DEPTH, CONVD])
    dt_bias = din("dt_bias", [DEPTH, NH]); a_log = din("a_log", [DEPTH, NH]); d_skip = din("d_skip", [DEPTH, NH])
    ssd_norm_w = din("ssd_norm_w", [DEPTH, 2048])
    lam_q1 = din("lambda_q1", [DEPTH, 64]); lam_k1 = din("lambda_k1", [DEPTH, 64])
    lam_q2 = din("lambda_q2", [DEPTH, 64]); lam_k2 = din("lambda_k2", [DEPTH, 64])
    diff_norm_w = din("diff_norm_w", [DEPTH, 128]); q_norm_w = din("mla_q_norm_w", [DEPTH, 768])
    kv_norm_w = din("mla_kv_norm_w", [DEPTH, 256])
    w_uq = din("w_uq", [DEPTH, 768, 1536]); w_ukT = din("w_ukT", [DEPTH, 128, 8, 256]); w_uv = din("w_uv", [DEPTH, 256, 8, 128])
    w_out = din("w_out", [DL, D, D]); ln_g = din("ln_g", [DEPTH, D]); ln_b = din("ln_b", [DEPTH, D])
    rope_d_p = din("rope_d_p", [T, 16]); rope_m_p = din("rope_m_p", [T, 64])
    rope_d_s = din("rope_d_s", [TS, 16]); rope_m_s = din("rope_m_s", [TS, 64])
    tri_in = din("tri", [128, 128]); negmask_in = din("negmask", [128, 128])

    y_p = dout("y_p", [T, D]); y_s = dout("y_s", [TS, D])
    dk_p = dout("dk_p", [DEPTH, T, 1024]); dv_p = dout("dv_p", [DEPTH, T, 1024])
    lat_p = dout("lat_p", [DEPTH, T, 256]); kr_p = dout("kr_p", [DEPTH, T, 64])
    ssm_p = dout("ssm_p", [DEPTH, 2048, 128]); conv_p = dout("conv_p", [DEPTH, 3, CONVD])
    dk_s = dout("dk_s", [DEPTH, TS, 1024]); dv_s = dout("dv_s", [DEPTH, TS, 1024])
    lat_s = dout("lat_s", [DEPTH, TS, 256]); kr_s = dout("kr_s", [DEPTH, TS, 64])
    ssm_s = dout("ssm_s", [DEPTH, 2, 2048, 128]); conv_s = dout("conv_s", [DEPTH, 2, 3, CONVD])

    w_in_bf = dint("w_in_bf", [DEPTH, NIB, 128, KT, BW], BF16)
    w_out_bf = dint("w_out_bf", [DEPTH, NOB, 128, KT, BW], BF16)
    w_uq_bf = dint("w_uq_bf", [DEPTH, 128, 6, 1536], BF16)
    w_ukT_bf = dint("w_ukT_bf", [DEPTH, 128, 8, 256], BF16)
    w_uv_bf = dint("w_uv_bf", [DEPTH, 128, 2, 8, 128], BF16)
    mod_scr = dint("mod_scr", [DEPTH, 3, 3 * D])
    x1_p = dint("x1_p", [T, D]); x1_s = dint("x1_s", [TS, D])
    TKP = T
    KT_scr = [dint("KT_p", [8, 128, TKP], BF16), dint("KT_s0", [8, 128, TK], BF16), dint("KT_s1", [8, 128, TK], BF16)]
    V_scr = [dint("V_p", [TKP, 1024], BF16), dint("V_s0", [TK, 1024], BF16), dint("V_s1", [TK, 1024], BF16)]
    LT_scr = [dint("LT_p", [2, 128, TKP], BF16), dint("LT_s0", [2, 128, TK], BF16), dint("LT_s1", [2, 128, TK], BF16)]
    L_scr = [dint("L_p", [TKP, 256], BF16), dint("L_s0", [TK, 256], BF16), dint("L_s1", [TK, 256], BF16)]
    RT_scr = [dint("RT_p", [64, TKP], BF16), dint("RT_s0", [64, TK], BF16), dint("RT_s1", [64, TK], BF16)]
    scrB = [dict(KT=Buf(), V=Buf(), LT=Buf(), L=Buf(), RT=Buf()) for _ in range(3)]

    es = ExitStack()
    P = Prog(nc, es)

    uniq = [0]

    def sb(name, shape, dt, stack=None):
        uniq[0] += 1
        return (stack or es).enter_context(nc.sbuf_tensor(f"{name}_{uniq[0]}", list(shape), dt))

    def ps(name, shape, dt):
        return es.enter_context(nc.psum_tensor(name, list(shape), dt))

    PB = [ps(f"pb{i}", [128, 512], F32) for i in range(6)]
    PBb = [Buf(f"pb{i}", excl=True) for i in range(6)]
    PT = [ps(f"pt{i}", [128, 1024], BF16) for i in range(2)]
    PTb = [Buf(f"pt{i}", excl=True) for i in range(2)]
    pb_rr = [0]
    pt_rr = [0]

    def next_pb(lo=0, hi=6):
        i = lo + pb_rr[0] % (hi - lo)
        pb_rr[0] += 1
        return i

    def next_pt():
        i = pt_rr[0] % 2
        pt_rr[0] += 1
        return i

    uT = sb("uT", [128, KT, 512], BF16); uTb = Buf("uT")
    mixT = sb("mixT", [128, KT, 512], BF16); mixTb = [Buf(f"mix{i}") for i in range(3)]
    NWS = 2
    WS = [sb(f"ws{i}", [128, KT, BW], BF16) for i in range(NWS)]
    WSb = [Buf(f"ws{i}") for i in range(NWS)]
    ident = sb("ident", [128, 128], BF16); identb = Buf("ident")
    identf = sb("identf", [128, 128], F32)
    tri = sb("tri", [128, 128], F32); negmask = sb("negmask", [128, 128], F32)
    onesf = sb("onesf", [128, 128], F32)
    ones512 = sb("ones512", [128, 128], BF16)
    constb = Buf("const")
    S32 = sb("S32", [128, 2048], F32); Sbf = sb("Sbf", [128, 2048], BF16)
    Sb = [Buf(f"S{i}") for i in range(4)]
    sc1 = sb("sc1", [128, 3, KT], F32); sh = sb("sh", [128, 3, KT], F32); modb = Buf("mod")
    convw = sb("convw", [128, 24, 4], F32); convb = sb("convb", [128, 24], F32)
    dtb_bc = sb("dtb_bc", [128, NH], F32); A_bc = sb("A_bc", [128, NH], F32); Dcol = sb("Dcol", [128, 16], F32)
    normw = sb("normw", [128, 16], F32)
    dnw_bc = sb("dnw_bc", [128, 128], F32); qnw_bc = sb("qnw_bc", [128, 768], F32); kvnw_bc = sb("kvnw_bc", [128, 256], F32)
    neglam = sb("neglam", [128, 1], F32)
    lamt = sb("lamt", [128, 4, 64], F32); lams = sb("lams", [128, 2], F32)
    parb = Buf("par")
    carry = sb("carry", [128, 24, 4], BF16); carryb = Buf("carry")
    kr_keep = sb("kr_keep", [128, 4, 64], F32); krb = Buf("kr_keep")

    P.dma("sp", tri[:], tri_in[:, :], writes=[constb], cbuf=constb)
    P.dma("sp", negmask[:], negmask_in[:, :], writes=[constb], cbuf=constb)
    P.op("pool", lambda: nc.gpsimd.memset(identf[:], 0.0), writes=[constb])
    P.op("pool", lambda: nc.gpsimd.affine_select(out=identf[:], in_=identf[:], pattern=[[-1, 128]],
                                                 compare_op=ALU.not_equal, fill=1.0, base=0, channel_multiplier=1),
         writes=[constb])
    P.op("pool", lambda: nc.gpsimd.tensor_copy(out=ident[:], in_=identf[:]), writes=[constb, identb])
    P.op("pool", lambda: nc.gpsimd.memset(onesf[:], 1.0), writes=[constb])
    P.op("pool", lambda: nc.gpsimd.memset(ones512[:], 1.0 / 512.0), writes=[constb])
    P.op("pool", lambda: nc.gpsimd.memset(uT[:], 0.0), writes=[uTb])
    P.op("pool", lambda: nc.gpsimd.memset(mixT[:], 0.0), writes=mixTb)

    wcast = [Buf(f"wcast{l}") for l in range(DEPTH)]

    def cast_weights(l):
        wc = wcast[l]
        for bi, (name, kind, c0, w) in enumerate(IN_BLOCKS):
            P.dma("pool", w_in_bf[l, bi, :, :, 0:w],
                  w_in[l, :, c0:c0 + w].rearrange("(k p) w -> p k w", p=128), writes=[wc], cbuf=wc)
        for bi in range(NOB):
            P.dma("pool", w_out_bf[l, bi], w_out[l, :, bi * BW:(bi + 1) * BW].rearrange("(k p) w -> p k w", p=128),
                  writes=[wc], cbuf=wc)
        P.dma("pool", w_uq_bf[l], w_uq[l].rearrange("(k p) w -> p k w", p=128), writes=[wc], cbuf=wc)
        P.dma("pool", w_ukT_bf[l], w_ukT[l], writes=[wc], cbuf=wc)
        P.dma("pool", w_uv_bf[l], w_uv[l].rearrange("(k p) h d -> p k h d", p=128), writes=[wc], cbuf=wc)

    cast_weights(0)

    with P.scope() as st:
      if stop_after is None or stop_after >= 1:
            scT = sb("scT", [128, KT, 3], F32, st); scTb = sb("scTb", [128, KT, 3], BF16, st)
            modrow = [sb(f"modrow{i}", [3, BW], F32, st) for i in range(2)]; bmod = [sb(f"bmod{i}", [3, BW], F32, st) for i in range(2)]
            mrb = [Buf("mr0"), Buf("mr1")]
            tb = Buf("modtmp")
            mws = 0
            P.dma("sp", scT[:], cT[:, :, :], writes=[tb], cbuf=tb)
            P.op("act", lambda: nc.scalar.activation(out=scTb[:], in_=scT[:], func=AF.Silu), reads=[tb], writes=[tb])
            for l in range(NL):
                for cb in range(3 * D // BW):
                    si = mws % NWS
                    mi = mws % 2
                    mws += 1
                    P.dma("pool", WS[si][:], w_mod[l, :, cb * BW:(cb + 1) * BW].rearrange("(k p) w -> p k w", p=128),
                          writes=[WSb[si]], cbuf=WSb[si])
                    P.dma("sp", bmod[mi][:], b_mod[l:l + 1, cb * BW:(cb + 1) * BW].partition_broadcast(3), writes=[mrb[mi]], cbuf=mrb[mi])
                    bi = next_pb()

                    def mm(si=si, bi=bi):
                        for k in range(KT):
                            ins = nc.tensor.matmul(PB[bi][0:3, 0:BW], lhsT=scTb[:, k, :], rhs=WS[si][:, k, :],
                                                   start=(k == 0), stop=(k == KT - 1))
                        return ins
                    P.op("pe", mm, reads=[WSb[si], tb], writes=[PBb[bi]])
                    P.op("dve", lambda bi=bi, mi=mi: nc.vector.tensor_tensor(
                        out=modrow[mi][:], in0=PB[bi][0:3, 0:BW], in1=bmod[mi][:], op=ALU.add),
                        reads=[PBb[bi], mrb[mi]], writes=[mrb[mi]])
                    P.dma("sp", mod_scr[l, :, cb * BW:(cb + 1) * BW], modrow[mi][:], reads=[mrb[mi]], writes=[modb], cbuf=mrb[mi])

    for l in range(1, NL):
        cast_weights(l)

    groups = []
    for gi in range(T // 512):
        groups.append(dict(
            G=512, tiles=[(gi * 512 + i * 128, 128, 0) for i in range(4)], segs=[(0, 0, 512, [0, 1, 2, 3])], prompt=True, g0=gi * 512,
            xin=[x_p, x1_p], xout=[x1_p if NL > 1 else y_p, y_p], rope_d=rope_d_p, rope_m=rope_m_p))
    if do_sample:
        groups.append(dict(
            G=160, tiles=[(0, 32, 1), (32, 32, 2)], segs=[(1, 0, 32, [0]), (2, 128, 32, [1])], prompt=False, g0=0,
            xin=[x_s, x1_s], xout=[x1_s if NL > 1 else y_s, y_s], rope_d=rope_d_s, rope_m=rope_m_s))

    wseq = []
    for l in range(NL):
        for grp in groups:
            for bi in range(NIB):
                wseq.append((w_in_bf[l, bi], l, IN_BLOCKS[bi][3]))
            for t in range(len(grp["tiles"])):
                for bi in range(NOB):
                    wseq.append((w_out_bf[l, bi], l, BW))
    wstate = dict(issued=0, consumed=0)

    def next_w():
        while wstate["issued"] < min(len(wseq), wstate["consumed"] + NWS):
            i = wstate["issued"]
            si = i % NWS
            ap, l, w = wseq[i]
            P.dma("sp", WS[si][:, :, 0:w], ap[:, :, 0:w], reads=[wcast[l]], writes=[WSb[si]], cbuf=WSb[si])
            wstate["issued"] += 1
        si = wstate["consumed"] % NWS
        wstate["consumed"] += 1
        return si

    def phase(n):
        if stop_after is not None and n > stop_after:
            raise _StopBuild()

    def layer_setup(l):
        for s in range(3):
            P.dma("sp", sh[:, s, :], mod_scr[l, s, 0:D].rearrange("(k p) -> p k", p=128), reads=[modb], writes=[parb],
                  cbuf=parb, nc_ok=True)
            P.dma("sp", sc1[:, s, :], mod_scr[l, s, D:2 * D].rearrange("(k p) -> p k", p=128), reads=[modb], writes=[parb],
                  cbuf=parb, nc_ok=True)
        P.op("dve", lambda: nc.vector.tensor_scalar_add(out=sc1[:], in0=sc1[:], scalar1=1.0), reads=[parb], writes=[parb])
        for j in range(4):
            P.dma("sp", convw[:, :, j], conv_w[l, j].rearrange("(t p) -> p t", p=128), writes=[parb], cbuf=parb, nc_ok=True)
        P.dma("sp", convb[:], conv_b[l].rearrange("(t p) -> p t", p=128), writes=[parb], cbuf=parb, nc_ok=True)
        P.dma("sp", dtb_bc[:], dt_bias[l:l + 1, :].partition_broadcast(128), writes=[parb], cbuf=parb)
        P.dma("sp", A_bc[:], a_log[l:l + 1, :].partition_broadcast(128), writes=[parb], cbuf=parb)
        P.op("act", lambda: nc.scalar.activation(out=A_bc[:], in_=A_bc[:], func=AF.Exp), reads=[parb], writes=[parb])
        P.op("dve", lambda: nc.vector.tensor_scalar_mul(out=A_bc[:], in0=A_bc[:], scalar1=-1.0), reads=[parb], writes=[parb])
        for hh in range(2):
            src = bass.AP(d_skip.tensor, d_skip[l, hh:hh + 1].offset, [[0, 64], [2, 16]])
            P.dma("sp", Dcol[hh * 64:(hh + 1) * 64, :], src, writes=[parb], cbuf=parb, nc_ok=True)
        P.dma("sp", normw[:], ssd_norm_w[l].rearrange("(t p) -> p t", p=128), writes=[parb], cbuf=parb, nc_ok=True)
        P.dma("sp", dnw_bc[:], diff_norm_w[l:l + 1, :].partition_broadcast(128), writes=[parb], cbuf=parb)
        lam_init = 0.8 - 0.6 * math.exp(-0.3 * l)
        P.op("dve", lambda: nc.vector.tensor_scalar_mul(out=dnw_bc[:], in0=dnw_bc[:], scalar1=1.0 - lam_init),
             reads=[parb], writes=[parb])
        P.dma("sp", qnw_bc[:], q_norm_w[l:l + 1, :].partition_broadcast(128), writes=[parb], cbuf=parb)
        P.dma("sp", kvnw_bc[:], kv_norm_w[l:l + 1, :].partition_broadcast(128), writes=[parb], cbuf=parb)
        for i, v in enumerate((lam_q1, lam_k1, lam_q2, lam_k2)):
            P.dma("sp", lamt[:, i, :], v[l:l + 1, :].partition_broadcast(128), writes=[parb], cbuf=parb)
        P.op("dve", lambda: nc.vector.tensor_tensor(out=lamt[:, 0, :], in0=lamt[:, 0, :], in1=lamt[:, 1, :], op=ALU.mult),
             reads=[parb], writes=[parb])
        P.op("dve", lambda: nc.vector.tensor_tensor(out=lamt[:, 2, :], in0=lamt[:, 2, :], in1=lamt[:, 3, :], op=ALU.mult),
             reads=[parb], writes=[parb])
        P.op("dve", lambda: nc.vector.reduce_sum(out=lams[:, 0:1], in_=lamt[:, 0, :], axis=AX.X), reads=[parb], writes=[parb])
        P.op("dve", lambda: nc.vector.reduce_sum(out=lams[:, 1:2], in_=lamt[:, 2, :], axis=AX.X), reads=[parb], writes=[parb])
        P.op("act", lambda: nc.scalar.activation(out=lams[:], in_=lams[:], func=AF.Exp), reads=[parb], writes=[parb])
        P.op("dve", lambda: nc.vector.tensor_tensor(out=neglam[:], in0=lams[:, 1:2], in1=lams[:, 0:1], op=ALU.subtract),
             reads=[parb], writes=[parb])
        P.op("dve", lambda: nc.vector.tensor_scalar_add(out=neglam[:], in0=neglam[:], scalar1=-lam_init),
             reads=[parb], writes=[parb])

    def transposes(items, evac, reads):
        i = 0
        while i < len(items):
            chunk = items[i:i + 8]
            ti = next_pt()

            def tr(chunk=chunk, ti=ti):
                for j, src in enumerate(chunk):
                    r, c = src.shape[0], src.shape[1]
                    ins = nc.tensor.transpose(PT[ti][0:c, j * 128:j * 128 + r], src, ident[0:r, 0:r])
                return ins
            P.op("pe", tr, reads=list(reads) + [identb], writes=[PTb[ti]])
            evac(ti, i, len(chunk))
            i += 8

    def rope_tm(eng, xv, tab, half, nr, nhd, tmpA, tmpB, bufs_r, bufs_w):
        eh = nc.vector if eng == "dve" else nc.gpsimd
        x1 = xv[:, :, 0:half]; x2 = xv[:, :, half:2 * half]
        cos = tab[:, 0:half].unsqueeze(1).to_broadcast([nr, nhd, half])
        sin = tab[:, half:2 * half].unsqueeze(1).to_broadcast([nr, nhd, half])
        a = tmpA[0:nr, 0:nhd, 0:half]; b = tmpB[0:nr, 0:nhd, 0:half]
        P.op(eng, lambda: eh.tensor_tensor(out=a, in0=x1, in1=sin, op=ALU.mult), reads=bufs_r, writes=bufs_w)
        P.op(eng, lambda: eh.tensor_tensor(out=b, in0=x2, in1=sin, op=ALU.mult), reads=bufs_r, writes=bufs_w)
        P.op(eng, lambda: eh.tensor_tensor(out=x1, in0=x1, in1=cos, op=ALU.mult), reads=bufs_r, writes=bufs_w)
        P.op(eng, lambda: eh.tensor_tensor(out=x2, in0=x2, in1=cos, op=ALU.mult), reads=bufs_r, writes=bufs_w)
        P.op(eng, lambda: eh.tensor_tensor(out=x1, in0=x1, in1=b, op=ALU.subtract), reads=bufs_r, writes=bufs_w)
        P.op(eng, lambda: eh.tensor_tensor(out=x2, in0=x2, in1=a, op=ALU.add), reads=bufs_r, writes=bufs_w)

    def rsqrt(out, in_, scale, eps, reads, writes):
        P.op("act", lambda: nc.scalar.activation(out=out, in_=in_, func=AF.Ln, scale=scale, bias=eps), reads=reads, writes=writes)
        P.op("act", lambda: nc.scalar.activation(out=out, in_=out, func=AF.Exp, scale=-0.5), reads=writes, writes=writes)

    outb = Buf("out")
    x1b = Buf("x1")
    NKT_MAX = 33

    def process_group(l, grp):
        G = grp["G"]; tiles = grp["tiles"]; segs = grp["segs"]; NT = len(tiles)
        xin = grp["xin"][l]; xout = grp["xout"][l]
        is_p = grp["prompt"]
        g0 = grp["g0"]
        rope_d = grp["rope_d"]; rope_m = grp["rope_m"]
        last_grp = (not is_p) or (g0 + G == T)

        def kbase_of(s):
            return g0 if s == 0 else PAST

        def fm_block(si, w, evac):
            for sub in range(w // 128):
                bi = next_pb()

                def mm(sub=sub, bi=bi):
                    for k in range(KT):
                        ins = nc.tensor.matmul(PB[bi][:, 0:G], lhsT=WS[si][:, k, sub * 128:(sub + 1) * 128], rhs=uT[:, k, 0:G],
                                               start=(k == 0), stop=(k == KT - 1))
                    return ins
                P.op("pe", mm, reads=[WSb[si], uTb], writes=[PBb[bi]])
                evac(bi, sub)

        def tm_block(si, w, evac):
            per_bank = 512 // w
            ti = 0
            while ti < NT:
                bi = next_pb()
                grp_t = list(range(ti, min(NT, ti + per_bank)))

                def mm(bi=bi, grp_t=grp_t):
                    for jj, t in enumerate(grp_t):
                        r0, nr, s = tiles[t]
                        for k in range(KT):
                            ins = nc.tensor.matmul(PB[bi][0:nr, jj * w:(jj + 1) * w], lhsT=uT[:, k, t * 128:t * 128 + nr],
                                                   rhs=WS[si][:, k, 0:w], start=(k == 0), stop=(k == KT - 1))
                    return ins
                P.op("pe", mm, reads=[WSb[si], uTb], writes=[PBb[bi]])
                for jj, t in enumerate(grp_t):
                    evac(bi, jj * w, t)
                ti += per_bank

        phase(2)
        with P.scope() as st:
            xt = sb("xt", [128, D], F32, st); xtb = Buf("xt")
            xb = sb("xb", [128, D], BF16, st); xbb = Buf("xb")
            for t, (r0, nr, s) in enumerate(tiles):
                P.dma("sp", xt[0:nr, :], xin[r0:r0 + nr, :], writes=[xtb], cbuf=xtb)
                P.op("dve", lambda nr=nr: nc.vector.tensor_copy(out=xb[0:nr, :], in_=xt[0:nr, :]), reads=[xtb], writes=[xbb])
                c0 = t * 128
                for k8 in range(KT // 8):
                    pti = next_pt()

                    def tr(k8=k8, pti=pti, nr=nr):
                        for j in range(8):
                            k = k8 * 8 + j
                            ins = nc.tensor.transpose(PT[pti][:, j * 128:j * 128 + nr], xb[0:nr, k * 128:(k + 1) * 128],
                                                      ident[0:nr, 0:nr])
                        return ins
                    P.op("pe", tr, reads=[xbb, identb], writes=[PTb[pti]])
                    for j in range(8):
                        k = k8 * 8 + j
                        if j % 2 == 0:
                            P.op("act", lambda k=k, j=j, s=s, pti=pti, c0=c0, nr=nr: nc.scalar.activation(
                                out=uT[:, k, c0:c0 + nr], in_=PT[pti][:, j * 128:j * 128 + nr], func=AF.Identity,
                                scale=sc1[:, s, k:k + 1], bias=sh[:, s, k:k + 1]), reads=[PTb[pti], parb], writes=[uTb])
                        else:
                            P.op("dve", lambda k=k, j=j, s=s, pti=pti, c0=c0, nr=nr: nc.vector.tensor_scalar(
                                out=uT[:, k, c0:c0 + nr], in0=PT[pti][:, j * 128:j * 128 + nr],
                                scalar1=sc1[:, s, k:k + 1], scalar2=sh[:, s, k:k + 1], op0=ALU.mult, op1=ALU.add),
                                reads=[PTb[pti], parb], writes=[uTb])

        phase(3)
        with P.scope() as st:
            NSEG = len(segs)
            SEGW = max(sn for (_, _, sn, _) in segs) + 4
            xbcT = sb("xbcT", [128, 24, NSEG * SEGW], BF16, st); xbcb = Buf("xbcT")
            sz = sb("sz", [128, 16, G], BF16, st); szb = Buf("sz")
            dt_tm = sb("dt_tm", [128, NT, NH], F32, st); dtb = Buf("dt")
            a_tm = sb("a_tm", [128, NT, NH], F32, st)
            c32 = sb("c32", [128, 24, NSEG * 4], F32, st); c32b = Buf("c32")
            tmp5 = sb("tmp5", [128, NT, NH], F32, st)
            CH = 128
            xcT = sb("xcT", [128, 24, CH], BF16, st); xcTb = Buf("xcT")
            ctmp = sb("ctmp", [128, 2, CH], F32, st); ctb = [Buf("ct0"), Buf("ct1")]
            ctmp2 = sb("ctmp2", [128, CH], F32, st); ct2b = Buf("ct2")
            xtok = sb("xtok", [128, 2048], BF16, st); xtokb = Buf("xtok")
            Btok = sb("Btok", [128, 4, 128], BF16, st); Btokb = Buf("Btok")
            xd = xtok; xdb = xtokb
            xdw = sb("xdw", [128, 2048], BF16, st); xdwb = Buf("xdw")
            acs = sb("acs", [128, NH], F32, st); acsb = Buf("acs")
            rhsb = sb("rhsb", [128, 8, CH], F32, st); rhsbb = Buf("rhsb")
            sg1 = sb("sg1", [128, 8, CH], F32, st); sg1b = Buf("sg1")
            Et = sb("Et", [128, 8, CH], F32, st); Etb = Buf("Et")
            MT = sb("MT", [128, 8, CH], BF16, st); MTb = Buf("MT")
            Cp = sb("Cp", [128, 8, CH], BF16, st); Cpb = Buf("Cp")
            te = sb("te", [128, 8], F32, st); teb = Buf("te")
            ygf = sb("ygf", [128, 4, CH], F32, st); ygb = Buf("ygf")
            sq = sb("sq", [128, 4, CH], BF16, st); sqb = Buf("sq")
            rstd = sb("rstd", [128, CH], F32, st); rstdb = Buf("rstd")
            stmp = sb("stmp", [128, 512], F32, st); stmpb = Buf("stmp")
            stio = sb("stio", [128, 16, 128], F32, st); stiob = Buf("stio")
            cst = [stio[0:3, i * 4:(i + 1) * 4, :].rearrange("p a b -> p (a b)") for i in range(2)]; cstb = [stiob, stiob]

            phase(3.01)
            si = next_w()
            phase(3.02)

            def ev_dt(bi, co, t):
                r0, nr, s = tiles[t]
                P.op("dve", lambda: nc.vector.tensor_tensor(out=dt_tm[0:nr, t, :], in0=PB[bi][0:nr, co:co + 32],
                                                            in1=dtb_bc[0:nr, :], op=ALU.add), reads=[PBb[bi], parb], writes=[dtb])
            tm_block(si, BW, ev_dt)
            si = next_w()

            def ev_kr(bi, co, t):
                r0, nr, s = tiles[t]
                P.op("act", lambda: nc.scalar.activation(out=kr_keep[0:nr, t, :], in_=PB[bi][0:nr, co:co + 64], func=AF.Identity),
                     reads=[PBb[bi]], writes=[krb])
            tm_block(si, BW, ev_kr)
            phase(3.05)
            nra = tiles[0][1]
            P.op("dve", lambda: nc.vector.tensor_scalar_mul(out=tmp5[0:nra], in0=dt_tm[0:nra], scalar1=-1.0), reads=[dtb], writes=[dtb])
            P.op("dve", lambda: nc.vector.tensor_tensor(out=tmp5[0:nra], in0=tmp5[0:nra], in1=dt_tm[0:nra], op=ALU.max), reads=[dtb], writes=[dtb])
            P.op("act", lambda: nc.scalar.activation(out=tmp5[0:nra], in_=tmp5[0:nra], func=AF.Exp, scale=-1.0),
                 reads=[dtb], writes=[dtb])
            P.op("dve", lambda: nc.vector.tensor_scalar_add(out=tmp5[0:nra], in0=tmp5[0:nra], scalar1=1.0), reads=[dtb], writes=[dtb])
            P.op("act", lambda: nc.scalar.activation(out=tmp5[0:nra], in_=tmp5[0:nra], func=AF.Ln),
                 reads=[dtb], writes=[dtb])
            P.op("dve", lambda: nc.vector.tensor_scalar_max(out=dt_tm[0:nra], in0=dt_tm[0:nra], scalar1=0.0),
                 reads=[dtb], writes=[dtb])
            P.op("dve", lambda: nc.vector.tensor_tensor(out=dt_tm[0:nra], in0=dt_tm[0:nra], in1=tmp5[0:nra], op=ALU.add),
                 reads=[dtb], writes=[dtb])
            P.op("dve", lambda: nc.vector.tensor_tensor(
                out=a_tm[0:nra], in0=dt_tm[0:nra], in1=A_bc[0:nra, :].unsqueeze(1).to_broadcast([nra, NT, NH]),
                op=ALU.mult), reads=[dtb, parb], writes=[dtb])

            phase(3.1)
            for sgi, (s, sc0, sn, stl) in enumerate(segs):
                base = sgi * SEGW
                if is_p:
                    if g0 == 0:
                        P.op("dve", lambda base=base: nc.vector.memset(xbcT[:, :, base:base + 4], 0.0), writes=[xbcb])
                    else:
                        P.op("dve", lambda base=base: nc.vector.tensor_copy(out=xbcT[:, :, base:base + 4], in_=carry[:]),
                             reads=[carryb], writes=[xbcb])
                else:
                    P.op("dve", lambda sgi=sgi: nc.vector.memset(c32[:, :, sgi * 4:sgi * 4 + 1], 0.0), writes=[c32b])
                    for j in range(3):
                        P.dma("sp", c32[:, :, sgi * 4 + 1 + j], st_conv[l, s - 1, j].rearrange("(t p) -> p t", p=128),
                              writes=[c32b], cbuf=c32b, nc_ok=True)
                    P.op("dve", lambda base=base, sgi=sgi: nc.vector.tensor_copy(out=xbcT[:, :, base:base + 4],
                                                                                in_=c32[:, :, sgi * 4:sgi * 4 + 4]),
                         reads=[c32b], writes=[xbcb])

            for i in range(12):
                si = next_w()

                def ev_xbc(bi, sub, i=i):
                    pt_ = i * 2 + sub
                    for sgi, (s, sc0, sn, stl) in enumerate(segs):
                        base = sgi * SEGW + 4
                        if pt_ % 2 == 0 and not _HOOK.get("dbg_noact"):
                            P.op("act", lambda: nc.scalar.activation(out=xbcT[:, pt_, base:base + sn], in_=PB[bi][:, sc0:sc0 + sn], func=AF.Identity),
                                 reads=[PBb[bi]], writes=[xbcb])
                        else:
                            P.op("dve", lambda: nc.vector.tensor_copy(out=xbcT[:, pt_, base:base + sn], in_=PB[bi][:, sc0:sc0 + sn]),
                                 reads=[PBb[bi]], writes=[xbcb])
                        if last_grp and not _HOOK.get("dbg_noact"):
                            P.op("dve", lambda: nc.vector.tensor_copy(out=c32[:, pt_, sgi * 4 + 1:sgi * 4 + 4],
                                                                      in_=PB[bi][:, sc0 + sn - 3:sc0 + sn]),
                                 reads=[PBb[bi]], writes=[c32b])
                fm_block(si, BW, ev_xbc)
            phase(3.12)
            for sgi, (s, sc0, sn, stl) in enumerate(segs):
                base = sgi * SEGW
                if is_p:
                    P.op("dve", lambda base=base, sn=sn: nc.vector.tensor_copy(out=carry[:], in_=xbcT[:, :, base + sn:base + sn + 4]),
                         reads=[xbcb], writes=[carryb])
                phase(3.15)
                if last_grp:
                    dst = conv_p[l] if is_p else conv_s[l, s - 1]
                    for c6 in range(6):
                        bi = next_pb()

                        def trc(bi=bi, c6=c6, sgi=sgi):
                            for j in range(4):
                                ins = nc.tensor.transpose(PB[bi][0:3, j * 128:(j + 1) * 128], c32[:, c6 * 4 + j, sgi * 4 + 1:sgi * 4 + 4], identf[:])
                            return ins
                        P.op("pe", trc, reads=[c32b, constb], writes=[PBb[bi]])
                        ci_ = c6 % 2
                        P.op("dve", lambda bi=bi, ci_=ci_: nc.vector.tensor_copy(out=cst[ci_], in_=PB[bi][0:3, :]), reads=[PBb[bi]], writes=[cstb[ci_]])
                        P.dma("pool", dst[:, c6 * 512:(c6 + 1) * 512], cst[ci_], reads=[cstb[ci_]], writes=[outb], cbuf=cstb[ci_])

            phase(3.2)
            for i in range(8):
                si = next_w()

                def ev_z(bi, sub, i=i):
                    P.op("act", lambda: nc.scalar.activation(out=sz[:, i * 2 + sub, 0:G], in_=PB[bi][:, 0:G], func=AF.Silu),
                         reads=[PBb[bi]], writes=[szb])
                fm_block(si, BW, ev_z)

            phase(3.3)
            for sgi, (s, sc0, sn, stl) in enumerate(segs):
                base = sgi * SEGW
                if not is_p:
                    P.dma("sp", stio[:], st_ssm[l, s - 1].rearrange("(t p) n -> p t n", p=128), writes=[stiob], cbuf=stiob)
                    for t4 in range(4):
                        bi = next_pb()

                        def tr(t4=t4, bi=bi):
                            for j in range(4):
                                ins = nc.tensor.transpose(PB[bi][:, j * 128:(j + 1) * 128], stio[:, t4 * 4 + j, :], identf[:])
                            return ins
                        P.op("pe", tr, reads=[stiob, constb], writes=[PBb[bi]])
                        P.op("dve", lambda t4=t4, bi=bi: nc.vector.tensor_copy(out=S32[:, t4 * 512:(t4 + 1) * 512], in_=PB[bi][:]),
                             reads=[PBb[bi]], writes=[Sb[t4]])
                        P.op("act", lambda t4=t4, bi=bi: nc.scalar.activation(out=Sbf[:, t4 * 512:(t4 + 1) * 512], in_=PB[bi][:], func=AF.Identity),
                             reads=[PBb[bi]], writes=[Sb[t4]])
                elif g0 == 0:
                    for t4 in range(4):
                        P.op("pool", lambda t4=t4: nc.gpsimd.memset(S32[:, t4 * 512:(t4 + 1) * 512], 0.0), writes=[Sb[t4]])
                        P.op("pool", lambda t4=t4: nc.gpsimd.memset(Sbf[:, t4 * 512:(t4 + 1) * 512], 0.0), writes=[Sb[t4]])

                for c, tt in enumerate(stl):
                    L = tiles[tt][1]
                    col0 = tt * 128
                    xc0 = base + 1 + c * CH
                    for pt_ in range(24):
                        ci = pt_ % 2
                        if True:
                            P.op("dve", lambda pt_=pt_, ci=ci: nc.vector.tensor_scalar(
                                out=ctmp[:, ci, 0:L], in0=xbcT[:, pt_, xc0:xc0 + L], scalar1=convw[:, pt_, 0:1],
                                scalar2=convb[:, pt_:pt_ + 1], op0=ALU.mult, op1=ALU.add), reads=[xbcb, parb], writes=[ctb[ci]])
                            for j in range(1, 4):
                                P.op("dve", lambda pt_=pt_, ci=ci, j=j: nc.vector.scalar_tensor_tensor(
                                    out=ctmp[:, ci, 0:L], in0=xbcT[:, pt_, xc0 + j:xc0 + j + L], scalar=convw[:, pt_, j:j + 1],
                                    in1=ctmp[:, ci, 0:L], op0=ALU.mult, op1=ALU.add), reads=[xbcb, parb], writes=[ctb[ci]])
                        else:
                            P.op("pool", lambda pt_=pt_, ci=ci: nc.gpsimd.tensor_scalar(
                                out=ctmp[:, ci, 0:L], in0=xbcT[:, pt_, xc0:xc0 + L], scalar1=convw[:, pt_, 0:1],
                                scalar2=convb[:, pt_:pt_ + 1], op0=ALU.mult, op1=ALU.add), reads=[xbcb, parb], writes=[ctb[ci]])
                            for j in range(1, 4):
                                P.op("pool", lambda pt_=pt_, j=j: nc.gpsimd.tensor_scalar(
                                    out=ctmp2[:, 0:L], in0=xbcT[:, pt_, xc0 + j:xc0 + j + L], scalar1=convw[:, pt_, j:j + 1],
                                    scalar2=None, op0=ALU.mult), reads=[xbcb, parb], writes=[ct2b])
                                P.op("pool", lambda ci=ci: nc.gpsimd.tensor_tensor(
                                    out=ctmp[:, ci, 0:L], in0=ctmp[:, ci, 0:L], in1=ctmp2[:, 0:L], op=ALU.add),
                                    reads=[ct2b], writes=[ctb[ci]])
                        P.op("act", lambda pt_=pt_, ci=ci: nc.scalar.activation(out=xcT[:, pt_, 0:L], in_=ctmp[:, ci, 0:L], func=AF.Silu),
                             reads=[ctb[ci]], writes=[xcTb])

                    phase(3.4)
                    def ev_x(ti_, i0, n):
                        P.op("act", lambda: nc.scalar.activation(out=xtok[0:L, i0 * 128:(i0 + n) * 128], in_=PT[ti_][0:L, 0:n * 128], func=AF.Identity),
                             reads=[PTb[ti_]], writes=[xtokb])
                    transposes([xcT[:, j, 0:L] for j in range(16)], ev_x, [xcTb])

                    def ev_b(ti_, i0, n):
                        P.op("dve", lambda: nc.vector.tensor_copy(out=Btok[0:L, :, :].rearrange("p a b -> p (a b)"), in_=PT[ti_][0:L, 0:512]),
                             reads=[PTb[ti_]], writes=[Btokb])
                    transposes([xcT[:, 16 + j, 0:L] for j in range(4)], ev_b, [xcTb])

                    P.op("dve", lambda: nc.vector.tensor_tensor(
                        out=xd[0:L, :].rearrange("p (h e) -> p h e", e=HP), in0=xtok[0:L, :].rearrange("p (h e) -> p h e", e=HP),
                        in1=dt_tm[0:L, tt, :].unsqueeze(2).to_broadcast([L, NH, HP]), op=ALU.mult),
                        reads=[xtokb, dtb], writes=[xdb])
                    bi = next_pb()
                    P.op("pe", lambda bi=bi: nc.tensor.matmul(PB[bi][0:L, 0:NH], lhsT=tri[0:L, 0:L], rhs=a_tm[0:L, tt, :], start=True, stop=True),
                         reads=[dtb, constb], writes=[PBb[bi]])
                    P.op("dve", lambda bi=bi: nc.vector.tensor_copy(out=acs[0:L, :], in_=PB[bi][0:L, 0:NH]),
                         reads=[PBb[bi]], writes=[acsb])

                    phase(3.5)
                    for gi in range(4):
                        hs = slice(gi * 8, gi * 8 + 8)
                        P.op("pool", lambda hs=hs: nc.gpsimd.tensor_tensor(
                            out=rhsb[0:L, :, 0:L], in0=a_tm[0:L, tt, hs].unsqueeze(2).to_broadcast([L, 8, L]),
                            in1=tri[0:L, 0:L].unsqueeze(1).to_broadcast([L, 8, L]), op=ALU.mult),
                            reads=[dtb, constb], writes=[rhsbb])
                        b0 = 2 * (gi % 2); b1 = b0 + 1

                        def bcmm(b0=b0, b1=b1):
                            for half, bb in enumerate((b0, b1)):
                                ins = nc.tensor.matmul(PB[bb][:, 0:4 * L], lhsT=onesf[0:L, :],
                                                       rhs=rhsb[0:L, half * 4:half * 4 + 4, 0:L], start=True, stop=True)
                            return ins
                        P.op("pe", bcmm, reads=[rhsbb, constb], writes=[PBb[b0], PBb[b1]])
                        for half, bb in enumerate((b0, b1)):
                            h4 = slice(half * 4, half * 4 + 4)
                            bcv = PB[bb][:, 0:4 * L].rearrange("p (h l) -> p h l", l=L)
                            P.op("dve", lambda bcv=bcv, h4=h4: nc.vector.tensor_tensor(
                                out=sg1[0:L, h4, 0:L], in0=bcv[0:L], in1=negmask[0:L, 0:L].unsqueeze(1).to_broadcast([L, 4, L]),
                                op=ALU.add), reads=[PBb[bb], constb], writes=[sg1b])
                            P.op("act", lambda bcv=bcv, h4=h4: nc.scalar.activation(out=Et[:, h4, 0:L], in_=bcv, func=AF.Exp),
                                 reads=[PBb[bb]], writes=[Etb])
                            P.op("dve", lambda bcv=bcv, h4=h4, half=half, gi=gi: nc.vector.tensor_tensor(
                                out=te[0:L, h4], in0=bcv[0:L, :, L - 1], in1=acs[0:L, gi * 8 + half * 4:gi * 8 + half * 4 + 4],
                                op=ALU.subtract), reads=[PBb[bb], acsb], writes=[teb])
                        P.op("pool", lambda gi=gi: nc.gpsimd.tensor_tensor(
                            out=sg1[0:L, :, 0:L], in0=sg1[0:L, :, 0:L],
                            in1=acs[0:L, gi * 8:gi * 8 + 8].unsqueeze(2).to_broadcast([L, 8, L]), op=ALU.subtract),
                            reads=[sg1b, acsb], writes=[sg1b])
                        P.op("act", lambda: nc.scalar.activation(out=sg1[0:L, :, 0:L], in_=sg1[0:L, :, 0:L], func=AF.Exp),
                             reads=[sg1b], writes=[sg1b])
                        P.op("act", lambda: nc.scalar.activation(out=te[0:L, :], in_=te[0:L, :], func=AF.Exp),
                             reads=[teb], writes=[teb])
                        phase(3.6)
                        bc_ = next_pb(4, 6)
                        P.op("pe", lambda bc_=bc_, gi=gi: nc.tensor.matmul(PB[bc_][0:L, 0:L], lhsT=xcT[:, 16 + gi, 0:L],
                                                                         rhs=xcT[:, 20 + gi, 0:L], start=True, stop=True),
                             reads=[xcTb], writes=[PBb[bc_]])
                        P.op("dve", lambda bc_=bc_: nc.vector.tensor_tensor(
                            out=MT[0:L, :, 0:L], in0=sg1[0:L, :, 0:L], in1=PB[bc_][0:L, 0:L].unsqueeze(1).to_broadcast([L, 8, L]),
                            op=ALU.mult), reads=[sg1b, PBb[bc_]], writes=[MTb])
                        P.op("pool", lambda gi=gi: nc.gpsimd.tensor_tensor(
                            out=Cp[:, :, 0:L], in0=Et[:, :, 0:L], in1=xcT[:, 20 + gi, 0:L].unsqueeze(1).to_broadcast([128, 8, L]),
                            op=ALU.mult), reads=[Etb, xcTb], writes=[Cpb])
                        phase(3.7)
                        by = next_pb(4, 6)

                        def ymm(by=by, gi=gi):
                            for jj in range(4):
                                for hh in range(2):
                                    hl = jj * 2 + hh
                                    h = gi * 8 + hl
                                    o = PB[by][hh * 64:(hh + 1) * 64, jj * 128:jj * 128 + L]
                                    nc.tensor.matmul(o, lhsT=xd[0:L, h * HP:(h + 1) * HP], rhs=MT[0:L, hl, 0:L], start=True, stop=False)
                                    ins = nc.tensor.matmul(o, lhsT=Sbf[:, h * HP:(h + 1) * HP], rhs=Cp[:, hl, 0:L], start=False, stop=True)
                            return ins
                        P.op("pe", ymm, reads=[xdb, MTb, Sb[gi], Cpb], writes=[PBb[by]])
                        for jj in range(4):
                            j = gi * 4 + jj
                            P.op("dve", lambda jj=jj, j=j, by=by: nc.vector.scalar_tensor_tensor(
                                out=ygf[:, jj, 0:L], in0=xcT[:, j, 0:L], scalar=Dcol[:, j:j + 1], in1=PB[by][:, jj * 128:jj * 128 + L],
                                op0=ALU.mult, op1=ALU.add), reads=[xcTb, parb, PBb[by]], writes=[ygb])
                            P.op("pool", lambda jj=jj, j=j: nc.gpsimd.tensor_tensor(
                                out=ygf[:, jj, 0:L], in0=ygf[:, jj, 0:L], in1=sz[:, j, col0:col0 + L], op=ALU.mult),
                                reads=[ygb, szb], writes=[ygb])
                        P.op("act", lambda: nc.scalar.activation(out=sq[:, :, 0:L], in_=ygf[:, :, 0:L], func=AF.Square),
                             reads=[ygb], writes=[sqb])
                        phase(3.8)
                        bm = next_pb(4, 6)

                        def msmm(bm=bm):
                            for jj in range(4):
                                ins = nc.tensor.matmul(PB[bm][:, 0:L], lhsT=ones512[:], rhs=sq[:, jj, 0:L], start=(jj == 0), stop=(jj == 3))
                            return ins
                        P.op("pe", msmm, reads=[sqb, constb], writes=[PBb[bm]])
                        rsqrt(rstd[:, 0:L], PB[bm][:, 0:L], 1.0, 1e-6, [PBb[bm]], [rstdb])
                        for jj in range(4):
                            j = gi * 4 + jj
                            P.op("dve", lambda jj=jj, j=j: nc.vector.scalar_tensor_tensor(
                                out=mixT[:, j, col0:col0 + L], in0=ygf[:, jj, 0:L], scalar=normw[:, j:j + 1], in1=rstd[:, 0:L],
                                op0=ALU.mult, op1=ALU.mult), reads=[ygb, rstdb, parb], writes=[mixTb[0]])
                        phase(3.85)
                        gc = slice(gi * 512, (gi + 1) * 512)
                        P.op("pool", lambda gc=gc: nc.gpsimd.tensor_tensor(
                            out=xdw[0:L, gc].rearrange("p (h e) -> p h e", e=HP), in0=xd[0:L, gc].rearrange("p (h e) -> p h e", e=HP),
                            in1=te[0:L, :].unsqueeze(2).to_broadcast([L, 8, HP]), op=ALU.mult), reads=[xdb, teb], writes=[xdwb])
                        bs = next_pb(4, 6)
                        P.op("pe", lambda bs=bs, gi=gi, gc=gc: nc.tensor.matmul(PB[bs][:, :], lhsT=Btok[0:L, gi, :], rhs=xdw[0:L, gc],
                                                                              start=True, stop=True),
                             reads=[Btokb, xdwb], writes=[PBb[bs]])
                        P.op("dve", lambda gc=gc: nc.vector.tensor_tensor(
                            out=stmp[:, :].rearrange("p (h e) -> p h e", e=HP), in0=S32[:, gc].rearrange("p (h e) -> p h e", e=HP),
                            in1=Et[:, :, L - 1].unsqueeze(2).to_broadcast([128, 8, HP]), op=ALU.mult),
                            reads=[Sb[gi], Etb], writes=[stmpb])
                        P.op("dve", lambda gc=gc, bs=bs: nc.vector.tensor_tensor(out=S32[:, gc], in0=stmp[:, :], in1=PB[bs][:, :], op=ALU.add),
                             reads=[stmpb, PBb[bs]], writes=[Sb[gi]])
                        P.op("act", lambda gc=gc: nc.scalar.activation(out=Sbf[:, gc], in_=S32[:, gc], func=AF.Identity), reads=[Sb[gi]], writes=[Sb[gi]])

                phase(3.9)
                if last_grp:
                    for t4 in range(4):
                        bi = next_pb()

                        def tr(t4=t4, bi=bi):
                            for j in range(4):
                                ins = nc.tensor.transpose(PB[bi][:, j * 128:(j + 1) * 128], S32[:, (t4 * 4 + j) * 128:(t4 * 4 + j + 1) * 128], identf[:])
                            return ins
                        P.op("pe", tr, reads=[Sb[t4], constb], writes=[PBb[bi]])
                        P.op("dve", lambda t4=t4, bi=bi: nc.vector.tensor_copy(
                            out=stio[:, t4 * 4:(t4 + 1) * 4, :].rearrange("p a b -> p (a b)"), in_=PB[bi][:]),
                            reads=[PBb[bi]], writes=[stiob])
                    dst = ssm_p[l] if is_p else ssm_s[l, s - 1]
                    P.dma("pool", dst.rearrange("(t p) n -> p t n", p=128), stio[:], reads=[stiob], writes=[outb], cbuf=stiob)

        if not is_p:
            with P.scope() as st:
                kt32 = sb("kt32", [128, 1024], F32, st); kt32b = Buf("kt32")
                ktbf = sb("ktbf", [128, 1024], BF16, st); ktbfb = Buf("ktbf")
                kTt = sb("kTt", [128, 8, 128], BF16, st); kTtb = Buf("kTt")
                v32 = sb("v32", [128, 1024], F32, st); v32b = Buf("v32")
                vbf = sb("vbf", [128, 1024], BF16, st); vbfb = Buf("vbf")
                l32 = sb("l32", [128, 320], F32, st); l32b = Buf("l32")
                lbf = sb("lbf", [128, 320], BF16, st); lbfb = Buf("lbf")
                lTt = sb("lTt", [128, 3, 128], BF16, st); lTtb = Buf("lTt")
                for (s, sc0, sn, stl) in segs:
                    sl = s - 1
                    for j in range(PAST // 128):
                        rs = slice(j * 128, (j + 1) * 128)
                        P.dma("sp", kt32[:], ck[l, sl, rs, :], writes=[kt32b], cbuf=kt32b)
                        P.op("dve", lambda: nc.vector.tensor_copy(out=ktbf[:], in_=kt32[:]), reads=[kt32b], writes=[ktbfb])

                        def ev_k(ti_, i0, n):
                            P.op("act", lambda: nc.scalar.activation(out=kTt[:, i0:i0 + n, :].rearrange("p a b -> p (a b)"), in_=PT[ti_][:, 0:n * 128], func=AF.Identity),
                                 reads=[PTb[ti_]], writes=[kTtb])
                        transposes([ktbf[:, h * 128:(h + 1) * 128] for h in range(8)], ev_k, [ktbfb])
                        P.dma("pool", KT_scr[s][:, :, rs].rearrange("h p t -> p h t"), kTt[:], reads=[kTtb], writes=[scrB[s]["KT"]], cbuf=kTtb)
                        P.dma("sp", v32[:], cv[l, sl, rs, :], writes=[v32b], cbuf=v32b)
                        P.op("pool", lambda: nc.gpsimd.tensor_copy(out=vbf[:], in_=v32[:]), reads=[v32b], writes=[vbfb])
                        P.dma("pool", V_scr[s][rs, :], vbf[:], reads=[vbfb], writes=[scrB[s]["V"]], cbuf=vbfb)
                        P.dma("sp", l32[:, 0:256], cl[l, sl, rs, :], writes=[l32b], cbuf=l32b)
                        P.dma("sp", l32[:, 256:320], cr[l, sl, rs, :], writes=[l32b], cbuf=l32b)
                        P.op("dve", lambda: nc.vector.tensor_copy(out=lbf[:], in_=l32[:]), reads=[l32b], writes=[lbfb])

                        def ev_l(ti_, i0, n):
                            P.op("act", lambda: nc.scalar.activation(out=lTt[:, :, :].rearrange("p a b -> p (a b)"), in_=PT[ti_][:, 0:384], func=AF.Identity),
                                 reads=[PTb[ti_]], writes=[lTtb])
                        transposes([lbf[:, 0:128], lbf[:, 128:256], lbf[:, 256:320]], ev_l, [lbfb])
                        P.dma("pool", LT_scr[s][:, :, rs].rearrange("a p t -> p a t"), lTt[:, 0:2, :], reads=[lTtb], writes=[scrB[s]["LT"]], cbuf=lTtb)
                        P.dma("pool", RT_scr[s][:, rs], lTt[0:64, 2, :], reads=[lTtb], writes=[scrB[s]["RT"]], cbuf=lTtb)
                        P.dma("pool", L_scr[s][rs, :], lbf[:, 0:256], reads=[lbfb], writes=[scrB[s]["L"]], cbuf=lbfb)

        def seg_qtiles(s, sc0, sn, stl):
            kb = kbase_of(s)
            return [(tt * 128, tiles[tt][1], (kb + c * 128) // 128) for c, tt in enumerate(stl)]

        phase(4)
        with P.scope() as st:
            QT = sb("QT", [128, 8, G], BF16, st); QTb = Buf("QT")
            gT = sb("gT", [128, 8, G], BF16, st); gTb = Buf("gT")
            with P.scope() as st1:
                kv32 = sb("kv32", [128, NT, 1024], F32, st1); kv32b = Buf("kv32")
                qkb = sb("qkb", [128, NT, 1024], BF16, st1); qkbb = Buf("qkb")
                rtab = sb("rtab", [128, NT, 16], F32, st1); rtabb = Buf("rtab")
                rtA = sb("rtA", [128, 16, 8], F32, st1); rtB = sb("rtB", [128, 16, 8], F32, st1)
                kTt = sb("kTt2", [128, 8, 128], BF16, st1); kTtb = Buf("kTt2")
                for t, (r0, nr, s) in enumerate(tiles):
                    P.dma("sp", rtab[0:nr, t, :], rope_d[r0:r0 + nr, :], writes=[rtabb], cbuf=rtabb)

                def ev_kv(i):
                    def ev(bi, co, t):
                        r0, nr, s = tiles[t]
                        P.op("act", lambda: nc.scalar.activation(out=kv32[0:nr, t, i * BW:(i + 1) * BW], in_=PB[bi][0:nr, co:co + BW], func=AF.Identity),
                             reads=[PBb[bi]], writes=[kv32b])
                    return ev
                for i in range(4):
                    si = next_w()
                    tm_block(si, BW, ev_kv(i))
                for t, (r0, nr, s) in enumerate(tiles):
                    rope_tm("dve", kv32[0:nr, t, :].rearrange("p (h e) -> p h e", e=64), rtab[0:nr, t, :], 8, nr, 16, rtA, rtB,
                            [rtabb, kv32b], [kv32b])
                    P.op("pool", lambda t=t, nr=nr: nc.gpsimd.tensor_copy(out=qkb[0:nr, t, :], in_=kv32[0:nr, t, :]), reads=[kv32b], writes=[qkbb])

                    def ev_qt(ti_, i0, n, t=t, nr=nr):
                        P.op("act", lambda: nc.scalar.activation(out=QT[:, i0:i0 + n, t * 128:t * 128 + nr],
                                                           in_=PT[ti_][:, 0:n * 128].rearrange("p (a b) -> p a b", b=128)[:, :, 0:nr], func=AF.Identity),
                             reads=[PTb[ti_]], writes=[QTb])
                    transposes([qkb[0:nr, t, h * 128:(h + 1) * 128] for h in range(8)], ev_qt, [qkbb])
                for i in range(4):
                    si = next_w()
                    tm_block(si, BW, ev_kv(i))
                dk_out = dk_p if is_p else dk_s
                for t, (r0, nr, s) in enumerate(tiles):
                    rope_tm("dve", kv32[0:nr, t, :].rearrange("p (h e) -> p h e", e=64), rtab[0:nr, t, :], 8, nr, 16, rtA, rtB,
                            [rtabb, kv32b], [kv32b])
                    P.dma("pool", dk_out[l, r0:r0 + nr, :], kv32[0:nr, t, :], reads=[kv32b], writes=[outb], cbuf=kv32b)
                    P.op("pool", lambda t=t, nr=nr: nc.gpsimd.tensor_copy(out=qkb[0:nr, t, :], in_=kv32[0:nr, t, :]), reads=[kv32b], writes=[qkbb])
                    kpos = kbase_of(s) + (r0 - g0 if is_p else 0)

                    def ev_kt(ti_, i0, n, nr=nr):
                        P.op("act", lambda: nc.scalar.activation(out=kTt[:, i0:i0 + n, 0:nr],
                                                           in_=PT[ti_][:, 0:n * 128].rearrange("p (a b) -> p a b", b=128)[:, :, 0:nr], func=AF.Identity),
                             reads=[PTb[ti_]], writes=[kTtb])
                    transposes([qkb[0:nr, t, h * 128:(h + 1) * 128] for h in range(8)], ev_kt, [qkbb])
                    P.dma("pool", KT_scr[s][:, :, kpos:kpos + nr].rearrange("h p t -> p h t"), kTt[:, :, 0:nr],
                          reads=[kTtb], writes=[scrB[s]["KT"]], cbuf=kTtb)
                for i in range(4):
                    si = next_w()
                    tm_block(si, BW, ev_kv(i))
                dv_out = dv_p if is_p else dv_s
                for t, (r0, nr, s) in enumerate(tiles):
                    P.dma("pool", dv_out[l, r0:r0 + nr, :], kv32[0:nr, t, :], reads=[kv32b], writes=[outb], cbuf=kv32b)
                    P.op("pool", lambda t=t, nr=nr: nc.gpsimd.tensor_copy(out=qkb[0:nr, t, :], in_=kv32[0:nr, t, :]), reads=[kv32b], writes=[qkbb])
                    kpos = kbase_of(s) + (r0 - g0 if is_p else 0)
                    P.dma("pool", V_scr[s][kpos:kpos + nr, :], qkb[0:nr, t, :], reads=[qkbb], writes=[scrB[s]["V"]], cbuf=qkbb)
                for i in range(4):
                    si = next_w()

                    def ev_g(bi, sub, i=i):
                        P.op("act", lambda: nc.scalar.activation(out=gT[:, i * 2 + sub, 0:G], in_=PB[bi][:, 0:G], func=AF.Silu),
                             reads=[PBb[bi]], writes=[gTb])
                    fm_block(si, BW, ev_g)

            phase(5)
            with P.scope() as st1:
                KTs = [sb(f"KTs{i}", [128, NKT_MAX * 128], BF16, st1) for i in range(2)]
                KTsb = [Buf(f"KTs{i}") for i in range(2)]
                Vs = [sb(f"Vs{i}", [128, NKT_MAX, 130], BF16, st1) for i in range(2)]
                Vsb = [Buf(f"Vs{i}") for i in range(2)]
                PTt = [sb(f"PTt{i}", [128, 512], BF16, st1) for i in range(3)]
                PTtb = [Buf(f"PTt{i}") for i in range(3)]
                osb = sb("osb", [128, 128], F32, st1); osbb = Buf("osb")
                otmp = sb("otmp", [128, 128], F32, st1)
                onb = sb("onb", [128, 128], BF16, st1); onbb = Buf("onb")
                rr = sb("rr", [128, 4], F32, st1); rrb = Buf("rr")
                junk = sb("junk", [128, 128], BF16, st1)
                for i in range(2):
                    P.op("dve", lambda i=i: nc.vector.memset(Vs[i][:, :, 128:129], 1.0), writes=[Vsb[i]])
                ptt_rr = 0
                hcount = 0
                for (s, sc0, sn, stl) in segs:
                    kbase = kbase_of(s)
                    nkeys = kbase + sn
                    nkt = (nkeys + 127) // 128
                    qts = seg_qtiles(s, sc0, sn, stl)
                    for h in range(8):
                        slot = hcount % 2
                        hcount += 1
                        P.dma("sp", KTs[slot][:, 0:nkeys], KT_scr[s][h, :, 0:nkeys], reads=[scrB[s]["KT"]], writes=[KTsb[slot]], cbuf=KTsb[slot])
                        nfull = nkeys // 128
                        if nfull:
                            P.dma("sp", Vs[slot][:, 0:nfull, 0:128],
                                  V_scr[s][0:nfull * 128, h * 128:(h + 1) * 128].rearrange("(j p) e -> p j e", p=128),
                                  reads=[scrB[s]["V"]], writes=[Vsb[slot]], cbuf=Vsb[slot])
                        if nkeys % 128:
                            rem = nkeys % 128
                            P.dma("sp", Vs[slot][0:rem, nfull, 0:128], V_scr[s][nfull * 128:nkeys, h * 128:(h + 1) * 128],
                                  reads=[scrB[s]["V"]], writes=[Vsb[slot]], cbuf=Vsb[slot])
                        for m in range(2):
                            ob = (2 + 2 * m, 3 + 2 * m)
                            for j in range(nkt):
                                nk = min(128, nkeys - j * 128)
                                vis = [qi for qi, (qc, nq, qg) in enumerate(qts) if (not is_p) or qg >= j]
                                if not vis:
                                    continue
                                qlo = qts[vis[0]][0]; qhi = qts[vis[-1]][0] + qts[vis[-1]][1]
                                sbk = next_pb(0, 2)
                                P.op("pe", lambda sbk=sbk, j=j, nk=nk, qlo=qlo, qhi=qhi, m=m, h=h, slot=slot: nc.tensor.matmul(
                                    PB[sbk][0:nk, 0:qhi - qlo], lhsT=KTs[slot][m * 64:(m + 1) * 64, j * 128:j * 128 + nk],
                                    rhs=QT[m * 64:(m + 1) * 64, h, qlo:qhi], start=True, stop=True),
                                    reads=[KTsb[slot], QTb], writes=[PBb[sbk]])
                                pi = ptt_rr % 3
                                ptt_rr += 1
                                P.op("act", lambda sbk=sbk, pi=pi, nk=nk, qlo=qlo, qhi=qhi: nc.scalar.activation(
                                    out=PTt[pi][0:nk, 0:qhi - qlo], in_=PB[sbk][0:nk, 0:qhi - qlo], func=AF.Exp, scale=DIFF_SCALE),
                                    reads=[PBb[sbk]], writes=[PTtb[pi]])
                                if is_p and qts[vis[0]][2] == j:
                                    P.op("pool", lambda pi=pi: nc.gpsimd.memset(PTt[pi][64:128, 0:64], 0.0), writes=[PTtb[pi]])

                                def avmm(pi=pi, j=j, nk=nk, vis=vis, qlo=qlo, ob=ob, slot=slot):
                                    for qi in vis:
                                        qc, nq, qg = qts[qi]
                                        o = PB[ob[qi // 2]][0:nq, (qi % 2) * 129:(qi % 2) * 129 + 129]
                                        last_j = (qg if is_p else nkt - 1)
                                        ins = nc.tensor.matmul(o, lhsT=PTt[pi][0:nk, qc - qlo:qc - qlo + nq], rhs=Vs[slot][0:nk, j, 0:129],
                                                               start=(j == 0 and qi % 2 == 0), stop=(j == last_j), skip_group_check=True)
                                    return ins
                                P.op("pe", avmm, reads=[PTtb[pi], Vsb[slot]], writes=[PBb[ob[0]], PBb[ob[1]]])
                        for qi, (qc, nq, qg) in enumerate(qts):
                            o0 = PB[2 + qi // 2][0:nq, (qi % 2) * 129:(qi % 2) * 129 + 129]
                            o1 = PB[4 + qi // 2][0:nq, (qi % 2) * 129:(qi % 2) * 129 + 129]
                            bufs = [PBb[2 + qi // 2], PBb[4 + qi // 2]]
                            P.op("dve", lambda o0=o0, nq=nq: nc.vector.reciprocal(out=rr[0:nq, 0:1], in_=o0[:, 128:129]), reads=bufs, writes=[rrb])
                            P.op("dve", lambda o1=o1, nq=nq: nc.vector.reciprocal(out=rr[0:nq, 1:2], in_=o1[:, 128:129]), reads=bufs, writes=[rrb])
                            P.op("dve", lambda nq=nq: nc.vector.tensor_tensor(out=rr[0:nq, 1:2], in0=rr[0:nq, 1:2], in1=neglam[0:nq, :], op=ALU.mult),
                                 reads=[rrb, parb], writes=[rrb])
                            P.op("dve", lambda o1=o1, nq=nq: nc.vector.tensor_scalar_mul(out=otmp[0:nq, :], in0=o1[:, 0:128], scalar1=rr[0:nq, 1:2]),
                                 reads=bufs + [rrb], writes=[osbb])
                            P.op("dve", lambda o0=o0, nq=nq: nc.vector.scalar_tensor_tensor(
                                out=osb[0:nq, :], in0=o0[:, 0:128], scalar=rr[0:nq, 0:1], in1=otmp[0:nq, :], op0=ALU.mult, op1=ALU.add),
                                reads=bufs + [rrb, osbb], writes=[osbb])
                            P.op("act", lambda nq=nq: nc.scalar.activation(out=junk[0:nq, :], in_=osb[0:nq, :], func=AF.Square,
                                                                          accum_out=rr[0:nq, 2:3]), reads=[osbb], writes=[rrb])
                            rsqrt(rr[0:nq, 3:4], rr[0:nq, 2:3], 1.0 / 128.0, 1e-6, [rrb], [rrb])
                            P.op("dve", lambda nq=nq: nc.vector.scalar_tensor_tensor(
                                out=onb[0:nq, :], in0=osb[0:nq, :], scalar=rr[0:nq, 3:4], in1=dnw_bc[0:nq, :], op0=ALU.mult, op1=ALU.mult),
                                reads=[osbb, rrb, parb], writes=[onbb])

                            def ev_o(ti_, i0, n, qc=qc, nq=nq, h=h):
                                P.op("dve", lambda: nc.vector.tensor_tensor(out=mixT[:, 16 + h, qc:qc + nq], in0=PT[ti_][:, 0:nq],
                                                                            in1=gT[:, h, qc:qc + nq], op=ALU.mult),
                                     reads=[PTb[ti_], gTb], writes=[mixTb[1]])
                            transposes([onb[0:nq, :]], ev_o, [onbb])

        phase(6)
        with P.scope() as st:
            qlT = sb("qlT", [128, 8, 2, G], BF16, st); qlTb = Buf("qlT")
            qrT = sb("qrT", [64, 8, G], BF16, st); qrTb = Buf("qrT")
            mgT = sb("mgT", [128, 8, G], BF16, st); mgTb = Buf("mgT")
            with P.scope() as st1:
                wuq = sb("wuq", [128, 6, 1536], BF16, st1); wukT = sb("wukT", [128, 8, 256], BF16, st1)
                wb = Buf("wuq")
                P.dma("sp", wuq[:], w_uq_bf[l], reads=[wcast[l]], writes=[wb], cbuf=wb)
                P.dma("sp", wukT[:], w_ukT_bf[l], reads=[wcast[l]], writes=[wb], cbuf=wb)
                cqb = sb("cqb", [128, NT, 768], BF16, st1); cqbb = Buf("cqb")
                cqT = sb("cqT", [128, 6, G], BF16, st1); cqTb = Buf("cqT")
                if not is_p:
                    P.op("pool", lambda: nc.gpsimd.memset(cqT[:], 0.0), writes=[cqTb])
                l32 = sb("l32m", [128, NT, 256], F32, st1); l32b = Buf("l32m")
                lkb = sb("lkb", [128, NT, 320], BF16, st1); lkbb = Buf("lkb")
                lTt = sb("lTt2", [128, 3, 128], BF16, st1); lTtb = Buf("lTt2")
                rtm = sb("rtm", [128, NT, 64], F32, st1); rtmb = Buf("rtm")
                rA = sb("rA", [128, 8, 32], F32, st1); rB = sb("rB", [128, 8, 32], F32, st1)
                ssq = sb("ssq", [128, NT, 8], F32, st1); ssqb = Buf("ssq")
                junk2 = sb("junk2", [128, 256], BF16, st1)
                qr32 = sb("qr32", [128, 512], F32, st1); qr32b = Buf("qr32")
                qrb = sb("qrb", [128, 512], BF16, st1); qrbb = Buf("qrb")
                qnT = sb("qnT", [128, G], BF16, st1); qnTb = Buf("qnT")
                for t, (r0, nr, s) in enumerate(tiles):
                    P.dma("sp", rtm[0:nr, t, :], rope_m[r0:r0 + nr, :], writes=[rtmb], cbuf=rtmb)
                for i in range(3):
                    si = next_w()

                    def ev_cq(bi, co, t, i=i):
                        r0, nr, s = tiles[t]
                        P.op("act", lambda: nc.scalar.activation(out=junk2[0:nr, :], in_=PB[bi][0:nr, co:co + BW], func=AF.Square,
                                                                 accum_out=ssq[0:nr, t, i:i + 1]), reads=[PBb[bi]], writes=[ssqb])
                        P.op("dve", lambda: nc.vector.tensor_copy(out=cqb[0:nr, t, i * BW:(i + 1) * BW], in_=PB[bi][0:nr, co:co + BW]),
                             reads=[PBb[bi]], writes=[cqbb])
                    tm_block(si, BW, ev_cq)
                si = next_w()

                def ev_ckv(bi, co, t):
                    r0, nr, s = tiles[t]
                    P.op("act", lambda: nc.scalar.activation(out=l32[0:nr, t, :], in_=PB[bi][0:nr, co:co + BW], func=AF.Identity), reads=[PBb[bi]], writes=[l32b])
                tm_block(si, BW, ev_ckv)
                lat_out = lat_p if is_p else lat_s
                kr_out = kr_p if is_p else kr_s
                for t, (r0, nr, s) in enumerate(tiles):
                    P.op("dve", lambda t=t, nr=nr: nc.vector.reduce_sum(out=ssq[0:nr, t, 3:4], in_=ssq[0:nr, t, 0:3], axis=AX.X),
                         reads=[ssqb], writes=[ssqb])
                    rsqrt(ssq[0:nr, t, 4:5], ssq[0:nr, t, 3:4], 1.0 / 768.0, 1e-6, [ssqb], [ssqb])
                    P.op("dve", lambda t=t, nr=nr: nc.vector.scalar_tensor_tensor(
                        out=cqb[0:nr, t, :], in0=cqb[0:nr, t, :], scalar=ssq[0:nr, t, 4:5], in1=qnw_bc[0:nr, :], op0=ALU.mult, op1=ALU.mult),
                        reads=[cqbb, ssqb, parb], writes=[cqbb])

                    def ev_cqT(ti_, i0, n, t=t, nr=nr):
                        P.op("act", lambda: nc.scalar.activation(out=cqT[:, i0:i0 + n, t * 128:t * 128 + nr],
                                                           in_=PT[ti_][:, 0:n * 128].rearrange("p (a b) -> p a b", b=128)[:, :, 0:nr], func=AF.Identity),
                             reads=[PTb[ti_]], writes=[cqTb])
                    transposes([cqb[0:nr, t, k * 128:(k + 1) * 128] for k in range(6)], ev_cqT, [cqbb])
                    P.op("act", lambda t=t, nr=nr: nc.scalar.activation(out=junk2[0:nr, 0:256], in_=l32[0:nr, t, :], func=AF.Square,
                                                                       accum_out=ssq[0:nr, t, 5:6]), reads=[l32b], writes=[ssqb])
                    rsqrt(ssq[0:nr, t, 6:7], ssq[0:nr, t, 5:6], 1.0 / 256.0, 1e-6, [ssqb], [ssqb])
                    P.op("dve", lambda t=t, nr=nr: nc.vector.scalar_tensor_tensor(
                        out=l32[0:nr, t, :], in0=l32[0:nr, t, :], scalar=ssq[0:nr, t, 6:7], in1=kvnw_bc[0:nr, :], op0=ALU.mult, op1=ALU.mult),
                        reads=[l32b, ssqb, parb], writes=[l32b])
                    P.dma("pool", lat_out[l, r0:r0 + nr, :], l32[0:nr, t, :], reads=[l32b], writes=[outb], cbuf=l32b)
                    rope_tm("pool", kr_keep[0:nr, t, :].unsqueeze(1), rtm[0:nr, t, :], 32, nr, 1, rA, rB, [rtmb, krb], [krb])
                    P.dma("pool", kr_out[l, r0:r0 + nr, :], kr_keep[0:nr, t, :], reads=[krb], writes=[outb], cbuf=krb)
                    P.op("pool", lambda t=t, nr=nr: nc.gpsimd.tensor_copy(out=lkb[0:nr, t, 0:256], in_=l32[0:nr, t, :]), reads=[l32b], writes=[lkbb])
                    P.op("pool", lambda t=t, nr=nr: nc.gpsimd.tensor_copy(out=lkb[0:nr, t, 256:320], in_=kr_keep[0:nr, t, :]), reads=[krb], writes=[lkbb])
                    kpos = kbase_of(s) + (r0 - g0 if is_p else 0)

                    def ev_l(ti_, i0, n, nr=nr):
                        P.op("act", lambda: nc.scalar.activation(out=lTt[:, :, 0:nr], in_=PT[ti_][:, 0:384].rearrange("p (a b) -> p a b", b=128)[:, :, 0:nr], func=AF.Identity),
                             reads=[PTb[ti_]], writes=[lTtb])
                    transposes([lkb[0:nr, t, 0:128], lkb[0:nr, t, 128:256], lkb[0:nr, t, 256:320]], ev_l, [lkbb])
                    P.dma("pool", LT_scr[s][:, :, kpos:kpos + nr].rearrange("a p t -> p a t"), lTt[:, 0:2, 0:nr], reads=[lTtb],
                          writes=[scrB[s]["LT"]], cbuf=lTtb)
                    P.dma("pool", RT_scr[s][:, kpos:kpos + nr], lTt[0:64, 2, 0:nr], reads=[lTtb], writes=[scrB[s]["RT"]], cbuf=lTtb)
                    P.dma("pool", L_scr[s][kpos:kpos + nr, :], lkb[0:nr, t, 0:256], reads=[lkbb], writes=[scrB[s]["L"]], cbuf=lkbb)
                for t, (r0, nr, s) in enumerate(tiles):
                    for hh4 in range(2):
                        bi = next_pb()

                        def mm(bi=bi, t=t, nr=nr, hh4=hh4):
                            for hq in range(4):
                                hd = hh4 * 4 + hq
                                for k in range(6):
                                    ins = nc.tensor.matmul(PB[bi][0:nr, hq * 64:hq * 64 + 64],
                                                           lhsT=cqT[:, k, t * 128:t * 128 + nr], rhs=wuq[:, k, hd * 192 + 128:hd * 192 + 192],
                                                           start=(k == 0), stop=(k == 5))
                            return ins
                        P.op("pe", mm, reads=[wb, cqTb], writes=[PBb[bi]])
                        P.op("act", lambda nr=nr, bi=bi, hh4=hh4: nc.scalar.activation(
                            out=qr32[0:nr, hh4 * 256:(hh4 + 1) * 256], in_=PB[bi][0:nr, 0:256], func=AF.Identity),
                            reads=[PBb[bi]], writes=[qr32b])
                    rope_tm("dve", qr32[0:nr, :].rearrange("p (h e) -> p h e", e=64), rtm[0:nr, t, :], 32, nr, 8, rA, rB, [rtmb, qr32b], [qr32b])
                    P.op("pool", lambda nr=nr: nc.gpsimd.tensor_copy(out=qrb[0:nr, :], in_=qr32[0:nr, :]), reads=[qr32b], writes=[qrbb])

                    def ev_qrT(ti_, i0, n, t=t, nr=nr):
                        for a in range(n):
                            for hh in range(2):
                                P.op("act", lambda a=a, hh=hh: nc.scalar.activation(out=qrT[:, (i0 + a) * 2 + hh, t * 128:t * 128 + nr],
                                                                             in_=PT[ti_][hh * 64:(hh + 1) * 64, a * 128:a * 128 + nr], func=AF.Identity),
                                     reads=[PTb[ti_]], writes=[qrTb])
                    transposes([qrb[0:nr, a * 128:(a + 1) * 128] for a in range(4)], ev_qrT, [qrbb])
                for h in range(8):
                    bi = next_pb()

                    def mm(bi=bi, h=h):
                        for k in range(6):
                            ins = nc.tensor.matmul(PB[bi][:, 0:G], lhsT=wuq[:, k, h * 192:h * 192 + 128], rhs=cqT[:, k, 0:G],
                                                   start=(k == 0), stop=(k == 5))
                        return ins
                    P.op("pe", mm, reads=[wb, cqTb], writes=[PBb[bi]])
                    P.op("dve", lambda bi=bi: nc.vector.tensor_copy(out=qnT[:, 0:G], in_=PB[bi][:, 0:G]), reads=[PBb[bi]], writes=[qnTb])
                    for eh in range(2):
                        b2 = next_pb()
                        P.op("pe", lambda b2=b2, h=h, eh=eh: nc.tensor.matmul(PB[b2][:, 0:G], lhsT=wukT[:, h, eh * 128:(eh + 1) * 128],
                                                                            rhs=qnT[:, 0:G], start=True, stop=True),
                             reads=[wb, qnTb], writes=[PBb[b2]])
                        P.op("act", lambda b2=b2, h=h, eh=eh: nc.scalar.activation(out=qlT[:, h, eh, 0:G], in_=PB[b2][:, 0:G], func=AF.Identity),
                             reads=[PBb[b2]], writes=[qlTb])
                for i in range(4):
                    si = next_w()

                    def ev_g(bi, sub, i=i):
                        P.op("act", lambda: nc.scalar.activation(out=mgT[:, i * 2 + sub, 0:G], in_=PB[bi][:, 0:G], func=AF.Silu),
                             reads=[PBb[bi]], writes=[mgTb])
                    fm_block(si, BW, ev_g)

            phase(7)
            with P.scope() as st1:
                wuv = sb("wuv", [128, 2, 8, 128], BF16, st1); wuvb = Buf("wuv")
                P.dma("sp", wuv[:], w_uv_bf[l], reads=[wcast[l]], writes=[wuvb], cbuf=wuvb)
                LTs = sb("LTs", [128, 2, NKT_MAX * 128], BF16, st1); RTs = sb("RTs", [64, NKT_MAX * 128], BF16, st1)
                Ls = sb("Ls", [128, NKT_MAX, 258], BF16, st1)
                kvb = Buf("mlakv")
                PTt = [sb(f"PTm{i}", [128, 512], BF16, st1) for i in range(3)]
                PTtb = [Buf(f"PTm{i}") for i in range(3)]
                olb = sb("olb", [128, 256], BF16, st1); olbb = Buf("olb")
                olT = sb("olT", [128, 2, G], BF16, st1); olTb = Buf("olT")
                rr = sb("rrm", [128, 1], F32, st1); rrb = Buf("rrm")
                P.op("dve", lambda: nc.vector.memset(Ls[:, :, 256:257], 1.0), writes=[kvb])
                ptt_rr = 0
                for (s, sc0, sn, stl) in segs:
                    kbase = kbase_of(s)
                    nkeys = kbase + sn
                    nkt = (nkeys + 127) // 128
                    qts = seg_qtiles(s, sc0, sn, stl)
                    P.dma("sp", LTs[:, :, 0:nkeys], LT_scr[s][:, :, 0:nkeys].rearrange("a p t -> p a t"), reads=[scrB[s]["LT"]], writes=[kvb], cbuf=kvb)
                    P.dma("sp", RTs[:, 0:nkeys], RT_scr[s][:, 0:nkeys], reads=[scrB[s]["RT"]], writes=[kvb], cbuf=kvb)
                    nfull = nkeys // 128
                    if nfull:
                        P.dma("sp", Ls[:, 0:nfull, 0:256], L_scr[s][0:nfull * 128, :].rearrange("(j p) e -> p j e", p=128),
                              reads=[scrB[s]["L"]], writes=[kvb], cbuf=kvb)
                    if nkeys % 128:
                        rem = nkeys % 128
                        P.dma("sp", Ls[0:rem, nfull, 0:256], L_scr[s][nfull * 128:nkeys, :], reads=[scrB[s]["L"]], writes=[kvb], cbuf=kvb)
                    for h in range(8):
                        for j in range(nkt):
                            nk = min(128, nkeys - j * 128)
                            vis = [qi for qi, (qc, nq, qg) in enumerate(qts) if (not is_p) or qg >= j]
                            if not vis:
                                continue
                            qlo = qts[vis[0]][0]; qhi = qts[vis[-1]][0] + qts[vis[-1]][1]
                            sbk = next_pb(0, 2)

                            def smm(sbk=sbk, j=j, nk=nk, qlo=qlo, qhi=qhi, h=h):
                                o = PB[sbk][0:nk, 0:qhi - qlo]
                                nc.tensor.matmul(o, lhsT=LTs[:, 0, j * 128:j * 128 + nk], rhs=qlT[:, h, 0, qlo:qhi], start=True, stop=False)
                                nc.tensor.matmul(o, lhsT=LTs[:, 1, j * 128:j * 128 + nk], rhs=qlT[:, h, 1, qlo:qhi], start=False, stop=False)
                                return nc.tensor.matmul(o, lhsT=RTs[:, j * 128:j * 128 + nk], rhs=qrT[:, h, qlo:qhi], start=False, stop=True)
                            P.op("pe", smm, reads=[kvb, qlTb, qrTb], writes=[PBb[sbk]])
                            pi = ptt_rr % 3
                            ptt_rr += 1
                            P.op("act", lambda sbk=sbk, pi=pi, nk=nk, qlo=qlo, qhi=qhi: nc.scalar.activation(
                                out=PTt[pi][0:nk, 0:qhi - qlo], in_=PB[sbk][0:nk, 0:qhi - qlo], func=AF.Exp, scale=MLA_SCALE),
                                reads=[PBb[sbk]], writes=[PTtb[pi]])
                            if is_p and qts[vis[0]][2] == j:
                                P.op("pool", lambda pi=pi: nc.gpsimd.memset(PTt[pi][64:128, 0:64], 0.0), writes=[PTtb[pi]])

                            def avmm(pi=pi, j=j, nk=nk, vis=vis, qlo=qlo):
                                for qi in vis:
                                    qc, nq, qg = qts[qi]
                                    last_j = (qg if is_p else nkt - 1)
                                    ins = nc.tensor.matmul(PB[2 + qi][0:nq, 0:257], lhsT=PTt[pi][0:nk, qc - qlo:qc - qlo + nq], rhs=Ls[0:nk, j, 0:257],
                                                           start=(j == 0), stop=(j == last_j))
                                return ins
                            P.op("pe", avmm, reads=[PTtb[pi], kvb], writes=[PBb[2 + qi] for qi in vis])
                        for qi, (qc, nq, qg) in enumerate(qts):
                            ob = PB[2 + qi]
                            P.op("dve", lambda ob=ob, nq=nq: nc.vector.reciprocal(out=rr[0:nq, :], in_=ob[0:nq, 256:257]), reads=[PBb[2 + qi]], writes=[rrb])
                            P.op("dve", lambda ob=ob, nq=nq: nc.vector.tensor_scalar_mul(out=olb[0:nq, :], in0=ob[0:nq, 0:256], scalar1=rr[0:nq, 0:1]),
                                 reads=[PBb[2 + qi], rrb], writes=[olbb])

                            def ev_ol(ti_, i0, n, qc=qc, nq=nq):
                                P.op("act", lambda: nc.scalar.activation(out=olT[:, :, qc:qc + nq],
                                                                   in_=PT[ti_][:, 0:256].rearrange("p (a b) -> p a b", b=128)[:, :, 0:nq], func=AF.Identity),
                                     reads=[PTb[ti_]], writes=[olTb])
                            transposes([olb[0:nq, 0:128], olb[0:nq, 128:256]], ev_ol, [olbb])
                        c_lo, c_hi = sc0, sc0 + sn
                        bo = next_pb(0, 2)

                        def omm(bo=bo, h=h, c_lo=c_lo, c_hi=c_hi):
                            nc.tensor.matmul(PB[bo][:, 0:c_hi - c_lo], lhsT=wuv[:, 0, h, :], rhs=olT[:, 0, c_lo:c_hi], start=True, stop=False)
                            return nc.tensor.matmul(PB[bo][:, 0:c_hi - c_lo], lhsT=wuv[:, 1, h, :], rhs=olT[:, 1, c_lo:c_hi], start=False, stop=True)
                        P.op("pe", omm, reads=[wuvb, olTb], writes=[PBb[bo]])
                        P.op("dve", lambda bo=bo, h=h, c_lo=c_lo, c_hi=c_hi: nc.vector.tensor_tensor(
                            out=mixT[:, 24 + h, c_lo:c_hi], in0=PB[bo][:, 0:c_hi - c_lo], in1=mgT[:, h, c_lo:c_hi], op=ALU.mult),
                            reads=[PBb[bo], mgTb], writes=[mixTb[2]])

        phase(8)
        with P.scope() as st:
            ot = sb("ot", [128, D], F32, st); otb = Buf("ot")
            xr = sb("xr", [128, D], F32, st); xrb = Buf("xr")
            gbc = sb("gbc", [128, D], F32, st); gbcb = Buf("gbc")
            lg = sb("lg", [128, D], F32, st); lb_ = sb("lb", [128, D], F32, st); lgb = Buf("lg")
            stats = sb("stats", [128, 8, 6], F32, st); mv = sb("mv", [128, 2], F32, st); stb = Buf("stats")
            P.dma("sp", lg[:], ln_g[l:l + 1, :].partition_broadcast(128), writes=[lgb], cbuf=lgb)
            P.dma("sp", lb_[:], ln_b[l:l + 1, :].partition_broadcast(128), writes=[lgb], cbuf=lgb)
            for t, (r0, nr, s) in enumerate(tiles):
                if t == 0 or not is_p:
                    P.dma("sp", gbc[0:nr, :], mod_scr[l, s:s + 1, 2 * D:3 * D].partition_broadcast(nr), reads=[modb], writes=[gbcb], cbuf=gbcb)
                P.dma("sp", xr[0:nr, :], xin[r0:r0 + nr, :], writes=[xrb], cbuf=xrb)
                P.op("pool", lambda nr=nr: nc.gpsimd.tensor_scalar(out=xr[0:nr, :], in0=xr[0:nr, :], scalar1=ALPHA, scalar2=None, op0=ALU.mult),
                     reads=[xrb], writes=[xrb])
                for bi_ in range(NOB):
                    si = next_w()
                    bk = next_pb()

                    def mm(bk=bk, si=si, t=t, nr=nr):
                        for k in range(KT):
                            ins = nc.tensor.matmul(PB[bk][0:nr, 0:BW], lhsT=mixT[:, k, t * 128:t * 128 + nr], rhs=WS[si][:, k, :],
                                                   start=(k == 0), stop=(k == KT - 1))
                        return ins
                    P.op("pe", mm, reads=[WSb[si]] + mixTb, writes=[PBb[bk]])
                    cs = slice(bi_ * BW, (bi_ + 1) * BW)
                    P.op("dve", lambda bk=bk, cs=cs, nr=nr: nc.vector.tensor_tensor(out=ot[0:nr, cs], in0=PB[bk][0:nr, 0:BW], in1=gbc[0:nr, cs], op=ALU.mult),
                         reads=[PBb[bk], gbcb], writes=[otb])
                    P.op("pool", lambda cs=cs, nr=nr: nc.gpsimd.tensor_tensor(out=ot[0:nr, cs], in0=ot[0:nr, cs], in1=xr[0:nr, cs], op=ALU.add),
                         reads=[xrb, otb], writes=[otb])
                for c in range(8):
                    P.op("dve", lambda c=c, nr=nr: nc.vector.bn_stats(out=stats[0:nr, c, :], in_=ot[0:nr, c * 512:(c + 1) * 512]), reads=[otb], writes=[stb])
                P.op("dve", lambda nr=nr: nc.vector.bn_aggr(out=mv[0:nr, :], in_=stats[0:nr, :, :]), reads=[stb], writes=[stb])
                rsqrt(mv[0:nr, 1:2], mv[0:nr, 1:2], 1.0, 1e-5, [stb], [stb])
                for hf in range(2):
                    cs = slice(hf * 2048, (hf + 1) * 2048)
                    P.op("dve", lambda cs=cs, nr=nr: nc.vector.tensor_scalar(out=ot[0:nr, cs], in0=ot[0:nr, cs], scalar1=mv[0:nr, 0:1], scalar2=mv[0:nr, 1:2],
                                                                           op0=ALU.subtract, op1=ALU.mult), reads=[otb, stb], writes=[otb])
                    P.op("pool", lambda cs=cs, nr=nr: nc.gpsimd.tensor_tensor(out=ot[0:nr, cs], in0=ot[0:nr, cs], in1=lg[0:nr, cs], op=ALU.mult),
                         reads=[otb, lgb], writes=[otb])
                    P.op("pool", lambda cs=cs, nr=nr: nc.gpsimd.tensor_tensor(out=ot[0:nr, cs], in0=ot[0:nr, cs], in1=lb_[0:nr, cs], op=ALU.add),
                         reads=[otb, lgb], writes=[otb])
                P.dma("pool", xout[r0:r0 + nr, :], ot[0:nr, :], reads=[otb], writes=[x1b if (l == 0 and NL > 1) else outb], cbuf=otb)

    try:
        phase(1)
        for l in range(NL):
            layer_setup(l)
            for grp in groups:
                process_group(l, grp)
    except _StopBuild:
        pass
    E = P.engs["sp"]
    for c in P.all_counters:
        if c.step == 16 and c.sem is not None and not getattr(c, "dead", False):
            P._wait_raw(E, (c.sem, c.val))
    P.barrier(("pe", "act", "dve", "pool", "sp"))
    print(f"[build] T={T} NL={NL} instr_ops={P.ninstr} sems={P.nsem}")
    es.close()
    return nc


_CACHE = {}
_HOOK = {}


def _rope_tab(pos, rot_dim):
    half = rot_dim // 2
    inv = (ROPE_THETA ** (-np.arange(half, dtype=np.float32) * (2.0 / rot_dim))).astype(np.float32)
    ang = pos.astype(np.float32)[:, None] * inv[None, :]
    return np.concatenate([np.cos(ang), np.sin(ang)], axis=1).astype(np.float32)


def kernel(T=SEQ, NL=DEPTH, **inp):
    n = 8
    f = lambda a: np.ascontiguousarray(np.asarray(a, dtype=np.float32))
    key = (T, NL)
    if key not in _CACHE:
        _CACHE[key] = build(T, NL, do_sample=not _HOOK.get("no_sample"), stop_after=_HOOK.get("stop_after"), DL=_HOOK.get("DL", DEPTH))
    nc = _CACHE[key]
    pos_p = np.arange(T)
    pos_s = np.tile(PAST + np.arange(DEC_SEQ), 2)
    tri = np.triu(np.ones((128, 128), np.float32))
    negmask = np.where(np.arange(128)[None, :] >= np.arange(128)[:, None], 0.0, -1e30).astype(np.float32)
    shared = dict(
        w_mod=f(np.asarray(inp["w_mod"])[:_HOOK.get("DL", DEPTH)]), b_mod=f(np.asarray(inp["b_mod"])[:_HOOK.get("DL", DEPTH)]),
        w_in=f(np.asarray(inp["w_in"])[:_HOOK.get("DL", DEPTH)]), conv_w=f(inp["conv_w"]), conv_b=f(inp["conv_b"]),
        dt_bias=f(inp["dt_bias"]), a_log=f(inp["a_log"]), d_skip=f(inp["d_skip"]), ssd_norm_w=f(inp["ssd_norm_w"]),
        lambda_q1=f(inp["lambda_q1"]), lambda_k1=f(inp["lambda_k1"]), lambda_q2=f(inp["lambda_q2"]), lambda_k2=f(inp["lambda_k2"]),
        diff_norm_w=f(inp["diff_norm_w"]), mla_q_norm_w=f(inp["mla_q_norm_w"]), mla_kv_norm_w=f(inp["mla_kv_norm_w"]),
        w_uq=f(inp["w_uq"]), w_ukT=f(np.transpose(np.asarray(inp["w_uk"]), (0, 3, 2, 1))), w_uv=f(inp["w_uv"]),
        w_out=f(np.asarray(inp["w_out"])[:_HOOK.get("DL", DEPTH)]), ln_g=f(inp["ln_g"]), ln_b=f(inp["ln_b"]),
        rope_d_p=_rope_tab(pos_p, 16), rope_m_p=_rope_tab(pos_p, 64), rope_d_s=_rope_tab(pos_s, 16), rope_m_s=_rope_tab(pos_s, 64),
        tri=tri, negmask=negmask)
    xp = np.asarray(inp["x_prompt"]); xs = np.asarray(inp["x_sample"])
    cp = np.asarray(inp["c_prompt"]); cs = np.asarray(inp["c_sample"])
    in_maps = []
    for c in range(n):
        b = c % 4
        s0 = 2 * c
        cc = np.stack([cp[b], cs[s0], cs[s0 + 1]], axis=0)
        cTl = f(cc.T.reshape(KT, 128, 3).transpose(1, 0, 2))
        m = dict(shared)
        m.update(
            x_p=f(xp[b, :T]), x_s=f(xs[s0:s0 + 2].reshape(64, D)), cT=cTl,
            ck=f(np.asarray(inp["cache_diff_k"])[:, s0:s0 + 2].reshape(DEPTH, 2, PAST, 1024)),
            cv=f(np.asarray(inp["cache_diff_v"])[:, s0:s0 + 2].reshape(DEPTH, 2, PAST, 1024)),
            cl=f(np.asarray(inp["cache_mla_latent"])[:, s0:s0 + 2]), cr=f(np.asarray(inp["cache_mla_krope"])[:, s0:s0 + 2]),
            st_ssm=f(np.asarray(inp["state_ssm"])[:, s0:s0 + 2].reshape(DEPTH, 2, 2048, 128)),
            st_conv=f(np.asarray(inp["state_conv"])[:, s0:s0 + 2]))
        in_maps.append(m)
    if _HOOK.get("in_maps_only"):
        return nc, in_maps
    if _HOOK.get("n_cores"):
        m_ = _HOOK["n_cores"]
        res = run_bass_kernel_spmd(nc, in_maps[:m_], core_ids=list(range(m_)))
        return res.results
    res = run_bass_kernel_spmd(nc, in_maps, core_ids=list(range(n)))
    R = res.results
    B = 4
    y_prompt = np.stack([R[b]["y_p"] for b in range(B)])
    y_sample = np.concatenate([R[c]["y_s"].reshape(2, DEC_SEQ, D) for c in range(n)])
    def gp(name, shp):
        return np.stack([R[b][name].reshape(shp) for b in range(B)], axis=1)
    def gs(name, shp):
        return np.concatenate([R[c][name].reshape(shp) for c in range(n)], axis=1)
    outs = (
        y_prompt, y_sample,
        gp("dk_p", (DEPTH, T, 8, 2, 64)), gp("dv_p", (DEPTH, T, 8, 128)), gp("lat_p", (DEPTH, T, 256)), gp("kr_p", (DEPTH, T, 64)),
        gp("ssm_p", (DEPTH, NH, HP, NS)), gp("conv_p", (DEPTH, 3, CONVD)),
        gs("dk_s", (DEPTH, 2, DEC_SEQ, 8, 2, 64)), gs("dv_s", (DEPTH, 2, DEC_SEQ, 8, 128)), gs("lat_s", (DEPTH, 2, DEC_SEQ, 256)),
        gs("kr_s", (DEPTH, 2, DEC_SEQ, 64)), gs("ssm_s", (DEPTH, 2, NH, HP, NS)), gs("conv_s", (DEPTH, 2, 3, CONVD)))
    return tuple(np.ascontiguousarray(o.astype(np.float32)) for o in outs)
```
